# Optimizing a Trainium2 kernel written in Bass

```python
import math
import jax, jax.numpy as jnp
from jax import lax
import numpy as np

D_MODEL = 1024
BATCH = 32
SEQ = 2048
DEPTH = 4
DEC_BATCH = 8
DEC_SEQ = 2048
PAST_LEN = 128

N_MIXERS = 2
N_CONV_LAYERS = (DEPTH + 1) // 2
N_SSM_LAYERS = DEPTH // 2
SSM_GROUP = 16
SSM_GROUPS = D_MODEL // SSM_GROUP
SSM_STATE = 64
N_DIR = 2
FFN_HIDDEN = 2816
DEEPNORM_ALPHA = (2 * DEPTH) ** 0.25
DEEPNORM_BETA = (8 * DEPTH) ** -0.25
LN_EPS = 1e-5
DT_MIN = 1e-3
DT_MAX = 1e-1

kernel_name = "hybrid_shortconv_s5_convffn_encoder"


def _layer_norm(x, g, b):
    xf = x.astype(jnp.float32)
    mu = jnp.mean(xf, axis=-1, keepdims=True)
    var = jnp.mean(jnp.square(xf - mu), axis=-1, keepdims=True)
    y = (xf - mu) * lax.rsqrt(var + LN_EPS)
    return (y * g.astype(jnp.float32) + b.astype(jnp.float32)).astype(x.dtype)


def _conv3(x, w, b):
    xp = jnp.pad(x, ((0, 0), (1, 1), (0, 0)))
    return xp[:, :-2] * w[0] + xp[:, 1:-1] * w[1] + xp[:, 2:] * w[2] + b


def _short_conv_mixer(u, w_in, conv_w, conv_b, w_out):
    bgate, cgate, h = jnp.split(u @ w_in, 3, axis=-1)
    return (bgate * _conv3(cgate * h, conv_w, conv_b)) @ w_out


def _cmul(ar, ai, br, bi):
    return ar * br - ai * bi, ar * bi + ai * br


def _scan_combine(e1, e2):
    a1r, a1i, b1r, b1i = e1
    a2r, a2i, b2r, b2i = e2
    ar, ai = _cmul(a1r, a1i, a2r, a2i)
    br, bi = _cmul(a2r, a2i, b1r, b1i)
    return ar, ai, br + b2r, bi + b2i


def _s5_sequence(u, a_re, a_im, log_dt, b_re, b_im, c_re, c_im, d_skip):
    f32 = jnp.float32
    L = u.shape[0]
    uf = u.astype(f32)
    ug = uf.reshape(L, SSM_GROUPS, SSM_GROUP)
    y = d_skip.astype(f32) * uf
    for k in range(N_DIR):
        lam_re = a_re[k].astype(f32)
        lam_im = a_im[k].astype(f32)
        dt = jnp.exp(log_dt[k].astype(f32))[:, None]
        mag = jnp.exp(lam_re * dt)
        ang = lam_im * dt
        lb_re = mag * jnp.cos(ang)
        lb_im = mag * jnp.sin(ang)
        den = lam_re * lam_re + lam_im * lam_im
        f_re = ((lb_re - 1.0) * lam_re + lb_im * lam_im) / den
        f_im = (lb_im * lam_re - (lb_re - 1.0) * lam_im) / den
        bu_re = jnp.einsum('lgi,gpi->lgp', ug, b_re[k].astype(f32))
        bu_im = jnp.einsum('lgi,gpi->lgp', ug, b_im[k].astype(f32))
        v_re, v_im = _cmul(f_re, f_im, bu_re, bu_im)
        if k == 1:
            v_re = jnp.flip(v_re, axis=0)
            v_im = jnp.flip(v_im, axis=0)
        shape = v_re.shape
        _, _, s_re, s_im = lax.associative_scan(
            _scan_combine,
            (jnp.broadcast_to(lb_re, shape), jnp.broadcast_to(lb_im, shape), v_re, v_im),
            axis=0)
        if k == 1:
            s_re = jnp.flip(s_re, axis=0)
            s_im = jnp.flip(s_im, axis=0)
        y_dir = (jnp.einsum('lgp,gip->lgi', s_re, c_re[k].astype(f32))
                 - jnp.einsum('lgp,gip->lgi', s_im, c_im[k].astype(f32)))
        y = y + y_dir.reshape(L, D_MODEL)
    return y.astype(u.dtype)


def _s5_mixer(u, a_re, a_im, log_dt, b_re, b_im, c_re, c_im, d_skip, w_glu):
    y = lax.map(lambda us: _s5_sequence(us, a_re, a_im, log_dt, b_re, b_im, c_re, c_im, d_skip), u)
    z = jax.nn.gelu(y)
    val, gate = jnp.split(z @ w_glu, 2, axis=-1)
    return val * jax.nn.sigmoid(gate)


def _conv_ffn(u, w_up, conv_w, conv_b, w_down):
    a, v = jnp.split(_conv3(u @ w_up, conv_w, conv_b), 2, axis=-1)
    return (jax.nn.gelu(a) * v) @ w_down


def _trunk(x, c, ada_w, ada_b, ln1_g, ln1_b, ln2_g, ln2_b,
           sc_w_in, sc_conv_w, sc_conv_b, sc_w_out,
           s5_a_re, s5_a_im, s5_log_dt, s5_b_re, s5_b_im, s5_c_re, s5_c_im, s5_d, s5_w_glu,
           ffn_w_up, ffn_conv_w, ffn_conv_b, ffn_w_down):
    c_act = jax.nn.silu(c)
    for i in range(DEPTH):
        mod = (c_act @ ada_w[i] + ada_b[i])[:, None, :]
        sh1, sc1, g1, sh2, sc2, g2 = jnp.split(mod, 6, axis=-1)
        j = i // N_MIXERS
        u = x * (1.0 + sc1) + sh1
        if i % N_MIXERS == 0:
            m = _short_conv_mixer(u, sc_w_in[j], sc_conv_w[j], sc_conv_b[j], sc_w_out[j])
        else:
            m = _s5_mixer(u, s5_a_re[j], s5_a_im[j], s5_log_dt[j], s5_b_re[j], s5_b_im[j],
                          s5_c_re[j], s5_c_im[j], s5_d[j], s5_w_glu[j])
        x = _layer_norm(DEEPNORM_ALPHA * x + (1.0 + g1) * m, ln1_g[i], ln1_b[i])
        u = x * (1.0 + sc2) + sh2
        f = _conv_ffn(u, ffn_w_up[i], ffn_conv_w[i], ffn_conv_b[i], ffn_w_down[i])
        x = _layer_norm(DEEPNORM_ALPHA * x + (1.0 + g2) * f, ln2_g[i], ln2_b[i])
    return x


def setup_inputs(seed: int = 0) -> dict:
    key = jax.random.key(seed)
    ks = jax.random.split(key, 28)
    f32 = jnp.float32
    D, F, G, P, I = D_MODEL, FFN_HIDDEN, SSM_GROUPS, SSM_STATE, SSM_GROUP
    NC, NS = N_CONV_LAYERS, N_SSM_LAYERS

    def nrm(k, shape, s):
        return jax.random.normal(k, shape, f32) * s

    n_idx = jnp.arange(P, dtype=f32)
    return {
        "x_prompt": nrm(ks[0], (BATCH, SEQ, D), 1.0),
        "x_sample": nrm(ks[1], (DEC_BATCH, DEC_SEQ, D), 1.0),
        "c_prompt": nrm(ks[2], (BATCH, D), 1.0),
        "c_sample": nrm(ks[3], (DEC_BATCH, D), 1.0),
        "ada_w": nrm(ks[4], (DEPTH, D, 6 * D), 0.1 * D ** -0.5),
        "ada_b": nrm(ks[5], (DEPTH, 6 * D), 0.01),
        "ln1_g": 1.0 + nrm(ks[6], (DEPTH, D), 0.02),
        "ln1_b": nrm(ks[7], (DEPTH, D), 0.02),
        "ln2_g": 1.0 + nrm(ks[8], (DEPTH, D), 0.02),
        "ln2_b": nrm(ks[9], (DEPTH, D), 0.02),
        "sc_w_in": nrm(ks[10], (NC, D, 3 * D), D ** -0.5),
        "sc_conv_w": nrm(ks[11], (NC, 3, D), 3 ** -0.5),
        "sc_conv_b": nrm(ks[12], (NC, D), 0.02),
        "sc_w_out": nrm(ks[13], (NC, D, D), DEEPNORM_BETA * D ** -0.5),
        "s5_a_re": -0.5 + nrm(ks[14], (NS, N_DIR, G, P), 0.01),
        "s5_a_im": math.pi * n_idx + nrm(ks[15], (NS, N_DIR, G, P), 0.01),
        "s5_log_dt": jax.random.uniform(ks[16], (NS, N_DIR, G), f32,
                                        math.log(DT_MIN), math.log(DT_MAX)),
        "s5_b_re": nrm(ks[17], (NS, N_DIR, G, P, I), (2 * I) ** -0.5),
        "s5_b_im": nrm(ks[18], (NS, N_DIR, G, P, I), (2 * I) ** -0.5),
        "s5_c_re": nrm(ks[19], (NS, N_DIR, G, I, P), P ** -0.5),
        "s5_c_im": nrm(ks[20], (NS, N_DIR, G, I, P), P ** -0.5),
        "s5_d": nrm(ks[21], (NS, D), 1.0),
        "s5_w_glu": nrm(ks[22], (NS, D, 2 * D), DEEPNORM_BETA * D ** -0.5),
        "ffn_w_up": nrm(ks[23], (DEPTH, D, 2 * F), D ** -0.5),
        "ffn_conv_w": nrm(ks[24], (DEPTH, 3, 2 * F), 3 ** -0.5),
        "ffn_conv_b": nrm(ks[25], (DEPTH, 2 * F), 0.02),
        "ffn_w_down": nrm(ks[26], (DEPTH, F, D), DEEPNORM_BETA * F ** -0.5),
    }


def reference(x_prompt, x_sample, c_prompt, c_sample, ada_w, ada_b, ln1_g, ln1_b, ln2_g, ln2_b,
              sc_w_in, sc_conv_w, sc_conv_b, sc_w_out,
              s5_a_re, s5_a_im, s5_log_dt, s5_b_re, s5_b_im, s5_c_re, s5_c_im, s5_d, s5_w_glu,
              ffn_w_up, ffn_conv_w, ffn_conv_b, ffn_w_down):
    y_prompt = _trunk(x_prompt, c_prompt, ada_w, ada_b, ln1_g, ln1_b, ln2_g, ln2_b,
                      sc_w_in, sc_conv_w, sc_conv_b, sc_w_out,
                      s5_a_re, s5_a_im, s5_log_dt, s5_b_re, s5_b_im, s5_c_re, s5_c_im, s5_d, s5_w_glu,
                      ffn_w_up, ffn_conv_w, ffn_conv_b, ffn_w_down)
    y_sample = _trunk(x_sample, c_sample, ada_w, ada_b, ln1_g, ln1_b, ln2_g, ln2_b,
                      sc_w_in, sc_conv_w, sc_conv_b, sc_w_out,
                      s5_a_re, s5_a_im, s5_log_dt, s5_b_re, s5_b_im, s5_c_re, s5_c_im, s5_d, s5_w_glu,
                      ffn_w_up, ffn_conv_w, ffn_conv_b, ffn_w_down)
    return (y_prompt, y_sample)
```

```python
from contextlib import ExitStack
import numpy as np
import concourse.bass as bass
import concourse.mybir as mybir
from concourse.bass_utils import run_bass_kernel_spmd
from concourse.ap import AP

F32 = mybir.dt.float32
BF16 = mybir.dt.bfloat16
I32 = mybir.dt.int32
AF = mybir.ActivationFunctionType
ALU = mybir.AluOpType

D = 1024
KC = 8
L = 2048
TT = 512
NTT = 4
FH = 2816
FC = 22
DEPTH = 4
G = 64
PS = 64
NCORES = 8
ALPHA = (2 * DEPTH) ** 0.25
EPS_P = 1e-5 / (ALPHA * ALPHA)
NSLOT = 4
HGROUPS = [(0, 8), (8, 16), (16, 22)]


class Op:
    __slots__ = ("eng", "fn", "reads", "writes", "deps", "is_dma", "chan", "sig", "val", "idx", "ss")

    def __init__(self, eng, fn, reads, writes, is_dma=False, chan=None):
        self.eng = eng
        self.fn = fn
        self.reads = reads
        self.writes = writes
        self.deps = ()
        self.is_dma = is_dma
        self.chan = chan
        self.sig = False
        self.val = 0
        self.idx = -1
        self.ss = False


class Prog:
    ENGS = ("pe", "act", "dve", "pool", "sp")

    def __init__(self):
        self.ops = []
        self.last_w = {}
        self.readers = {}

    def op(self, eng, fn, reads=(), writes=(), ss=False):
        o = Op(eng, fn, tuple(reads), tuple(writes))
        o.ss = ss
        self._add(o)
        return o

    def dma(self, fn, reads=(), writes=(), chan=None, eng="sp"):
        o = Op(eng, fn, tuple(reads), tuple(writes), is_dma=True, chan=chan)
        self._add(o)
        return o

    def _add(self, o):
        o.idx = len(self.ops)
        deps = set()
        for k in o.reads:
            w = self.last_w.get(k)
            if w is not None:
                deps.add(w)
        for k in o.writes:
            w = self.last_w.get(k)
            if w is not None:
                deps.add(w)
            deps.update(self.readers.get(k, ()))
        o.deps = deps
        for k in o.reads:
            self.readers.setdefault(k, []).append(o.idx)
        for k in o.writes:
            self.last_w[k] = o.idx
            self.readers[k] = []
        self.ops.append(o)

    def channels(self):
        return sorted({o.chan for o in self.ops if o.is_dma}, key=str)

    def finalize(self, block, sems):
        ops = self.ops
        for o in ops:
            for d in o.deps:
                p = ops[d]
                if p.is_dma or p.eng != o.eng or o.is_dma or p.ss:
                    p.sig = True
        cnt = {e: 0 for e in self.ENGS}
        ccnt = {}
        for o in ops:
            if o.is_dma:
                ccnt[o.chan] = ccnt.get(o.chan, 0) + 1
                o.val = 16 * ccnt[o.chan]
                o.sig = True
            elif o.sig:
                cnt[o.eng] += 1
                o.val = cnt[o.eng]
        per_eng = {e: [] for e in self.ENGS}
        for o in ops:
            per_eng[o.eng].append(o)

        def emit_stream(engname, e):
            waited = {}
            for o in per_eng[engname]:
                need = {}
                for d in o.deps:
                    p = ops[d]
                    if p.is_dma:
                        key = ("chan", p.chan)
                    else:
                        if p.eng == engname and not o.is_dma and not p.ss:
                            continue
                        key = p.eng
                    if p.val > need.get(key, 0):
                        need[key] = p.val
                for key, v in need.items():
                    if waited.get(key, 0) >= v:
                        continue
                    waited[key] = v
                    e.wait_ge(sems[key], v)
                ins = o.fn(e)
                if o.sig:
                    if o.is_dma:
                        ins.then_inc(sems[("chan", o.chan)], 16)
                    else:
                        ins.then_inc(sems[engname], 1)

        @block.tensor
        def _(e):
            emit_stream("pe", e)

        @block.scalar
        def _(e):
            emit_stream("act", e)

        @block.vector
        def _(e):
            emit_stream("dve", e)

        @block.gpsimd
        def _(e):
            emit_stream("pool", e)

        @block.sync
        def _(e):
            emit_stream("sp", e)


def build_program(NS, layers, stop=None):
    nc = bass.Bass("TRN2", target_bir_lowering=False)

    def din(name, shape):
        return nc.dram_tensor(name, list(shape), F32, kind="ExternalInput").ap()

    xin = din("xin", (NS, L, D))
    cin = din("cin", (NS, D))
    consts = din("consts", (128, 256))
    ada_w = din("ada_w", (DEPTH, D, 6 * D))
    ada_b = din("ada_b", (DEPTH, 6 * D))
    ln_gb = [din("ln1_g", (DEPTH, D)), din("ln1_b", (DEPTH, D)), din("ln2_g", (DEPTH, D)), din("ln2_b", (DEPTH, D))]
    sc_w_in = din("sc_w_in", (2, D, 3 * D))
    sc_conv_w = din("sc_conv_w", (2, 3, D))
    sc_conv_b = din("sc_conv_b", (2, D))
    sc_w_out = din("sc_w_out", (2, D, D))
    s5_a_re = din("s5_a_re", (2, 2, G, PS))
    s5_a_im = din("s5_a_im", (2, 2, G, PS))
    s5_log_dt = din("s5_log_dt", (2, 2, G))
    s5_b_re = din("s5_b_re", (2, 2, G, PS, 16))
    s5_b_im = din("s5_b_im", (2, 2, G, PS, 16))
    s5_c_re = din("s5_c_re", (2, 2, G, 16, PS))
    s5_c_im = din("s5_c_im", (2, 2, G, 16, PS))
    s5_d = din("s5_d", (2, D))
    s5_w_glu = din("s5_w_glu", (2, D, 2 * D))
    ffn_w_up = din("ffn_w_up", (DEPTH, D, 2 * FH))
    ffn_conv_w = din("ffn_conv_w", (DEPTH, 3, 2 * FH))
    ffn_conv_b = din("ffn_conv_b", (DEPTH, 2 * FH))
    ffn_w_down = din("ffn_w_down", (DEPTH, FH, D))
    yout = nc.dram_tensor("yout", [NS, L, D], F32, kind="ExternalOutput").ap()
    dbg = nc.dram_tensor("dbg", [128, 8, L], BF16, kind="ExternalOutput").ap() if stop else None
    dbg32 = nc.dram_tensor("dbg32", [128, 16384], F32, kind="ExternalOutput").ap() if stop else None

    def dscr(name, shape, dt=BF16):
        return nc.dram_tensor(name, list(shape), dt).ap()

    wb_in = dscr("wb_in", (2, D, 3 * D))
    wb_out = dscr("wb_out", (2, D, D))
    wb_glu = dscr("wb_glu", (2, D, 2 * D))
    wb_up = (nc.dram_tensor("wb_up", [DEPTH, D, 2 * FH], BF16, kind="ExternalOutput").ap() if stop == "wup"
             else dscr("wb_up", (DEPTH, D, 2 * FH)))
    wb_dn = dscr("wb_dn", (DEPTH, FH, D))

    if stop == "s5w":
        twb = nc.dram_tensor("twb", [2, G, 128, 4, 128], BF16, kind="ExternalOutput").ap()
        twc = nc.dram_tensor("twc", [2, G, 128, 7, 128], BF16, kind="ExternalOutput").ap()
    else:
        twb = dscr("twb", (2, G, 128, 4, 128))
        twc = dscr("twc", (2, G, 128, 7, 128))
    d_in = dscr("d_in", (8, D, 256))
    d_out = dscr("d_out", (8, D, 256))

    P = Prog()
    jobs = []

    with ExitStack() as es:
        def sb(name, shape, dt):
            return es.enter_context(nc.sbuf_tensor(name, list(shape), dt))

        x = sb("x", (128, KC, L), F32)
        CU = sb("CU", (128, 16384), F32)
        bufC = CU[:, 0:8192].rearrange("p (a b) -> p a b", a=8)
        ub = CU[:, 8192:16384].bitcast(BF16).rearrange("p (a b) -> p a b", a=KC)
        bufB = sb("bufB", (128, 8, L), BF16)
        scT = sb("scT", (128, 256), F32)
        ring = sb("ring", (128, NSLOT, 2048), F32)
        cst = sb("cst", (128, 256), F32)
        ones_b = sb("ones_b", (128, 128), BF16)
        modT = sb("modT", (128, DEPTH, 48, NS), F32)
        g2c = sb("g2c", (128, DEPTH, 2, KC, NS), F32)
        b2c = sb("b2c", (128, DEPTH, 2, KC, NS), F32)
        lnp = sb("lnp", (128, 4, DEPTH, KC), F32)
        adab = sb("adab", (128, DEPTH, 48), F32)
        cT = sb("cT", (128, NS, KC), F32)
        sccw = sb("sccw", (128, 2, 3, KC), F32)
        sccb = sb("sccb", (128, 2, KC), F32)
        fcw = sb("fcw", (128, DEPTH, 3, 2 * FC), F32)
        fcb = sb("fcb", (128, DEPTH, 2 * FC), F32)
        identb = sb("identb", (128, 128), BF16)
        scA = sb("scA", (128, 2, 2, 2, G), F32)
        dcol = sb("dcol", (16, 2, G), F32)
        pa = es.enter_context(nc.psum_tensor("pa", [128, 2048], F32))
        pb = es.enter_context(nc.psum_tensor("pb", [128, 2048], F32))
        ident = cst[:, 0:128]

        PSA = [("ps", i) for i in range(4)]
        PSB = [("ps", 4 + i) for i in range(4)]
        XK = [("x", k) for k in range(KC)]
        UK = [("ub", k) for k in range(KC)]
        BK = [("B", k) for k in range(8)]
        CK = [("C", k) for k in range(8)]

        ring_n = [0]

        def wjob(src, view_fn, run, rkeys=()):
            jobs.append(((src, view_fn, tuple(rkeys)), run))

        def cjob(run):
            jobs.append((None, run))

        def run_jobs():
            loads = [i for i, j in enumerate(jobs) if j[0] is not None]
            tiles = {}
            li = [0]

            def issue(upto):
                while li[0] < len(loads) and li[0] < upto:
                    ji = loads[li[0]]
                    s = ring_n[0] % NSLOT
                    ring_n[0] += 1
                    src, view_fn, rkeys = jobs[ji][0]
                    tv = view_fn(ring[:, s, :])
                    if not isinstance(src, list):
                        src = [(src, lambda t: t)]
                    wkeys = []
                    for si, (sap, sel) in enumerate(src):
                        dst = sel(tv)
                        P.dma(lambda e, dst=dst, sap=sap: e.dma_start(out=dst, in_=sap),
                              reads=rkeys, writes=[("w", s, si)], chan=("w", s))
                        wkeys.append(("w", s, si))
                    for si in range(len(src), 3):
                        wkeys.append(("w", s, si))
                    tiles[ji] = (tv, wkeys)
                    li[0] += 1

            nl = 0
            for ji, (ld, run) in enumerate(jobs):
                if ld is not None:
                    issue(nl + NSLOT - 1)
                    issue(nl + 1)
                    nl += 1
                    tv, key = tiles.pop(ji)
                    run(tv, key)
                else:
                    run()

        def bfview(slot, shape):
            v = slot.bitcast(BF16)
            n = int(np.prod(shape))
            v = v[:, 0:n]
            if len(shape) == 2:
                return v.rearrange("p (a b) -> p a b", a=shape[0])
            if len(shape) == 3:
                return v.rearrange("p (a b c) -> p a b c", a=shape[0], b=shape[1])
            if len(shape) == 4:
                return v.rearrange("p (a b c d) -> p a b c d", a=shape[0], b=shape[1], c=shape[2])
            return v

        def mm(out, lhsT, rhs, start, stop, reads, writes):
            P.op("pe", lambda e: e.matmul(out, lhsT=lhsT, rhs=rhs, start=start, stop=stop),
                 reads=reads, writes=writes)

        P.dma(lambda e: e.dma_start(out=cst[:], in_=consts[:]), writes=["cst"], chan="cst")
        P.op("dve", lambda e: e.memset(ones_b[:], 1.0), writes=["ones"])

        def load_rows_T(src2d, nrows, dst_fn, tag):
            r0 = 0
            i = 0
            while r0 < nrows:
                r = min(128, nrows - r0)
                blk = i % 2
                st = bufC[:, blk, 0:128]
                pst = pa[:, blk * 512: blk * 512 + 128]
                P.dma(lambda e, st=st, r=r, r0=r0: e.dma_start(out=st[0:r, :], in_=src2d[r0:r0 + r, :]),
                      writes=[CK[blk]], chan=("ldT", blk))
                P.op("pe", lambda e, st=st, r=r, pst=pst: e.transpose(out=pst[:, 0:r], in_=st[0:r, :], identity=ident[0:r, 0:r]),
                     reads=[CK[blk], "cst"], writes=[PSA[blk]])
                dst = dst_fn(r0, r)
                P.op("dve", lambda e, dst=dst, r=r, pst=pst: e.tensor_copy(out=dst, in_=pst[:, 0:r]),
                     reads=[PSA[blk]], writes=[tag], ss=True)
                r0 += r
                i += 1

        for i4 in range(4):
            src = ln_gb[i4].rearrange("l (k p) -> (l k) p", p=128)
            load_rows_T(src, DEPTH * KC, lambda r0, r, i4=i4: lnp[:, i4, :, :].rearrange("p l k -> p (l k)")[:, r0:r0 + r], "lnp")
        load_rows_T(ada_b.rearrange("l (o p) -> (l o) p", p=128), DEPTH * 48,
                    lambda r0, r: adab[:].rearrange("p l o -> p (l o)")[:, r0:r0 + r], "adab")
        load_rows_T(cin.rearrange("s (k p) -> (s k) p", p=128), NS * KC,
                    lambda r0, r: cT[:].rearrange("p s k -> p (s k)")[:, r0:r0 + r], "cT")
        load_rows_T(sc_conv_w.rearrange("j t (k p) -> (j t k) p", p=128), 2 * 3 * KC,
                    lambda r0, r: sccw[:].rearrange("p j t k -> p (j t k)")[:, r0:r0 + r], "sccw")
        load_rows_T(sc_conv_b.rearrange("j (k p) -> (j k) p", p=128), 2 * KC,
                    lambda r0, r: sccb[:].rearrange("p j k -> p (j k)")[:, r0:r0 + r], "sccb")
        load_rows_T(ffn_conv_w.rearrange("l t (m p) -> (l t m) p", p=128), DEPTH * 3 * 2 * FC,
                    lambda r0, r: fcw[:].rearrange("p l t m -> p (l t m)")[:, r0:r0 + r], "fcw")
        load_rows_T(ffn_conv_b.rearrange("l (m p) -> (l m) p", p=128), DEPTH * 2 * FC,
                    lambda r0, r: fcb[:].rearrange("p l m -> p (l m)")[:, r0:r0 + r], "fcb")
        P.op("act", lambda e: e.activation(out=cT[:], in_=cT[:], func=AF.Silu), reads=["cT"], writes=["cT"], ss=True)

        conv_ctr = [0]

        scr_blocks = {}

        conv_blocks = []

        def convert(src2d, dst2d, K, M, name):
            scr_blocks[name] = []
            for r0 in range(0, K, 128):
                for c0 in range(0, M, 1024):
                    cw = min(1024, M - c0)
                    conv_blocks.append((src2d, dst2d, r0, c0, cw, name))
                    scr_blocks[name].append(("wscr", name, r0, c0))

        def pump(n):
            blocks = list(conv_blocks)
            del conv_blocks[:]
            nb = len(blocks)
            info = {}

            def rec_load(bi):
                src2d, dst2d, r0, c0, cw, name = blocks[bi]
                i = bi % NSLOT
                st32 = ring[:, i, 0:cw]
                st16 = ring[:, i, 1024:1536].bitcast(BF16)[:, 0:cw]
                k32 = [("w", i, 0), ("w", i, 1)]
                k16 = [("w", i, 2)]
                info[bi] = (st32, st16, k32, k16)
                P.dma(lambda e: e.dma_start(out=st32, in_=src2d[r0:r0 + 128, c0:c0 + cw]), writes=k32, chan=("cvi", i))

            def rec_rest(bi):
                src2d, dst2d, r0, c0, cw, name = blocks[bi]
                i = bi % NSLOT
                st32, st16, k32, k16 = info.pop(bi)
                P.op("pool", lambda e: e.tensor_copy(out=st16, in_=st32), reads=k32, writes=k16)
                P.dma(lambda e: e.dma_start(out=dst2d[r0:r0 + 128, c0:c0 + cw], in_=st16),
                      reads=k16, writes=[("wscr", name, r0, c0)], chan=("cvo", i))
            AHEAD = 2
            for bi in range(min(AHEAD, nb)):
                rec_load(bi)
            for bi in range(nb):
                if bi + AHEAD < nb:
                    rec_load(bi + AHEAD)
                rec_rest(bi)

        used_conv = sorted({l // 2 for l in layers if l % 2 == 0})
        used_s5 = sorted({l // 2 for l in layers if l % 2 == 1})
        wkeys = {}
        for l in layers:
            j = l // 2
            if l % 2 == 0:
                convert(sc_w_in[j], wb_in[j], D, 3 * D, ('in', j))
                convert(sc_w_out[j], wb_out[j], D, D, ('out', j))
            else:
                convert(s5_w_glu[j], wb_glu[j], D, 2 * D, ('glu', j))
            convert(ffn_w_up[l], wb_up[l], D, 2 * FH, ('up', l))
            convert(ffn_w_down[l], wb_dn[l], FH, D, ('dn', l))

        ada_list = []

        def emit_ada(n):
            for _ in range(n):
                if not ada_list:
                    return
                a_src, a_view, a_run = ada_list.pop(0)
                wjob(a_src, a_view, a_run)

        def ada_jobs(l):
            for cb in range(24):
                src = ada_w[l].rearrange("(k p) o -> p k o", p=128)[:, :, cb * 256:(cb + 1) * 256]

                def run(tv, key, cb=cb, l=l):
                    for oo in range(2):
                        oc = cb * 2 + oo
                        pst = pb[:, (oc % 4) * 512:(oc % 4) * 512 + NS]
                        for kc in range(KC):
                            mm(pst, tv[:, kc, oo * 128:(oo + 1) * 128], cT[:, :, kc], kc == 0, kc == KC - 1,
                               key + ["cT"], [PSB[oc % 4]])
                        P.op("dve", lambda e, pst=pst, oc=oc, l=l: e.tensor_scalar(
                            out=modT[:, l, oc, :], in0=pst, scalar1=adab[:, l, oc:oc + 1], scalar2=None, op0=ALU.add),
                            reads=[PSB[oc % 4], "adab"], writes=[("mod", l)], ss=True)
                ada_list.append((src, (lambda slot: slot[:, 0:2048].rearrange("p (k o) -> p k o", k=KC)), run))

        def ada_post(l, nxt):
            def run(l=l, nxt=nxt):
                for w in (1, 4):
                    P.op("dve", lambda e, w=w: e.tensor_scalar(out=modT[:, l, w * 8:(w + 1) * 8, :], in0=modT[:, l, w * 8:(w + 1) * 8, :],
                                                               scalar1=1.0, scalar2=None, op0=ALU.add),
                         reads=[("mod", l)], writes=[("mod", l)], ss=True)
                for w in (2, 5):
                    P.op("dve", lambda e, w=w: e.tensor_scalar(out=modT[:, l, w * 8:(w + 1) * 8, :], in0=modT[:, l, w * 8:(w + 1) * 8, :],
                                                               scalar1=1.0, scalar2=1.0 / ALPHA, op0=ALU.add, op1=ALU.mult),
                         reads=[("mod", l)], writes=[("mod", l)], ss=True)
            cjob(run)

        def fuse_consts(l, which, ml, msc, msh):
            def run():
                gi = 0 if which == 0 else 2
                for s in range(NS):
                    P.op("dve", lambda e, s=s: e.tensor_tensor(out=g2c[:, l, which, :, s], in0=lnp[:, gi, l, :],
                                                                in1=modT[:, ml, msc * 8:(msc + 1) * 8, s], op=ALU.mult),
                         reads=["lnp", ("mod", ml)], writes=[("g2c", l)], ss=True)
                    P.op("dve", lambda e, s=s: e.tensor_tensor(out=b2c[:, l, which, :, s], in0=lnp[:, gi + 1, l, :],
                                                                in1=modT[:, ml, msc * 8:(msc + 1) * 8, s], op=ALU.mult),
                         reads=["lnp", ("mod", ml)], writes=[("g2c", l)], ss=True)
                    P.op("dve", lambda e, s=s: e.tensor_tensor(out=b2c[:, l, which, :, s], in0=b2c[:, l, which, :, s],
                                                                in1=modT[:, ml, msh * 8:(msh + 1) * 8, s], op=ALU.add),
                         reads=[("g2c", l), ("mod", ml)], writes=[("g2c", l)], ss=True)
            cjob(run)

        def swap_x(s_out, s_in, l0):
            def run():
                allk = []
                for tb in range(16):
                    blk = tb % 2
                    tsl = slice(tb * 128, (tb + 1) * 128)
                    XT = ("xt", tb)
                    allk.append(XT)
                    pt = pa if blk == 0 else pb
                    pk = PSA if blk == 0 else PSB
                    if s_in is not None:
                        lst = bufC[:, 2 * blk:2 * blk + 2, :].rearrange("p a b -> p (a b)")[:, 0:D]
                        lkeys = [CK[2 * blk], CK[2 * blk + 1]]
                        P.dma(lambda e, lst=lst, tb=tb: e.dma_start(out=lst, in_=xin[s_in, tb * 128:(tb + 1) * 128, :]),
                              writes=lkeys, chan=("xs", blk))
                    if s_out is not None:
                        sst = bufC[:, 4 + 2 * blk:6 + 2 * blk, :].rearrange("p a b -> p (a b)")[:, 0:D]
                        skeys = [CK[4 + 2 * blk], CK[5 + 2 * blk]]
                        for kc in range(KC):
                            P.op("pe", lambda e, kc=kc, pt=pt, tsl=tsl: e.transpose(out=pt[:, kc * 128:(kc + 1) * 128], in_=x[:, kc, tsl], identity=ident),
                                 reads=[XK[kc], XT, "cst"], writes=[pk[kc // 4]])
                        if blk == 0:
                            P.op("act", lambda e, sst=sst, pt=pt: e.activation(out=sst, in_=pt[:, 0:1024], func=AF.Copy),
                                 reads=[pk[0], pk[1]], writes=skeys)
                        else:
                            P.op("dve", lambda e, sst=sst, pt=pt: e.tensor_copy(out=sst, in_=pt[:, 0:1024]),
                                 reads=[pk[0], pk[1]], writes=skeys)
                        P.dma(lambda e, sst=sst, tb=tb: e.dma_start(out=yout[s_out, tb * 128:(tb + 1) * 128, :], in_=sst),
                              reads=skeys, writes=[("yout", s_out, tb)], chan=("ys", blk))
                    if s_in is not None:
                        for kc in range(KC):
                            P.op("pe", lambda e, lst=lst, kc=kc, pt=pt: e.transpose(out=pt[:, 1024 + kc * 128:1024 + (kc + 1) * 128], in_=lst[:, kc * 128:(kc + 1) * 128], identity=ident),
                                 reads=lkeys + ["cst"], writes=[pk[2 + kc // 4]])
                        if blk == 1:
                            P.op("act", lambda e, tsl=tsl, pt=pt: e.activation(out=x[:, :, tsl], in_=pt[:, 1024:2048].rearrange("p (k t) -> p k t", k=KC), func=AF.Copy),
                                 reads=[pk[2], pk[3]], writes=[XT])
                        else:
                            P.op("dve", lambda e, tsl=tsl, pt=pt: e.tensor_copy(out=x[:, :, tsl], in_=pt[:, 1024:2048].rearrange("p (k t) -> p k t", k=KC)),
                                 reads=[pk[2], pk[3]], writes=[XT])
                if s_in is None:
                    return
                P.op("dve", lambda e: e.memset(scT[:, 0:1], 0.0), reads=allk + ["scT"], writes=XK + ["scT"])
                for kc in range(KC if l0 is not None else 0):
                    if l0 % 2 == 1:
                        uo_ = ub[:, kc, :].rearrange("p (t c) -> p t c", t=8)
                        ui_ = x[:, kc, :].rearrange("p (c t) -> p t c", t=8)
                    else:
                        uo_ = ub[:, kc, :]
                        ui_ = x[:, kc, :]
                    P.op("act", lambda e, kc=kc, uo_=uo_, ui_=ui_: e.activation(out=uo_, in_=ui_, func=AF.Identity,
                                                              scale=modT[:, l0, 8 + kc, s_in:s_in + 1], bias=modT[:, l0, kc, s_in:s_in + 1]),
                         reads=[XK[kc], ("mod", l0)], writes=[UK[kc]])
            cjob(run)

        def conv_mixer(l, s):
            j = l // 2
            mid = bufB
            cgs = bufC[:, 0:2, :].rearrange("p a b -> p (a b)")
            chp = bufC[:, 2:5, :].rearrange("p a b -> p (a b)")
            tA = bufC[:, 5:7, :].rearrange("p a b -> p (a b)")
            CG = [CK[0], CK[1]]
            CH = [CK[2], CK[3], CK[4]]
            TA = [CK[5], CK[6]]

            def pre():
                P.op("dve", lambda e: e.memset(chp[:, 0:1], 0.0), writes=CH)
                P.op("dve", lambda e: e.memset(chp[:, 2049:2050], 0.0), writes=CH)
            cjob(pre)
            for f in range(KC):
                w5 = wb_in[j].rearrange("(k p) (c f o) -> p k c f o", p=128, c=3, f=KC)
                src = [(w5[:, :, c, f, :], (lambda t, c=c: t[:, :, c, :])) for c in range(3)]

                def run(tv, key, f=f):
                    for part, pt, pk in ((1, pa, PSA), (2, pb, PSB)):
                        for tt in range(NTT):
                            for kc in range(KC):
                                mm(pt[:, tt * TT:(tt + 1) * TT], tv[:, kc, part, :], ub[:, kc, tt * TT:(tt + 1) * TT],
                                   kc == 0, kc == KC - 1, key + [UK[kc]], [pk[tt]])
                        if part == 1:
                            P.op("act", lambda e: e.activation(out=cgs, in_=pa[:], func=AF.Copy), reads=PSA, writes=CG)
                        else:
                            P.op("dve", lambda e: e.tensor_tensor(out=chp[:, 1:2049], in0=pb[:], in1=cgs, op=ALU.mult),
                                 reads=PSB + CG, writes=CH)
                    for tt in range(NTT):
                        for kc in range(KC):
                            mm(pa[:, tt * TT:(tt + 1) * TT], tv[:, kc, 0, :], ub[:, kc, tt * TT:(tt + 1) * TT],
                               kc == 0, kc == KC - 1, key + [UK[kc]], [PSA[tt]])
                    P.op("act", lambda e: e.activation(out=tA, in_=chp[:, 1:2049], func=AF.Identity,
                                                       scale=sccw[:, j, 1, f:f + 1], bias=sccb[:, j, f:f + 1]),
                         reads=CH + ["sccw", "sccb"], writes=TA)
                    P.op("dve", lambda e: e.scalar_tensor_tensor(out=tA, in0=chp[:, 0:2048], scalar=sccw[:, j, 0, f:f + 1], in1=tA,
                                                                 op0=ALU.mult, op1=ALU.add), reads=CH + TA + ["sccw"], writes=TA)
                    P.op("dve", lambda e: e.scalar_tensor_tensor(out=tA, in0=chp[:, 2:2050], scalar=sccw[:, j, 2, f:f + 1], in1=tA,
                                                                 op0=ALU.mult, op1=ALU.add), reads=CH + TA + ["sccw"], writes=TA)
                    P.op("dve", lambda e: e.tensor_tensor(out=mid[:, f, :], in0=pa[:], in1=tA, op=ALU.mult),
                         reads=PSA + TA, writes=[BK[f]])
                wjob(src, lambda slot: bfview(slot, (KC, 3, 128)), run, scr_blocks[('in', j)])
            for q in range(2):
                src = wb_out[j].rearrange("(k p) o -> p k o", p=128)[:, :, q * 512:(q + 1) * 512]

                def run(tv, key, q=q):
                    for oo in range(4):
                        o = q * 4 + oo
                        pt, pk = (pa, PSA) if o % 2 == 0 else (pb, PSB)
                        for tt in range(NTT):
                            for kc in range(KC):
                                mm(pt[:, tt * TT:(tt + 1) * TT], tv[:, kc, oo * 128:(oo + 1) * 128], mid[:, kc, tt * TT:(tt + 1) * TT],
                                   kc == 0, kc == KC - 1, key + [BK[kc]], [pk[tt]])
                        P.op("dve", lambda e, o=o, pt=pt: e.scalar_tensor_tensor(out=x[:, o, :], in0=pt[:], scalar=modT[:, l, 16 + o, s:s + 1],
                                                                                 in1=x[:, o, :], op0=ALU.mult, op1=ALU.add),
                             reads=pk + [XK[o], ("mod", l)], writes=[XK[o]])
                wjob(src, lambda slot: bfview(slot, (KC, 512)), run, scr_blocks[('out', j)])

        def layer_norm(l, which, s, want_u, perm_u=False):
            gi = 0 if which == 0 else 2

            def run():
                xb = bufC[:, 0:2, :].rearrange("p a b -> p (a b)").bitcast(BF16).rearrange("p (k t) -> p k t", k=KC)
                sq = bufC[:, 2:4, :].rearrange("p a b -> p (a b)").bitcast(BF16).rearrange("p (k t) -> p k t", k=KC)
                XBK = [CK[0], CK[1]]
                SQK = [CK[2], CK[3]]
                allk = []

                def tile_ctx(tt):
                    par = tt % 2
                    sl = slice(tt * TT, (tt + 1) * TT)
                    LX = [("lnx", tt, kc) for kc in range(KC)]
                    ps1 = pa[:, (2 * par) * TT:(2 * par + 1) * TT]
                    ps2 = pa[:, (2 * par + 1) * TT:(2 * par + 2) * TT]
                    PK1, PK2 = PSA[2 * par], PSA[2 * par + 1]
                    c0, c1 = CK[4 + 2 * par], CK[5 + 2 * par]
                    stat = bufC[:, 4 + 2 * par:6 + 2 * par, :].rearrange("p a b -> p (a b)").rearrange("p (a t) -> p a t", a=4)
                    return par, sl, LX, ps1, ps2, PK1, PK2, c0, c1, stat

                def stage_a(tt):
                    par, sl, LX, ps1, ps2, PK1, PK2, c0, c1, stat = tile_ctx(tt)
                    allk.extend(LX)
                    P.op("act", lambda e: e.activation(out=xb, in_=x[:, :, sl], func=AF.Copy), reads=XK, writes=XBK)
                    P.op("act", lambda e: e.activation(out=sq, in_=x[:, :, sl], func=AF.Square), reads=XK, writes=SQK)
                    for kc in range(KC):
                        mm(ps1, ones_b[:], xb[:, kc, :], kc == 0, kc == KC - 1, XBK + ["ones"], [PK1])
                    for kc in range(KC):
                        mm(ps2, ones_b[:], sq[:, kc, :], kc == 0, kc == KC - 1, SQK + ["ones"], [PK2])

                def stage_b(tt):
                    par, sl, LX, ps1, ps2, PK1, PK2, c0, c1, stat = tile_ctx(tt)
                    mean = stat[:, 0, :]
                    msq = stat[:, 1, :]
                    var = stat[:, 2, :]
                    rstd = stat[:, 3, :]
                    s0, s1, s2, s3 = [("st", par, i) for i in range(4)]
                    P.op("dve", lambda e: e.tensor_scalar(out=mean, in0=ps1, scalar1=1.0 / D, scalar2=None, op0=ALU.mult),
                         reads=[PK1], writes=[c0, s0], ss=True)
                    P.op("dve", lambda e: e.tensor_tensor(out=msq, in0=mean, in1=mean, op=ALU.mult), reads=[s0, c0], writes=[c0, s1], ss=True)
                    P.op("dve", lambda e: e.scalar_tensor_tensor(out=var, in0=ps2, scalar=1.0 / D, in1=msq,
                                                                 op0=ALU.mult, op1=ALU.subtract), reads=[PK2, s1, c0], writes=[c1, s2], ss=True)
                    P.op("act", lambda e: e.activation(out=var, in_=var, func=AF.Sqrt, bias=cst[:, 130:131], scale=1.0),
                         reads=[s2, "cst", c1], writes=[c1, s2], ss=True)
                    P.op("dve", lambda e: e.reciprocal(out=rstd, in_=var), reads=[s2, c1], writes=[c1, s3], ss=True)
                    mb = mean.unsqueeze(1).to_broadcast([128, KC, TT])
                    rb = rstd.unsqueeze(1).to_broadcast([128, KC, TT])
                    P.op("dve", lambda e: e.tensor_tensor(out=x[:, :, sl], in0=x[:, :, sl], in1=mb, op=ALU.subtract),
                         reads=XK + [s0, c0], writes=LX)
                    P.op("dve", lambda e: e.tensor_tensor(out=x[:, :, sl], in0=x[:, :, sl], in1=rb, op=ALU.mult),
                         reads=LX + [s3, c1], writes=LX)

                def stage_c(tt):
                    par, sl, LX, ps1, ps2, PK1, PK2, c0, c1, stat = tile_ctx(tt)
                    for kc in range(KC):
                        if want_u:
                            if perm_u:
                                uo = ub[:, kc, :].rearrange("p (t c) -> p t c", t=8)[:, :, tt * 64:(tt + 1) * 64]
                                ui = x[:, kc, sl].rearrange("p (c t) -> p t c", t=8)
                            else:
                                uo = ub[:, kc, sl]
                                ui = x[:, kc, sl]
                            P.op("dve", lambda e, uo=uo, ui=ui, kc=kc: e.tensor_scalar(out=uo, in0=ui, scalar1=g2c[:, l, which, kc, s:s + 1],
                                                                                       scalar2=b2c[:, l, which, kc, s:s + 1], op0=ALU.mult, op1=ALU.add),
                                 reads=[LX[kc], ("g2c", l)], writes=[UK[kc]])
                        P.op("act", lambda e, kc=kc: e.activation(out=x[:, kc, sl], in_=x[:, kc, sl], func=AF.Identity,
                                                                 scale=lnp[:, gi, l, kc:kc + 1], bias=lnp[:, gi + 1, l, kc:kc + 1]),
                             reads=[LX[kc], "lnp"], writes=[LX[kc]])

                stage_a(0)
                for tt in range(NTT):
                    if tt + 1 < NTT:
                        stage_a(tt + 1)
                    stage_b(tt)
                    stage_c(tt)
                P.op("dve", lambda e: e.memset(scT[:, 0:1], 0.0), reads=allk + ["scT"], writes=XK + ["scT"])
            cjob(run)

        def ffn(l, s):
            g = bufB
            tA = bufC[:, 0:3, :].rearrange("p a b -> p (a b)")
            tV = bufC[:, 3:6, :].rearrange("p a b -> p (a b)")
            ga = bufC[:, 6, :].bitcast(BF16)
            TA = [CK[0], CK[1], CK[2]]
            TV = [CK[3], CK[4], CK[5]]
            GA = [CK[6]]
            for (m0, m1) in HGROUPS:
                nk = m1 - m0
                for mp in range(m0, m1, 2):
                    w5 = wb_up[l].rearrange("(k p) (c m o) -> p k c m o", p=128, c=2, m=FC)
                    src = [(w5[:, :, c, mp:mp + 2, :].rearrange("p k m o -> p k (m o)"),
                            (lambda t, c=c: t[:, :, c, :, :].rearrange("p k m o -> p k (m o)"))) for c in range(2)]

                    def run(tv, key, mp=mp, m0=m0):
                        for mi in range(2):
                            m = mp + mi
                            for c, pt, pk in ((0, pa, PSA), (1, pb, PSB)):
                                for tt in range(NTT):
                                    for kc in range(KC):
                                        mm(pt[:, tt * TT:(tt + 1) * TT], tv[:, kc, c, mi, :], ub[:, kc, tt * TT:(tt + 1) * TT],
                                           kc == 0, kc == KC - 1, key + [UK[kc]], [pk[tt]])
                                ch = c * FC + m
                                tt_, tk = (tA, TA) if c == 0 else (tV, TV)
                                P.op("act", lambda e, pt=pt, tt_=tt_, ch=ch: e.activation(out=tt_[:, 1:L + 1], in_=pt[:], func=AF.Identity,
                                                                                     scale=fcw[:, l, 1, ch:ch + 1], bias=fcb[:, l, ch:ch + 1]),
                                     reads=pk + ["fcw", "fcb"], writes=tk)
                                P.op("dve", lambda e, pt=pt, tt_=tt_, ch=ch: e.scalar_tensor_tensor(out=tt_[:, 2:L + 2], in0=pt[:], scalar=fcw[:, l, 0, ch:ch + 1],
                                                                                               in1=tt_[:, 2:L + 2], op0=ALU.mult, op1=ALU.add),
                                     reads=pk + tk + ["fcw"], writes=tk)
                                P.op("dve", lambda e, pt=pt, tt_=tt_, ch=ch: e.scalar_tensor_tensor(out=tt_[:, 0:L], in0=pt[:], scalar=fcw[:, l, 2, ch:ch + 1],
                                                                                               in1=tt_[:, 0:L], op0=ALU.mult, op1=ALU.add),
                                     reads=pk + tk + ["fcw"], writes=tk)
                                if stop == "ffn_dbg" and c == 0:
                                    b32 = bufB[:].rearrange("p a b -> p (a b)").bitcast(F32).rearrange("p (a b) -> p a b", a=4)
                                    P.op("act", lambda e: e.activation(out=b32[:, 0, :], in_=pa[:], func=AF.Copy), reads=PSA + TA, writes=BK)
                                    P.op("act", lambda e: e.activation(out=b32[:, 1, :], in_=tA[:, 1:L + 1], func=AF.Copy), reads=TA, writes=BK)
                                    P.op("act", lambda e: e.activation(out=ga, in_=tA[:, 1:L + 1], func=AF.Gelu_apprx_tanh), reads=TA, writes=GA)
                                    P.op("act", lambda e: e.activation(out=b32[:, 2, :], in_=ga, func=AF.Copy), reads=GA, writes=BK)
                                    P.dma(lambda e: e.dma_start(out=dbg[:], in_=bufB[:]), reads=BK, writes=["dbg"], chan="dbg")
                                    return
                                if c == 0:
                                    P.op("act", lambda e: e.activation(out=ga, in_=tA[:, 1:L + 1], func=AF.Gelu_apprx_tanh), reads=TA, writes=GA)
                                else:
                                    P.op("dve", lambda e, m=m, m0=m0: e.tensor_tensor(out=g[:, m - m0, :], in0=tV[:, 1:L + 1], in1=ga, op=ALU.mult),
                                         reads=TV + GA, writes=[BK[m - m0]])
                    wjob(src, lambda slot: bfview(slot, (KC, 2, 2, 128)), run, scr_blocks[('up', l)])
                if stop == "ffn_dbg":
                    return
                if stop == "ffn_g0":
                    def dump():
                        P.dma(lambda e: e.dma_start(out=dbg[:], in_=bufB[:]), reads=BK, writes=["dbg"], chan="dbg")
                    cjob(dump)
                    return
                for q in range(4):
                    src = wb_dn[l].rearrange("(k p) o -> p k o", p=128)[:, m0:m1, q * 256:(q + 1) * 256]

                    def run(tv, key, q=q, nk=nk):
                        for oo in range(2):
                            o = q * 2 + oo
                            pt, pk = (pa, PSA) if o % 2 == 0 else (pb, PSB)
                            for tt in range(NTT):
                                for kc in range(nk):
                                    mm(pt[:, tt * TT:(tt + 1) * TT], tv[:, kc, oo * 128:(oo + 1) * 128], g[:, kc, tt * TT:(tt + 1) * TT],
                                       kc == 0, kc == nk - 1, key + [BK[kc]], [pk[tt]])
                            P.op("dve", lambda e, o=o, pt=pt: e.scalar_tensor_tensor(out=x[:, o, :], in0=pt[:], scalar=modT[:, l, 40 + o, s:s + 1],
                                                                                     in1=x[:, o, :], op0=ALU.mult, op1=ALU.add),
                                 reads=pk + [XK[o], ("mod", l)], writes=[XK[o]])
                    wjob(src, lambda slot, nk=nk: bfview(slot, (nk, 256)), run, scr_blocks[('dn', l)])

        xs = x[:].rearrange("p a b -> p (a b)")
        TWO_PI = 6.283185307179586

        def s5_setup(j):
            KT = ["s5tab", XK[0], XK[1]]
            A = xs[:, 0:4096]

            def slot(i, n=272):
                return A[:, i * 272:i * 272 + n]

            def sm(i):
                return A[:, 3840 + 16 * i:3840 + 16 * i + 16]

            def t3(ap):
                return ap.rearrange("p (k g) -> p k g", g=16)

            def D_(fn, reads=(), writes=(), eng="dve"):
                P.op(eng, fn, reads=list(reads) + KT, writes=list(writes) + KT, ss=True)

            kv = cst[:, 131:148]
            kvb = kv.unsqueeze(2).to_broadcast([128, 17, 16])
            sgn = cst[:, 128:129]
            tmp1 = xs[:, 4096:8192]
            tmp2 = xs[:, 8192:12288]
            T1K = [XK[2], XK[3]]
            T2K = [XK[4], XK[5]]
            dbase = 12288
            nat = xs[:, dbase:dbase + 256]
            ub32 = ub[:].rearrange("p a b -> p (a b)").bitcast(F32)
            cnat = ub32[:, 0:2048].rearrange("p (g r) -> p g r", g=16)
            cnat2 = ub32[:, 2048:4096].rearrange("p (g r) -> p g r", g=16)
            DK = [XK[6], XK[7], UK[0], UK[1], UK[2], UK[3]]

            def dtile(d, i):
                o = dbase + 256 + (d * 4 + i) * 256
                return xs[:, o:o + 256].rearrange("p (g c) -> p g c", g=16)

            hz = [bufC[:, 0:4, :].rearrange("p a b -> p (a b)"), bufC[:, 4:8, :].rearrange("p a b -> p (a b)")]
            HZK = [[CK[0], CK[1], CK[2], CK[3]], [CK[4], CK[5], CK[6], CK[7]]]
            hcout = bufB[:, 0:2, :].rearrange("p a b -> p (a b)")
            xout = bufB[:, 2:4, :].rearrange("p a b -> p (a b)")
            pbb = pb[:].bitcast(BF16)

            def first():
                P.dma(lambda e: e.dma_start(out=dcol[:, j, :], in_=s5_d[j].rearrange("(g c) -> c g", c=16), allow_slow_non_contiguous=True),
                      writes=[("dcol", j)], chan="s5d", eng="act")

            def expand(out4, Tre, Tim, D1, D2, okeys, rkeys):
                a0 = Tre.rearrange("p k g -> p g k").unsqueeze(3).to_broadcast([128, 16, 16, 16])
                a1 = Tim.rearrange("p k g -> p g k").unsqueeze(3).to_broadcast([128, 16, 16, 16])
                b0 = D1.unsqueeze(2).to_broadcast([128, 16, 16, 16])
                b1 = D2.unsqueeze(2).to_broadcast([128, 16, 16, 16])
                t1 = tmp1.rearrange("p (g k c) -> p g k c", g=16, k=16)
                t2 = tmp2.rearrange("p (g k c) -> p g k c", g=16, k=16)
                P.op("dve", lambda e: e.tensor_tensor(out=t1, in0=a0, in1=b0, op=ALU.mult), reads=KT + DK + rkeys, writes=T1K)
                P.op("dve", lambda e: e.tensor_tensor(out=t2, in0=a1, in1=b1, op=ALU.mult), reads=KT + DK + rkeys, writes=T2K)
                P.op("dve", lambda e: e.tensor_tensor(out=out4, in0=t1, in1=t2, op=ALU.add), reads=T1K + T2K, writes=okeys)

            def per_qd(q, d, g0):
                if True:
                    for h in range(2):
                        P.dma(lambda e, h=h: e.dma_start(out=nat[0:16, 64 * h:64 * h + 64], in_=s5_a_re[j, d, g0:g0 + 16, :]),
                              writes=[("nat", h)] + DK, chan=("s5n", h), eng="act")
                        P.dma(lambda e, h=h: e.dma_start(out=nat[0:16, 128 + 64 * h:192 + 64 * h], in_=s5_a_im[j, d, g0:g0 + 16, :]),
                              writes=[("nat", 2 + h)] + DK, chan=("s5n", 2 + h), eng="act")
                    P.dma(lambda e: e.dma_start(out=sm(2), in_=s5_log_dt[j, d:d + 1, g0:g0 + 16].to_broadcast([128, 16])),
                          writes=["s5dt"] + KT, chan="s5dt", eng="act")
                    NK = [("nat", i) for i in range(4)]
                    P.op("pe", lambda e: e.transpose(out=pa[:, 0:16], in_=nat[0:16, 0:128], identity=ident[0:16, 0:16]),
                         reads=NK + DK + ["cst"], writes=[PSA[0]])
                    P.op("pe", lambda e: e.transpose(out=pa[:, 16:32], in_=nat[0:16, 128:256], identity=ident[0:16, 0:16]),
                         reads=NK + DK + ["cst"], writes=[PSA[0]])
                    are, aim, dtb, x1, q1, den, rden, e1, fre, fim, u1, u2 = [sm(i) for i in range(12)]
                    D_(lambda e: e.tensor_copy(out=A[:, 3840:3872], in_=pa[:, 0:32]), reads=[PSA[0]])
                    D_(lambda e: e.activation(out=dtb, in_=dtb, func=AF.Exp), reads=["s5dt"], eng="act")
                    D_(lambda e: e.tensor_tensor(out=x1, in0=are, in1=dtb, op=ALU.mult))
                    D_(lambda e: e.tensor_tensor(out=q1, in0=aim, in1=dtb, op=ALU.mult))
                    D_(lambda e: e.tensor_scalar(out=q1, in0=q1, scalar1=1.0 / TWO_PI, scalar2=None, op0=ALU.mult))
                    xk, mag, qs, fr, nf, mk, sn, cs, Pre, Pim, PFre, PFim, w1, w2 = [slot(i) for i in range(14)]
                    ni = slot(4).bitcast(I32)
                    D_(lambda e: e.tensor_tensor(out=t3(xk), in0=kvb, in1=x1.unsqueeze(1).to_broadcast([128, 17, 16]), op=ALU.mult))
                    D_(lambda e: e.activation(out=mag, in_=xk, func=AF.Exp), eng="act")
                    D_(lambda e: e.tensor_tensor(out=t3(qs), in0=kvb, in1=q1.unsqueeze(1).to_broadcast([128, 17, 16]), op=ALU.mult))

                    def range_reduce(src):
                        D_(lambda e: e.tensor_copy(out=ni, in_=src))
                        D_(lambda e: e.tensor_copy(out=nf, in_=ni))
                        D_(lambda e: e.tensor_tensor(out=fr, in0=src, in1=nf, op=ALU.subtract))
                        D_(lambda e: e.tensor_single_scalar(out=mk, in_=fr, scalar=0.5, op=ALU.is_gt))
                        D_(lambda e: e.tensor_tensor(out=fr, in0=fr, in1=mk, op=ALU.subtract))
                        D_(lambda e: e.tensor_single_scalar(out=mk, in_=fr, scalar=-0.5, op=ALU.is_lt))
                        D_(lambda e: e.tensor_tensor(out=fr, in0=fr, in1=mk, op=ALU.add))
                    range_reduce(qs)
                    D_(lambda e: e.activation(out=sn, in_=fr, func=AF.Sin, scale=TWO_PI), eng="act")
                    D_(lambda e: e.tensor_scalar(out=qs, in0=qs, scalar1=0.25, scalar2=None, op0=ALU.add))
                    range_reduce(qs)
                    D_(lambda e: e.activation(out=cs, in_=fr, func=AF.Sin, scale=TWO_PI), eng="act")
                    D_(lambda e: e.tensor_tensor(out=Pre, in0=mag, in1=cs, op=ALU.mult))
                    D_(lambda e: e.tensor_tensor(out=Pim, in0=mag, in1=sn, op=ALU.mult))
                    P1re = t3(Pre)[:, 1, :]
                    P1im = t3(Pim)[:, 1, :]
                    D_(lambda e: e.tensor_scalar(out=e1, in0=P1re, scalar1=-1.0, scalar2=None, op0=ALU.add))
                    D_(lambda e: e.tensor_tensor(out=den, in0=are, in1=are, op=ALU.mult))
                    D_(lambda e: e.tensor_tensor(out=u1, in0=aim, in1=aim, op=ALU.mult))
                    D_(lambda e: e.tensor_tensor(out=den, in0=den, in1=u1, op=ALU.add))
                    D_(lambda e: e.reciprocal(out=rden, in_=den))
                    D_(lambda e: e.tensor_tensor(out=fre, in0=e1, in1=are, op=ALU.mult))
                    D_(lambda e: e.tensor_tensor(out=u1, in0=P1im, in1=aim, op=ALU.mult))
                    D_(lambda e: e.tensor_tensor(out=fre, in0=fre, in1=u1, op=ALU.add))
                    D_(lambda e: e.tensor_tensor(out=fre, in0=fre, in1=rden, op=ALU.mult))
                    D_(lambda e: e.tensor_tensor(out=fim, in0=P1im, in1=are, op=ALU.mult))
                    D_(lambda e: e.tensor_tensor(out=u1, in0=e1, in1=aim, op=ALU.mult))
                    D_(lambda e: e.tensor_tensor(out=fim, in0=fim, in1=u1, op=ALU.subtract))
                    D_(lambda e: e.tensor_tensor(out=fim, in0=fim, in1=rden, op=ALU.mult))
                    freb = fre.unsqueeze(1).to_broadcast([128, 16, 16])
                    fimb = fim.unsqueeze(1).to_broadcast([128, 16, 16])
                    P16r = t3(Pre)[:, 0:16, :]
                    P16i = t3(Pim)[:, 0:16, :]
                    W1 = t3(w1)[:, 0:16, :]
                    W2 = t3(w2)[:, 0:16, :]
                    D_(lambda e: e.tensor_tensor(out=W1, in0=P16r, in1=freb, op=ALU.mult))
                    D_(lambda e: e.tensor_tensor(out=W2, in0=P16i, in1=fimb, op=ALU.mult))
                    D_(lambda e: e.tensor_tensor(out=t3(PFre)[:, 0:16, :], in0=W1, in1=W2, op=ALU.subtract))
                    D_(lambda e: e.tensor_tensor(out=W1, in0=P16r, in1=fimb, op=ALU.mult))
                    D_(lambda e: e.tensor_tensor(out=W2, in0=P16i, in1=freb, op=ALU.mult))
                    D_(lambda e: e.tensor_tensor(out=t3(PFim)[:, 0:16, :], in0=W1, in1=W2, op=ALU.add))
                    D_(lambda e: e.tensor_copy(out=scA[:, j, 0, d, g0:g0 + 16], in_=t3(Pre)[:, 16, :]), writes=[("scA", j)])
                    D_(lambda e: e.tensor_scalar(out=scA[:, j, 1, d, g0:g0 + 16], in0=t3(Pim)[:, 16, :], scalar1=sgn, scalar2=None, op0=ALU.mult),
                       reads=["cst"], writes=[("scA", j)])
                    CTn, CTs2, BB, BBs2 = [dtile(d, i) for i in range(4)]
                    P.dma(lambda e: e.dma_start(out=cnat[0:16, :, 0:64], in_=s5_c_re[j, d, g0:g0 + 16].rearrange("g i p -> i g p")), writes=[("cn", 0)] + DK, chan=("s5c", 0), eng="act")
                    P.dma(lambda e: e.dma_start(out=cnat[0:16, :, 64:128], in_=s5_c_im[j, d, g0:g0 + 16].rearrange("g i p -> i g p")), writes=[("cn", 1)] + DK, chan=("s5c", 1), eng="act")
                    P.dma(lambda e: e.dma_start(out=cnat2[0:16, :, 0:64], in_=s5_c_im[j, d, g0:g0 + 16].rearrange("g i p -> i g p")), writes=[("cn", 2)] + DK, chan=("s5c", 2), eng="act")
                    P.dma(lambda e: e.dma_start(out=cnat2[0:16, :, 64:128], in_=s5_c_re[j, d, g0:g0 + 16].rearrange("g i p -> i g p")), writes=[("cn", 3)] + DK, chan=("s5c", 3), eng="act")
                    CNK = [("cn", i) for i in range(4)]
                    for gl in range(16):
                        P.op("pe", lambda e, gl=gl: e.transpose(out=pa[:, 512 + gl * 16:528 + gl * 16], in_=cnat[0:16, gl, :], identity=ident[0:16, 0:16]),
                             reads=CNK + DK + ["cst"], writes=[PSA[1]])
                        P.op("pe", lambda e, gl=gl: e.transpose(out=pa[:, 1024 + gl * 16:1040 + gl * 16], in_=cnat2[0:16, gl, :], identity=ident[0:16, 0:16]),
                             reads=CNK + DK + ["cst"], writes=[PSA[2]])
                    P.op("dve", lambda e: e.tensor_scalar(out=CTn, in0=pa[:, 512:768].rearrange("p (g c) -> p g c", g=16), scalar1=sgn, scalar2=None, op0=ALU.mult),
                         reads=[PSA[1], "cst"], writes=DK + [("dt", d)], ss=True)
                    P.op("dve", lambda e: e.tensor_scalar(out=CTs2, in0=pa[:, 1024:1280].rearrange("p (g c) -> p g c", g=16), scalar1=-1.0, scalar2=None, op0=ALU.mult),
                         reads=[PSA[2]], writes=DK + [("dt", d)], ss=True)
                    P.dma(lambda e: e.dma_start(out=BB[0:64], in_=s5_b_re[j, d, g0:g0 + 16].rearrange("g p c -> p g c")), writes=[("bb", 0)] + DK, chan=("s5b", 0), eng="act")
                    P.dma(lambda e: e.dma_start(out=BB[64:128], in_=s5_b_im[j, d, g0:g0 + 16].rearrange("g p c -> p g c")), writes=[("bb", 1)] + DK, chan=("s5b", 1), eng="act")
                    P.dma(lambda e: e.dma_start(out=BBs2[0:64], in_=s5_b_im[j, d, g0:g0 + 16].rearrange("g p c -> p g c")), writes=[("bb", 2)] + DK, chan=("s5b", 2), eng="act")
                    P.dma(lambda e: e.dma_start(out=BBs2[64:128], in_=s5_b_re[j, d, g0:g0 + 16].rearrange("g p c -> p g c")), writes=[("bb", 3)] + DK, chan=("s5b", 3), eng="act")
                    BBK = [("bb", i) for i in range(4)]
                    P.op("dve", lambda e: e.tensor_scalar(out=BBs2, in0=BBs2, scalar1=sgn, scalar2=-1.0, op0=ALU.mult, op1=ALU.mult),
                         reads=BBK + ["cst"], writes=DK + [("dt", d), ("bb", 2), ("bb", 3)], ss=True)
                    DTK = [("dt", d)] + BBK
                    TPre, TPim, TFre, TFim = t3(Pre), t3(Pim), t3(PFre), t3(PFim)
                    if d == 0:
                        cre, cim = TPre[:, 1:17, :], TPim[:, 1:17, :]
                    else:
                        cre, cim = TPre[:, 16:0:-1, :], TPim[:, 16:0:-1, :]
                    hc4 = hcout.rearrange("p (g k c) -> p g k c", g=16, k=16)
                    expand(hc4, cre, cim, CTn, CTs2, [BK[0], BK[1]], DTK)
                    dstc = twc[j, g0:g0 + 16].rearrange("g p s c -> p g (s c)")[:, :, (3 + 2 * d) * 128:(5 + 2 * d) * 128]
                    P.dma(lambda e, dstc=dstc: e.dma_start(out=dstc, in_=hcout.rearrange("p (g r) -> p g r", g=16)),
                          reads=[BK[0], BK[1]], writes=[("twc", j, q, "c", d)], chan="s5o0", eng="act")
                    if d == 0:
                        zre, zim = TFre[:, 0:16, :], TFim[:, 0:16, :]
                    else:
                        zre, zim = TFre[:, 15::-1, :], TFim[:, 15::-1, :]
                    hz4 = hz[d].rearrange("p (g k c) -> p g k c", g=16, k=16)
                    expand(hz4, zre, zim, CTn, CTs2, HZK[d], DTK)
                    if d == 0:
                        bre, bim = TFre[:, 15::-1, :], TFim[:, 15::-1, :]
                    else:
                        bre, bim = TFre[:, 0:16, :], TFim[:, 0:16, :]
                    x4 = xout.rearrange("p (g k c) -> p g k c", g=16, k=16)
                    expand(x4, bre, bim, BB, BBs2, [BK[2], BK[3]], DTK)
                    xo3 = xout.rearrange("p (g r) -> p g r", g=16)
                    for g4 in range(4):
                        half = g4 % 2
                        PK = [PSB[2 * half], PSB[2 * half + 1]]
                        for gg in range(4):
                            gl = g4 * 4 + gg
                            for b in range(2):
                                idx = gg * 2 + b
                                P.op("pe", lambda e, gl=gl, b=b, idx=idx, half=half: e.transpose(
                                    out=pbb[:, half * 2048 + idx * 128:half * 2048 + (idx + 1) * 128],
                                    in_=xo3[:, gl, b * 128:(b + 1) * 128], identity=identb[:]),
                                    reads=[BK[2], BK[3], "identb"], writes=PK)
                        stg = bufB[:, 4 + half, 0:1024]
                        P.op("act", lambda e, stg=stg, half=half: e.activation(out=stg, in_=pbb[:, half * 2048:half * 2048 + 1024], func=AF.Copy),
                             reads=PK, writes=[BK[4 + half]])
                        dstb = twb[j, g0 + g4 * 4:g0 + g4 * 4 + 4].rearrange("g p s c -> p g (s c)")[:, :, 2 * d * 128:(2 * d + 2) * 128]
                        P.dma(lambda e, dstb=dstb, stg=stg: e.dma_start(out=dstb, in_=stg.rearrange("p (g r) -> p g r", g=4)),
                              reads=[BK[4 + half]], writes=[("twb", j, q, g4, d)], chan=("s5o1", half), eng="act")
            def per_q(q, g0):
                BBf = dtile(0, 2)
                BBb = dtile(1, 2)
                hzf = hz[0].rearrange("p (g r) -> p g r", g=16)
                hzb = hz[1].rearrange("p (g r) -> p g r", g=16)
                P.op("sp", lambda e: e.nop(), reads=[], writes=[BK[6], ("zs", 0), ("zs", 1), ("zs", 2), ("zs", 3)])
                for gl in range(16):
                    g = g0 + gl
                    zpt, zpk = (pa, PSA[3]) if gl % 2 == 0 else (pb, PSB[3])
                    P.op("pe", lambda e, gl=gl, zpt=zpt: e.matmul(zpt[0:16, 1536 + 240:1536 + 496], lhsT=BBf[:, gl, :], rhs=hzf[:, gl, :], start=True, stop=False),
                         reads=HZK[0] + DK + [("dt", 0), ("bb", 0), ("bb", 1)], writes=[zpk])
                    P.op("pe", lambda e, gl=gl, zpt=zpt: e.matmul(zpt[0:16, 1536:1536 + 256], lhsT=BBb[:, gl, :], rhs=hzb[:, gl, :], start=False, stop=True),
                         reads=HZK[1] + DK + [("dt", 1), ("bb", 0), ("bb", 1)], writes=[zpk])
                    zi = gl % 4
                    zs = bufB[0:16, 6, zi * 512:zi * 512 + 496]
                    ZK = ("zs", zi)
                    P.op("act", lambda e, zs=zs, zpt=zpt: e.activation(out=zs, in_=zpt[0:16, 1536:2032], func=AF.Copy), reads=[zpk], writes=[ZK])
                    P.op("dve", lambda e, zs=zs, g=g, zpt=zpt: e.scalar_tensor_tensor(out=zs[:, 240:256], in0=cst[0:16, 0:16], scalar=dcol[:, j, g:g + 1],
                                                                            in1=zpt[0:16, 1536 + 240:1536 + 256], op0=ALU.mult, op1=ALU.add),
                         reads=[zpk, ZK, "cst", ("dcol", j)], writes=[ZK], ss=True)
                    zt = bufB[:, 6, :]
                    for di in range(3):
                        off0 = zi * 512 + (15 + 8 * (di - 1)) * 16
                        src = AP(zt.tensor, zt.offset + off0, [[zt.ap[0][0], 16], [-16, 8], [1, 128]])
                        dstz = twc[j, g, :, di, :].rearrange("(s c) r -> c s r", c=16)
                        P.dma(lambda e, src=src, dstz=dstz: e.dma_start(out=dstz, in_=src),
                              reads=[ZK], writes=[("twc", j, q, "z", gl, di)], chan=("s5z", zi), eng="act")
                P.op("sp", lambda e: e.nop(), reads=[("zs", 0), ("zs", 1), ("zs", 2), ("zs", 3)], writes=[BK[6]])

            stages = [first]
            for q in range(4):
                stages.append(lambda q=q: per_qd(q, 0, 16 * q))
                stages.append(lambda q=q: (per_qd(q, 1, 16 * q), per_q(q, 16 * q)))
            return stages

        def s5_dump():
            P.dma(lambda e: e.dma_start(out=dbg32[:, :], in_=xs[:, :]), reads=["s5tab"] + XK, writes=["dbg"], chan="dbg")

        def s5_keys(j):
            ks = []
            for q in range(4):
                for d in range(2):
                    ks.append(("twc", j, q, "c", d))
                    for g4 in range(4):
                        ks.append(("twb", j, q, g4, d))
                for gl in range(16):
                    for di in range(3):
                        ks.append(("twc", j, q, "z", gl, di))
            return ks

        def s5_mixer(l, s):
            j = l // 2
            U = bufB[:].rearrange("p a b -> p (a b)").rearrange("p (g c) -> p g c", g=G)
            S = CU[:, :].rearrange("p (m c) -> p m c", m=128)
            CUb = CU[:, :].bitcast(BF16)
            Sbf = CUb[:, 0:8192].rearrange("p (m c) -> p m c", m=64)
            Sbn = CUb[:, 8192:16384].rearrange("p (m c) -> p m c", m=64)
            T3 = scT[:, 0:128].rearrange("p (d g) -> p d g", d=2)
            P3 = scT[:, 128:256].rearrange("p (d g) -> p d g", d=2)
            UD = [("Ud", t) for t in range(8)]
            tkeys = s5_keys(j)

            def bounce_in():
                P.op("sp", lambda e: e.nop(), reads=[("Ug", g) for g in range(G)], writes=["Ufree"] + BK)
                for kc in range(KC):
                    P.dma(lambda e, kc=kc: e.dma_start(out=d_in[:, kc * 128:(kc + 1) * 128, :].rearrange("t f c -> f t c"),
                                                       in_=ub[:, kc, :].rearrange("p (t c) -> p t c", t=8)),
                          reads=[UK[kc]], writes=[("din", kc)], chan=("bi", kc % 4))
                for kc in range(KC):
                    for t in range(8):
                        P.dma(lambda e, t=t, kc=kc: e.dma_start(out=U[t * 16:(t + 1) * 16, kc * 8:(kc + 1) * 8, :],
                                                              in_=d_in[t, kc * 128:(kc + 1) * 128, :].rearrange("(g j) c -> j g c", j=16)),
                              reads=[("din", kc), "Ufree"], writes=[("Ud", kc, t)], chan=("bu", kc))
            cjob(bounce_in)

            for tix in range(8):
                gb = tix * 8
                src = twb[j, gb:gb + 8].rearrange("g p s c -> p g (s c)")

                def run(tv, key, gb=gb, tix=tix):
                    tv = tv.rearrange("p g (s c) -> p g s c", s=4)
                    pt, pk = (pa, PSA) if tix % 2 == 0 else (pb, PSB)
                    for d in range(2):
                        for gl in range(8):
                            g = gb + gl
                            o = pt[:, (d * 8 + gl) * 128:(d * 8 + gl + 1) * 128]
                            kk = pk[(d * 8 + gl) // 4]
                            udk = [("Ud", g // 8, t_) for t_ in range(8)]
                            mm(o, tv[:, gl, 2 * d, :], U[:, g, 0:256:2], True, False, key + udk + [("Ug", g)], [kk])
                            mm(o, tv[:, gl, 2 * d + 1, :], U[:, g, 1:256:2], False, True, key + udk + [("Ug", g)], [kk])
                    P.op("act", lambda e, pt=pt, gb=gb: e.activation(out=S[:, gb:gb + 8, :], in_=pt[:, 0:1024].rearrange("p (g c) -> p g c", g=8), func=AF.Copy),
                         reads=[pk[0], pk[1]], writes=CK)
                    P.op("dve", lambda e, pt=pt, gb=gb: e.tensor_copy(out=S[:, 64 + gb:64 + gb + 8, :],
                                                                    in_=pt[:, 1024:2048].rearrange("p (g c) -> p g c", g=8)[:, :, 127::-1]),
                         reads=[pk[2], pk[3]], writes=UK)
                wjob(src, lambda slot: bfview(slot, (8, 4, 128)).rearrange("p g s c -> p g (s c)"), run, tkeys)

            def scan():
                Ar = scA[:, j, 0, :, :]
                AiN = scA[64:128, j, 1, :, :]
                AiP = scA[0:64, j, 1, :, :]
                SK = CK + UK
                for n in range(1, 128):
                    prev = S[:, :, n - 1].rearrange("p (d g) -> p d g", d=2)
                    cur = S[:, :, n].rearrange("p (d g) -> p d g", d=2)
                    kp, kc_ = ("scS", (n - 1) % 2), ("scS", n % 2)
                    base = SK + [("scA", j)]
                    P.op("dve", lambda e, prev=prev: e.tensor_tensor(out=P3, in0=prev, in1=Ar, op=ALU.mult),
                         reads=base + [kp], writes=["scP"], ss=True)
                    P.op("dve", lambda e, prev=prev: e.tensor_tensor(out=T3[0:64], in0=prev[64:128], in1=AiN, op=ALU.mult),
                         reads=base + [kp], writes=["scTa"], ss=True)
                    P.op("pool", lambda e, prev=prev: e.tensor_tensor(out=T3[64:128], in0=prev[0:64], in1=AiP, op=ALU.mult),
                         reads=base + [kp], writes=["scTb"], ss=True)
                    P.op("dve", lambda e, cur=cur: e.tensor_tensor(out=cur, in0=cur, in1=P3, op=ALU.add),
                         reads=base + ["scP", kc_], writes=[kc_], ss=True)
                    P.op("dve", lambda e, cur=cur: e.tensor_tensor(out=cur, in0=cur, in1=T3, op=ALU.add),
                         reads=base + ["scTa", "scTb", kc_], writes=[kc_], ss=True)
                P.op("act", lambda e: e.activation(out=Sbf, in_=S[:, 0:64, :], func=AF.Copy), reads=CK + [("scS", 0), ("scS", 1)], writes=CK, ss=True)
                P.op("act", lambda e: e.activation(out=Sbn, in_=S[:, 64:128, 127::-1], func=AF.Copy), reads=CK + UK + [("scS", 0), ("scS", 1)], writes=CK, ss=True)
            cjob(scan)

            for tix in range(16):
                gb = tix * 4
                src = twc[j, gb:gb + 4].rearrange("g p s c -> p g (s c)")

                def run(tv, key, gb=gb, tix=tix):
                    tv = tv.rearrange("p g (s c) -> p g s c", s=7)
                    pt, pk = (pa, PSA) if tix % 2 == 0 else (pb, PSB)
                    for gl in range(4):
                        g = gb + gl
                        for bo in range(2):
                            o = pt[:, (gl * 2 + bo) * 128:(gl * 2 + bo + 1) * 128]
                            kk = pk[(gl * 2 + bo) // 4]
                            rk = key + [("Ud", g // 8, t_) for t_ in range(8)] + [("Ug", g)]
                            mm(o, tv[:, gl, bo + 1, :], U[:, g, 0:256:2], True, False, rk, [kk])
                            mm(o, tv[:, gl, bo, :], U[:, g, 1:256:2], False, False, rk, [kk])
                            mm(o[:, 1:128], tv[:, gl, 3 + bo, :], Sbf[:, g, 0:127], False, False, key + CK, [kk])
                            mm(o[:, 0:127], tv[:, gl, 5 + bo, :], Sbn[:, g, 1:128], False, True, key + CK, [kk])
                    yo = U[:, gb:gb + 4, :].rearrange("p g (c b) -> p g b c", b=2)
                    yi = pt[:, 0:1024].rearrange("p (g b c) -> p g b c", g=4, b=2)
                    P.op("act", lambda e, yo=yo, yi=yi: e.activation(out=yo, in_=yi, func=AF.Gelu_apprx_tanh),
                         reads=[pk[0], pk[1]], writes=[("Ug", g_) for g_ in range(gb, gb + 4)] + [("Yd", gb // 4)])
                wjob(src, lambda slot: bfview(slot, (4, 7, 128)).rearrange("p g s c -> p g (s c)"), run, tkeys)

            if stop == "s5_yg":
                def dump():
                    P.dma(lambda e: e.dma_start(out=dbg[:], in_=bufB[:]), reads=[("Yd", i) for i in range(16)] + BK, writes=["dbg"], chan="dbg")
                cjob(dump)
                return
            def bounce_out():
                for kc in range(KC):
                    ykeys = [("Yd", 2 * kc), ("Yd", 2 * kc + 1)] + [("Ug", g) for g in range(kc * 8, kc * 8 + 8)]
                    for t in range(8):
                        P.dma(lambda e, t=t, kc=kc: e.dma_start(out=d_out[t, kc * 128:(kc + 1) * 128, :].rearrange("(g j) c -> j g c", j=16),
                                                              in_=U[t * 16:(t + 1) * 16, kc * 8:(kc + 1) * 8, :]),
                              reads=ykeys, writes=[("dout", kc, t)], chan=("bo", kc))
                    P.dma(lambda e, kc=kc: e.dma_start(out=ub[:, kc, :].rearrange("p (t c) -> p t c", t=8),
                                                       in_=d_out[:, kc * 128:(kc + 1) * 128, :].rearrange("t f c -> f t c")),
                          reads=[("dout", kc, t) for t in range(8)], writes=[UK[kc]], chan=("bz", kc % 4))
            cjob(bounce_out)

            sg = bufC[:, 0:2, :].rearrange("p a b -> p (a b)")
            mt = bufC[:, 2:4, :].rearrange("p a b -> p (a b)")
            for q in range(4):
                w4 = wb_glu[j].rearrange("(k p) (c o) -> p k c o", p=128, c=2)
                src = [(w4[:, :, c, q * 256:(q + 1) * 256], (lambda t, c=c: t[:, :, c, :])) for c in range(2)]

                def run(tv, key, q=q):
                    for oo in range(2):
                        o = q * 2 + oo
                        for c, pt, pk in ((1, pb, PSB), (0, pa, PSA)):
                            for tt in range(NTT):
                                for kc in range(KC):
                                    mm(pt[:, tt * TT:(tt + 1) * TT], tv[:, kc, c, oo * 128:(oo + 1) * 128], ub[:, kc, tt * TT:(tt + 1) * TT],
                                       kc == 0, kc == KC - 1, key + [UK[kc]], [pk[tt]])
                            if c == 1:
                                P.op("act", lambda e: e.activation(out=sg, in_=pb[:], func=AF.Sigmoid), reads=PSB, writes=[CK[0], CK[1]])
                        P.op("dve", lambda e: e.tensor_tensor(out=mt, in0=pa[:], in1=sg, op=ALU.mult), reads=PSA + [CK[0], CK[1]], writes=[CK[2], CK[3]])
                        P.op("dve", lambda e, o=o: e.scalar_tensor_tensor(out=x[:, o, :].rearrange("p (c t) -> p c t", t=8),
                                                                         in0=mt.rearrange("p (t c) -> p c t", t=8), scalar=modT[:, l, 16 + o, s:s + 1],
                                                                         in1=x[:, o, :].rearrange("p (c t) -> p c t", t=8), op0=ALU.mult, op1=ALU.add),
                             reads=[CK[2], CK[3], XK[o], ("mod", l)], writes=[XK[o]])
                wjob(src, lambda slot: bfview(slot, (KC, 2, 256)), run, scr_blocks[('glu', j)])


        P.op("dve", lambda e: e.tensor_copy(out=identb[:], in_=ident), reads=["cst"], writes=["identb"])
        pump(0)
        for l in layers:
            ada_jobs(l)
        for ji, j_ in enumerate(used_s5):
            stages = s5_setup(j_)
            last = (ji == len(used_s5) - 1) and len(used_s5) > 1
            per = (len(ada_list) + 7) // 8
            for st_i, st_fn in enumerate(stages):
                cjob(st_fn)
                if last and st_i >= 1:
                    emit_ada(per)
        if stop == "s5w":
            cjob(s5_dump)
        emit_ada(10 ** 6)
        for li, l in enumerate(layers):
            nxt = layers[li + 1] if li + 1 < len(layers) else None
            ada_post(l, nxt)
        for li, l in enumerate(layers):
            nxt = layers[li + 1] if li + 1 < len(layers) else None
            fuse_consts(l, 0, l, 4, 3)
            if nxt is not None:
                fuse_consts(l, 1, nxt, 1, 0)
        for s in range(NS if stop != "s5w" else 0):
            swap_x(s - 1 if s > 0 else None, s, layers[0] if layers else None)
            for li, l in enumerate(layers):
                nxt = layers[li + 1] if li + 1 < len(layers) else None
                if l % 2 == 0:
                    conv_mixer(l, s)
                else:
                    s5_mixer(l, s)
                if stop in ("mixer", "s5_yg"):
                    break
                layer_norm(l, 0, s, True)
                if stop == "ln1":
                    break
                if stop == "cst":
                    def dump():
                        n1 = DEPTH * 48 * NS
                        n2 = DEPTH * 2 * KC * NS
                        P.dma(lambda e: e.dma_start(out=dbg32[:, 0:n1], in_=modT[:].rearrange("p a b c -> p (a b c)")), reads=[("mod", l)], writes=["dbg"], chan="dbg")
                        P.dma(lambda e: e.dma_start(out=dbg32[:, n1:n1 + n2], in_=g2c[:].rearrange("p a b c d -> p (a b c d)")), reads=[("g2c", l)], writes=["dbg1"], chan="dbg1")
                        P.dma(lambda e: e.dma_start(out=dbg32[:, n1 + n2:n1 + 2 * n2], in_=b2c[:].rearrange("p a b c d -> p (a b c d)")), reads=[("g2c", l)], writes=["dbg2"], chan="dbg2")
                        P.dma(lambda e: e.dma_start(out=dbg32[:, n1 + 2 * n2:n1 + 2 * n2 + 128], in_=lnp[:].rearrange("p a b c -> p (a b c)")), reads=["lnp"], writes=["dbg3"], chan="dbg3")
                    cjob(dump)
                    break
                if stop == "ub":
                    def dump():
                        P.dma(lambda e: e.dma_start(out=dbg[:], in_=ub[:]), reads=UK, writes=["dbg"], chan="dbg")
                    cjob(dump)
                    break
                ffn(l, s)
                if stop in ("ffn", "ffn_g0", "ffn_dbg"):
                    break
                layer_norm(l, 1, s, nxt is not None, perm_u=(nxt is not None and nxt % 2 == 1))
        if stop != "s5w":
            swap_x(NS - 1, None, None)
        run_jobs()
        ykeys = [("yout", s, tb) for s in range(NS) for tb in range(16)] if stop != "s5w" else s5_keys(0) + ["dbg"]
        P.op("sp", lambda e: e.nop(), reads=ykeys + (["dbg"] if stop in ("ffn_g0", "ffn_dbg", "ub", "cst", "s5_yg") else []) + (["dbg1", "dbg2", "dbg3"] if stop == "cst" else []), writes=["done"])

        sems = {}
        for en in ("pe", "act", "dve", "pool", "sp"):
            sems[en] = es.enter_context(nc.semaphore("sem_" + en))
        for ci, ch in enumerate(P.channels()):
            sems[("chan", ch)] = es.enter_context(nc.semaphore("semc_%d" % ci))
        block = es.enter_context(nc.Block())
        P.finalize(block, sems)
    build_program.P = P
    return nc


def make_consts():
    c = np.zeros((128, 256), np.float32)
    c[:, 0:128] = np.eye(128, dtype=np.float32)
    c[0:64, 128] = 1.0
    c[64:128, 128] = -1.0
    c[:, 129] = 1.0
    c[:, 130] = EPS_P
    c[:, 131:131 + 32] = np.arange(32, dtype=np.float32)[None, :]
    return c


_WNAMES = ["ada_w", "ada_b", "ln1_g", "ln1_b", "ln2_g", "ln2_b", "sc_w_in", "sc_conv_w", "sc_conv_b", "sc_w_out",
           "s5_a_re", "s5_a_im", "s5_log_dt", "s5_b_re", "s5_b_im", "s5_c_re", "s5_c_im", "s5_d", "s5_w_glu",
           "ffn_w_up", "ffn_conv_w", "ffn_conv_b", "ffn_w_down"]


def run_cores(x_all, c_all, weights, NS, layers, ncores, stop=None):
    nc = build_program(NS, layers, stop)
    consts = make_consts()
    in_maps = []
    for ci in range(ncores):
        m = {"xin": np.ascontiguousarray(x_all[ci * NS:(ci + 1) * NS]),
             "cin": np.ascontiguousarray(c_all[ci * NS:(ci + 1) * NS]),
             "consts": consts}
        for n in _WNAMES:
            m[n] = weights[n]
        in_maps.append(m)
    res = run_bass_kernel_spmd(nc, in_maps, core_ids=list(range(ncores)))
    if stop:
        run_cores.dbg = res.results[0]["dbg"]
        run_cores.dbg32 = res.results[0]["dbg32"]
    if stop == "wup":
        run_cores.wup = res.results[0]["wb_up"]
    if stop == "s5w":
        run_cores.twb = res.results[0]["twb"]
        run_cores.twc = res.results[0]["twc"]
    return np.concatenate([r["yout"] for r in res.results], axis=0)


def kernel(x_prompt, x_sample, c_prompt, c_sample, **w):
    x_prompt = np.asarray(x_prompt, np.float32)
    x_sample = np.asarray(x_sample, np.float32)
    c_prompt = np.asarray(c_prompt, np.float32)
    c_sample = np.asarray(c_sample, np.float32)
    weights = {n: np.ascontiguousarray(np.asarray(w[n], np.float32)) for n in _WNAMES}
    xs, cs = [], []
    for i in range(NCORES):
        xs.append(x_prompt[4 * i:4 * i + 4]); xs.append(x_sample[i:i + 1])
        cs.append(c_prompt[4 * i:4 * i + 4]); cs.append(c_sample[i:i + 1])
    x_all = np.concatenate(xs, axis=0)
    c_all = np.concatenate(cs, axis=0)
    y = run_cores(x_all, c_all, weights, 5, [0, 1, 2, 3], NCORES)
    y = y.reshape(NCORES, 5, L, D)
    y_prompt = np.ascontiguousarray(y[:, 0:4].reshape(32, L, D))
    y_sample = np.ascontiguousarray(y[:, 4].reshape(8, L, D))
    return (y_prompt, y_sample)
```

```python
from contextlib import ExitStack
import numpy as np
import concourse.bass as bass
import concourse.mybir as mybir
from concourse.bass_utils import run_bass_kernel_spmd
from concourse.ap import AP

F32 = mybir.dt.float32
BF16 = mybir.dt.bfloat16
I32 = mybir.dt.int32
AF = mybir.ActivationFunctionType
ALU = mybir.AluOpType

D = 1024
KC = 8
L = 2048
TT = 512
NTT = 4
FH = 2816
FC = 22
DEPTH = 4
G = 64
PS = 64
NCORES = 8
ALPHA = (2 * DEPTH) ** 0.25
EPS_P = 1e-5 / (ALPHA * ALPHA)
NSLOT = 4
HGROUPS = [(0, 8), (8, 16), (16, 22)]


class Op:
    __slots__ = ("eng", "fn", "reads", "writes", "deps", "is_dma", "chan", "sig", "val", "idx", "ss")

    def __init__(self, eng, fn, reads, writes, is_dma=False, chan=None):
        self.eng = eng
        self.fn = fn
        self.reads = reads
        self.writes = writes
        self.deps = ()
        self.is_dma = is_dma
        self.chan = chan
        self.sig = False
        self.val = 0
        self.idx = -1
        self.ss = False


class Prog:
    ENGS = ("pe", "act", "dve", "pool", "sp")

    def __init__(self):
        self.ops = []
        self.last_w = {}
        self.readers = {}

    def op(self, eng, fn, reads=(), writes=(), ss=False):
        o = Op(eng, fn, tuple(reads), tuple(writes))
        o.ss = ss
        self._add(o)
        return o

    def dma(self, fn, reads=(), writes=(), chan=None, eng="sp"):
        o = Op(eng, fn, tuple(reads), tuple(writes), is_dma=True, chan=chan)
        self._add(o)
        return o

    def _add(self, o):
        o.idx = len(self.ops)
        deps = set()
        for k in o.reads:
            w = self.last_w.get(k)
            if w is not None:
                deps.add(w)
        for k in o.writes:
            w = self.last_w.get(k)
            if w is not None:
                deps.add(w)
            deps.update(self.readers.get(k, ()))
        o.deps = deps
        for k in o.reads:
            self.readers.setdefault(k, []).append(o.idx)
        for k in o.writes:
            self.last_w[k] = o.idx
            self.readers[k] = []
        self.ops.append(o)

    def channels(self):
        return sorted({o.chan for o in self.ops if o.is_dma}, key=str)

    def finalize(self, block, sems):
        ops = self.ops
        for o in ops:
            for d in o.deps:
                p = ops[d]
                if p.is_dma or p.eng != o.eng or o.is_dma or p.ss:
                    p.sig = True
        cnt = {e: 0 for e in self.ENGS}
        ccnt = {}
        for o in ops:
            if o.is_dma:
                ccnt[o.chan] = ccnt.get(o.chan, 0) + 1
                o.val = 16 * ccnt[o.chan]
                o.sig = True
            elif o.sig:
                cnt[o.eng] += 1
                o.val = cnt[o.eng]
        per_eng = {e: [] for e in self.ENGS}
        for o in ops:
            per_eng[o.eng].append(o)

        def emit_stream(engname, e):
            waited = {}
            for o in per_eng[engname]:
                need = {}
                for d in o.deps:
                    p = ops[d]
                    if p.is_dma:
                        key = ("chan", p.chan)
                    else:
                        if p.eng == engname and not o.is_dma and not p.ss:
                            continue
                        key = p.eng
                    if p.val > need.get(key, 0):
                        need[key] = p.val
                for key, v in need.items():
                    if waited.get(key, 0) >= v:
                        continue
                    waited[key] = v
                    e.wait_ge(sems[key], v)
                ins = o.fn(e)
                if o.sig:
                    if o.is_dma:
                        ins.then_inc(sems[("chan", o.chan)], 16)
                    else:
                        ins.then_inc(sems[engname], 1)

        @block.tensor
        def _(e):
            emit_stream("pe", e)

        @block.scalar
        def _(e):
            emit_stream("act", e)

        @block.vector
        def _(e):
            emit_stream("dve", e)

        @block.gpsimd
        def _(e):
            emit_stream("pool", e)

        @block.sync
        def _(e):
            emit_stream("sp", e)


def build_program(NS, layers, stop=None):
    nc = bass.Bass("TRN2", target_bir_lowering=False)

    def din(name, shape):
        return nc.dram_tensor(name, list(shape), F32, kind="ExternalInput").ap()

    xin = din("xin", (NS, L, D))
    cin = din("cin", (NS, D))
    consts = din("consts", (128, 256))
    ada_w = din("ada_w", (DEPTH, D, 6 * D))
    ada_b = din("ada_b", (DEPTH, 6 * D))
    ln_gb = [din("ln1_g", (DEPTH, D)), din("ln1_b", (DEPTH, D)), din("ln2_g", (DEPTH, D)), din("ln2_b", (DEPTH, D))]
    sc_w_in = din("sc_w_in", (2, D, 3 * D))
    sc_conv_w = din("sc_conv_w", (2, 3, D))
    sc_conv_b = din("sc_conv_b", (2, D))
    sc_w_out = din("sc_w_out", (2, D, D))
    s5_a_re = din("s5_a_re", (2, 2, G, PS))
    s5_a_im = din("s5_a_im", (2, 2, G, PS))
    s5_log_dt = din("s5_log_dt", (2, 2, G))
    s5_b_re = din("s5_b_re", (2, 2, G, PS, 16))
    s5_b_im = din("s5_b_im", (2, 2, G, PS, 16))
    s5_c_re = din("s5_c_re", (2, 2, G, 16, PS))
    s5_c_im = din("s5_c_im", (2, 2, G, 16, PS))
    s5_d = din("s5_d", (2, D))
    s5_w_glu = din("s5_w_glu", (2, D, 2 * D))
    ffn_w_up = din("ffn_w_up", (DEPTH, D, 2 * FH))
    ffn_conv_w = din("ffn_conv_w", (DEPTH, 3, 2 * FH))
    ffn_conv_b = din("ffn_conv_b", (DEPTH, 2 * FH))
    ffn_w_down = din("ffn_w_down", (DEPTH, FH, D))
    yout = nc.dram_tensor("yout", [NS, L, D], F32, kind="ExternalOutput").ap()
    dbg = nc.dram_tensor("dbg", [128, 8, L], BF16, kind="ExternalOutput").ap() if stop else None
    dbg32 = nc.dram_tensor("dbg32", [128, 16384], F32, kind="ExternalOutput").ap() if stop else None

    def dscr(name, shape, dt=BF16):
        return nc.dram_tensor(name, list(shape), dt).ap()

    wb_in = dscr("wb_in", (2, D, 3 * D))
    wb_out = dscr("wb_out", (2, D, D))
    wb_glu = dscr("wb_glu", (2, D, 2 * D))
    wb_up = (nc.dram_tensor("wb_up", [DEPTH, D, 2 * FH], BF16, kind="ExternalOutput").ap() if stop == "wup"
             else dscr("wb_up", (DEPTH, D, 2 * FH)))
    wb_dn = dscr("wb_dn", (DEPTH, FH, D))

    if stop == "s5w":
        twb = nc.dram_tensor("twb", [2, G, 128, 4, 128], BF16, kind="ExternalOutput").ap()
        twc = nc.dram_tensor("twc", [2, G, 128, 7, 128], BF16, kind="ExternalOutput").ap()
    else:
        twb = dscr("twb", (2, G, 128, 4, 128))
        twc = dscr("twc", (2, G, 128, 7, 128))
    d_in = dscr("d_in", (8, D, 256))
    d_out = dscr("d_out", (8, D, 256))

    P = Prog()
    jobs = []

    with ExitStack() as es:
        def sb(name, shape, dt):
            return es.enter_context(nc.sbuf_tensor(name, list(shape), dt))

        x = sb("x", (128, KC, L), F32)
        CU = sb("CU", (128, 16384), F32)
        bufC = CU[:, 0:8192].rearrange("p (a b) -> p a b", a=8)
        ub = CU[:, 8192:16384].bitcast(BF16).rearrange("p (a b) -> p a b", a=KC)
        bufB = sb("bufB", (128, 8, L), BF16)
        scT = sb("scT", (128, 256), F32)
        ring = sb("ring", (128, NSLOT, 2048), F32)
        cst = sb("cst", (128, 256), F32)
        ones_b = sb("ones_b", (128, 128), BF16)
        modT = sb("modT", (128, DEPTH, 48, NS), F32)
        g2c = sb("g2c", (128, DEPTH, 2, KC, NS), F32)
        b2c = sb("b2c", (128, DEPTH, 2, KC, NS), F32)
        lnp = sb("lnp", (128, 4, DEPTH, KC), F32)
        adab = sb("adab", (128, DEPTH, 48), F32)
        cT = sb("cT", (128, NS, KC), F32)
        sccw = sb("sccw", (128, 2, 3, KC), F32)
        sccb = sb("sccb", (128, 2, KC), F32)
        fcw = sb("fcw", (128, DEPTH, 3, 2 * FC), F32)
        fcb = sb("fcb", (128, DEPTH, 2 * FC), F32)
        identb = sb("identb", (128, 128), BF16)
        scA = sb("scA", (128, 2, 2, 2, G), F32)
        dcol = sb("dcol", (16, 2, G), F32)
        pa = es.enter_context(nc.psum_tensor("pa", [128, 2048], F32))
        pb = es.enter_context(nc.psum_tensor("pb", [128, 2048], F32))
        ident = cst[:, 0:128]

        PSA = [("ps", i) for i in range(4)]
        PSB = [("ps", 4 + i) for i in range(4)]
        XK = [("x", k) for k in range(KC)]
        UK = [("ub", k) for k in range(KC)]
        BK = [("B", k) for k in range(8)]
        CK = [("C", k) for k in range(8)]

        ring_n = [0]

        def wjob(src, view_fn, run, rkeys=()):
            jobs.append(((src, view_fn, tuple(rkeys)), run))

        def cjob(run):
            jobs.append((None, run))

        def run_jobs():
            loads = [i for i, j in enumerate(jobs) if j[0] is not None]
            tiles = {}
            li = [0]

            def issue(upto):
                while li[0] < len(loads) and li[0] < upto:
                    ji = loads[li[0]]
                    s = ring_n[0] % NSLOT
                    ring_n[0] += 1
                    src, view_fn, rkeys = jobs[ji][0]
                    tv = view_fn(ring[:, s, :])
                    if not isinstance(src, list):
                        src = [(src, lambda t: t)]
                    wkeys = []
                    for si, (sap, sel) in enumerate(src):
                        dst = sel(tv)
                        P.dma(lambda e, dst=dst, sap=sap: e.dma_start(out=dst, in_=sap),
                              reads=rkeys, writes=[("w", s, si)], chan=("w", s))
                        wkeys.append(("w", s, si))
                    for si in range(len(src), 3):
                        wkeys.append(("w", s, si))
                    tiles[ji] = (tv, wkeys)
                    li[0] += 1

            nl = 0
            for ji, (ld, run) in enumerate(jobs):
                if ld is not None:
                    issue(nl + NSLOT - 1)
                    issue(nl + 1)
                    nl += 1
                    tv, key = tiles.pop(ji)
                    run(tv, key)
                else:
                    run()

        def bfview(slot, shape):
            v = slot.bitcast(BF16)
            n = int(np.prod(shape))
            v = v[:, 0:n]
            if len(shape) == 2:
                return v.rearrange("p (a b) -> p a b", a=shape[0])
            if len(shape) == 3:
                return v.rearrange("p (a b c) -> p a b c", a=shape[0], b=shape[1])
            if len(shape) == 4:
                return v.rearrange("p (a b c d) -> p a b c d", a=shape[0], b=shape[1], c=shape[2])
            return v

        def mm(out, lhsT, rhs, start, stop, reads, writes):
            P.op("pe", lambda e: e.matmul(out, lhsT=lhsT, rhs=rhs, start=start, stop=stop),
                 reads=reads, writes=writes)

        P.dma(lambda e: e.dma_start(out=cst[:], in_=consts[:]), writes=["cst"], chan="cst")
        P.op("dve", lambda e: e.memset(ones_b[:], 1.0), writes=["ones"])

        def load_rows_T(src2d, nrows, dst_fn, tag):
            r0 = 0
            i = 0
            while r0 < nrows:
                r = min(128, nrows - r0)
                blk = i % 2
                st = bufC[:, blk, 0:128]
                pst = pa[:, blk * 512: blk * 512 + 128]
                P.dma(lambda e, st=st, r=r, r0=r0: e.dma_start(out=st[0:r, :], in_=src2d[r0:r0 + r, :]),
                      writes=[CK[blk]], chan=("ldT", blk))
                P.op("pe", lambda e, st=st, r=r, pst=pst: e.transpose(out=pst[:, 0:r], in_=st[0:r, :], identity=ident[0:r, 0:r]),
                     reads=[CK[blk], "cst"], writes=[PSA[blk]])
                dst = dst_fn(r0, r)
                P.op("dve", lambda e, dst=dst, r=r, pst=pst: e.tensor_copy(out=dst, in_=pst[:, 0:r]),
                     reads=[PSA[blk]], writes=[tag], ss=True)
                r0 += r
                i += 1

        for i4 in range(4):
            src = ln_gb[i4].rearrange("l (k p) -> (l k) p", p=128)
            load_rows_T(src, DEPTH * KC, lambda r0, r, i4=i4: lnp[:, i4, :, :].rearrange("p l k -> p (l k)")[:, r0:r0 + r], "lnp")
        load_rows_T(ada_b.rearrange("l (o p) -> (l o) p", p=128), DEPTH * 48,
                    lambda r0, r: adab[:].rearrange("p l o -> p (l o)")[:, r0:r0 + r], "adab")
        load_rows_T(cin.rearrange("s (k p) -> (s k) p", p=128), NS * KC,
                    lambda r0, r: cT[:].rearrange("p s k -> p (s k)")[:, r0:r0 + r], "cT")
        load_rows_T(sc_conv_w.rearrange("j t (k p) -> (j t k) p", p=128), 2 * 3 * KC,
                    lambda r0, r: sccw[:].rearrange("p j t k -> p (j t k)")[:, r0:r0 + r], "sccw")
        load_rows_T(sc_conv_b.rearrange("j (k p) -> (j k) p", p=128), 2 * KC,
                    lambda r0, r: sccb[:].rearrange("p j k -> p (j k)")[:, r0:r0 + r], "sccb")
        load_rows_T(ffn_conv_w.rearrange("l t (m p) -> (l t m) p", p=128), DEPTH * 3 * 2 * FC,
                    lambda r0, r: fcw[:].rearrange("p l t m -> p (l t m)")[:, r0:r0 + r], "fcw")
        load_rows_T(ffn_conv_b.rearrange("l (m p) -> (l m) p", p=128), DEPTH * 2 * FC,
                    lambda r0, r: fcb[:].rearrange("p l m -> p (l m)")[:, r0:r0 + r], "fcb")
        P.op("act", lambda e: e.activation(out=cT[:], in_=cT[:], func=AF.Silu), reads=["cT"], writes=["cT"], ss=True)

        conv_ctr = [0]

        scr_blocks = {}

        conv_blocks = []

        def convert(src2d, dst2d, K, M, name):
            scr_blocks[name] = []
            for r0 in range(0, K, 128):
                for c0 in range(0, M, 1024):
                    cw = min(1024, M - c0)
                    conv_blocks.append((src2d, dst2d, r0, c0, cw, name))
                    scr_blocks[name].append(("wscr", name, r0, c0))

        def pump(n):
            blocks = list(conv_blocks)
            del conv_blocks[:]
            nb = len(blocks)
            info = {}

            def rec_load(bi):
                src2d, dst2d, r0, c0, cw, name = blocks[bi]
                i = bi % NSLOT
                st32 = ring[:, i, 0:cw]
                st16 = ring[:, i, 1024:1536].bitcast(BF16)[:, 0:cw]
                k32 = [("w", i, 0), ("w", i, 1)]
                k16 = [("w", i, 2)]
                info[bi] = (st32, st16, k32, k16)
                P.dma(lambda e: e.dma_start(out=st32, in_=src2d[r0:r0 + 128, c0:c0 + cw]), writes=k32, chan=("cvi", i))

            def rec_rest(bi):
                src2d, dst2d, r0, c0, cw, name = blocks[bi]
                i = bi % NSLOT
                st32, st16, k32, k16 = info.pop(bi)
                P.op("pool", lambda e: e.tensor_copy(out=st16, in_=st32), reads=k32, writes=k16)
                P.dma(lambda e: e.dma_start(out=dst2d[r0:r0 + 128, c0:c0 + cw], in_=st16),
                      reads=k16, writes=[("wscr", name, r0, c0)], chan=("cvo", i))
            AHEAD = 2
            for bi in range(min(AHEAD, nb)):
                rec_load(bi)
            for bi in range(nb):
                if bi + AHEAD < nb:
                    rec_load(bi + AHEAD)
                rec_rest(bi)

        used_conv = sorted({l // 2 for l in layers if l % 2 == 0})
        used_s5 = sorted({l // 2 for l in layers if l % 2 == 1})
        wkeys = {}
        for l in layers:
            j = l // 2
            if l % 2 == 0:
                convert(sc_w_in[j], wb_in[j], D, 3 * D, ('in', j))
                convert(sc_w_out[j], wb_out[j], D, D, ('out', j))
            else:
                convert(s5_w_glu[j], wb_glu[j], D, 2 * D, ('glu', j))
            convert(ffn_w_up[l], wb_up[l], D, 2 * FH, ('up', l))
            convert(ffn_w_down[l], wb_dn[l], FH, D, ('dn', l))

        ada_list = []

        def emit_ada(n):
            for _ in range(n):
                if not ada_list:
                    return
                a_src, a_view, a_run = ada_list.pop(0)
                wjob(a_src, a_view, a_run)

        def ada_jobs(l):
            for cb in range(24):
                src = ada_w[l].rearrange("(k p) o -> p k o", p=128)[:, :, cb * 256:(cb + 1) * 256]

                def run(tv, key, cb=cb, l=l):
                    for oo in range(2):
                        oc = cb * 2 + oo
                        pst = pb[:, (oc % 4) * 512:(oc % 4) * 512 + NS]
                        for kc in range(KC):
                            mm(pst, tv[:, kc, oo * 128:(oo + 1) * 128], cT[:, :, kc], kc == 0, kc == KC - 1,
                               key + ["cT"], [PSB[oc % 4]])
                        P.op("dve", lambda e, pst=pst, oc=oc, l=l: e.tensor_scalar(
                            out=modT[:, l, oc, :], in0=pst, scalar1=adab[:, l, oc:oc + 1], scalar2=None, op0=ALU.add),
                            reads=[PSB[oc % 4], "adab"], writes=[("mod", l)], ss=True)
                ada_list.append((src, (lambda slot: slot[:, 0:2048].rearrange("p (k o) -> p k o", k=KC)), run))

        def ada_post(l, nxt):
            def run(l=l, nxt=nxt):
                for w in (1, 4):
                    P.op("dve", lambda e, w=w: e.tensor_scalar(out=modT[:, l, w * 8:(w + 1) * 8, :], in0=modT[:, l, w * 8:(w + 1) * 8, :],
                                                               scalar1=1.0, scalar2=None, op0=ALU.add),
                         reads=[("mod", l)], writes=[("mod", l)], ss=True)
                for w in (2, 5):
                    P.op("dve", lambda e, w=w: e.tensor_scalar(out=modT[:, l, w * 8:(w + 1) * 8, :], in0=modT[:, l, w * 8:(w + 1) * 8, :],
                                                               scalar1=1.0, scalar2=1.0 / ALPHA, op0=ALU.add, op1=ALU.mult),
                         reads=[("mod", l)], writes=[("mod", l)], ss=True)
            cjob(run)

        def fuse_consts(l, which, ml, msc, msh):
            def run():
                gi = 0 if which == 0 else 2
                for s in range(NS):
                    P.op("dve", lambda e, s=s: e.tensor_tensor(out=g2c[:, l, which, :, s], in0=lnp[:, gi, l, :],
                                                                in1=modT[:, ml, msc * 8:(msc + 1) * 8, s], op=ALU.mult),
                         reads=["lnp", ("mod", ml)], writes=[("g2c", l)], ss=True)
                    P.op("dve", lambda e, s=s: e.tensor_tensor(out=b2c[:, l, which, :, s], in0=lnp[:, gi + 1, l, :],
                                                                in1=modT[:, ml, msc * 8:(msc + 1) * 8, s], op=ALU.mult),
                         reads=["lnp", ("mod", ml)], writes=[("g2c", l)], ss=True)
                    P.op("dve", lambda e, s=s: e.tensor_tensor(out=b2c[:, l, which, :, s], in0=b2c[:, l, which, :, s],
                                                                in1=modT[:, ml, msh * 8:(msh + 1) * 8, s], op=ALU.add),
                         reads=[("g2c", l), ("mod", ml)], writes=[("g2c", l)], ss=True)
            cjob(run)

        def swap_x(s_out, s_in, l0):
            def run():
                allk = []
                for tb in range(16):
                    blk = tb % 2
                    tsl = slice(tb * 128, (tb + 1) * 128)
                    XT = ("xt", tb)
                    allk.append(XT)
                    pt = pa if blk == 0 else pb
                    pk = PSA if blk == 0 else PSB
                    if s_in is not None:
                        lst = bufC[:, 2 * blk:2 * blk + 2, :].rearrange("p a b -> p (a b)")[:, 0:D]
                        lkeys = [CK[2 * blk], CK[2 * blk + 1]]
                        P.dma(lambda e, lst=lst, tb=tb: e.dma_start(out=lst, in_=xin[s_in, tb * 128:(tb + 1) * 128, :]),
                              writes=lkeys, chan=("xs", blk))
                    if s_out is not None:
                        sst = bufC[:, 4 + 2 * blk:6 + 2 * blk, :].rearrange("p a b -> p (a b)")[:, 0:D]
                        skeys = [CK[4 + 2 * blk], CK[5 + 2 * blk]]
                        for kc in range(KC):
                            P.op("pe", lambda e, kc=kc, pt=pt, tsl=tsl: e.transpose(out=pt[:, kc * 128:(kc + 1) * 128], in_=x[:, kc, tsl], identity=ident),
                                 reads=[XK[kc], XT, "cst"], writes=[pk[kc // 4]])
                        if blk == 0:
                            P.op("act", lambda e, sst=sst, pt=pt: e.activation(out=sst, in_=pt[:, 0:1024], func=AF.Copy),
                                 reads=[pk[0], pk[1]], writes=skeys)
                        else:
                            P.op("dve", lambda e, sst=sst, pt=pt: e.tensor_copy(out=sst, in_=pt[:, 0:1024]),
                                 reads=[pk[0], pk[1]], writes=skeys)
                        P.dma(lambda e, sst=sst, tb=tb: e.dma_start(out=yout[s_out, tb * 128:(tb + 1) * 128, :], in_=sst),
                              reads=skeys, writes=[("yout", s_out, tb)], chan=("ys", blk))
                    if s_in is not None:
                        for kc in range(KC):
                            P.op("pe", lambda e, lst=lst, kc=kc, pt=pt: e.transpose(out=pt[:, 1024 + kc * 128:1024 + (kc + 1) * 128], in_=lst[:, kc * 128:(kc + 1) * 128], identity=ident),
                                 reads=lkeys + ["cst"], writes=[pk[2 + kc // 4]])
                        if blk == 1:
                            P.op("act", lambda e, tsl=tsl, pt=pt: e.activation(out=x[:, :, tsl], in_=pt[:, 1024:2048].rearrange("p (k t) -> p k t", k=KC), func=AF.Copy),
                                 reads=[pk[2], pk[3]], writes=[XT])
                        else:
                            P.op("dve", lambda e, tsl=tsl, pt=pt: e.tensor_copy(out=x[:, :, tsl], in_=pt[:, 1024:2048].rearrange("p (k t) -> p k t", k=KC)),
                                 reads=[pk[2], pk[3]], writes=[XT])
                if s_in is None:
                    return
                P.op("dve", lambda e: e.memset(scT[:, 0:1], 0.0), reads=allk + ["scT"], writes=XK + ["scT"])
                for kc in range(KC if l0 is not None else 0):
                    if l0 % 2 == 1:
                        uo_ = ub[:, kc, :].rearrange("p (t c) -> p t c", t=8)
                        ui_ = x[:, kc, :].rearrange("p (c t) -> p t c", t=8)
                    else:
                        uo_ = ub[:, kc, :]
                        ui_ = x[:, kc, :]
                    P.op("act", lambda e, kc=kc, uo_=uo_, ui_=ui_: e.activation(out=uo_, in_=ui_, func=AF.Identity,
                                                              scale=modT[:, l0, 8 + kc, s_in:s_in + 1], bias=modT[:, l0, kc, s_in:s_in + 1]),
                         reads=[XK[kc], ("mod", l0)], writes=[UK[kc]])
            cjob(run)

        def conv_mixer(l, s):
            j = l // 2
            mid = bufB
            cgs = bufC[:, 0:2, :].rearrange("p a b -> p (a b)")
            chp = bufC[:, 2:5, :].rearrange("p a b -> p (a b)")
            tA = bufC[:, 5:7, :].rearrange("p a b -> p (a b)")
            CG = [CK[0], CK[1]]
            CH = [CK[2], CK[3], CK[4]]
            TA = [CK[5], CK[6]]

            def pre():
                P.op("dve", lambda e: e.memset(chp[:, 0:1], 0.0), writes=CH)
                P.op("dve", lambda e: e.memset(chp[:, 2049:2050], 0.0), writes=CH)
            cjob(pre)
            for f in range(KC):
                w5 = wb_in[j].rearrange("(k p) (c f o) -> p k c f o", p=128, c=3, f=KC)
                src = [(w5[:, :, c, f, :], (lambda t, c=c: t[:, :, c, :])) for c in range(3)]

                def run(tv, key, f=f):
                    for part, pt, pk in ((1, pa, PSA), (2, pb, PSB)):
                        for tt in range(NTT):
                            for kc in range(KC):
                                mm(pt[:, tt * TT:(tt + 1) * TT], tv[:, kc, part, :], ub[:, kc, tt * TT:(tt + 1) * TT],
                                   kc == 0, kc == KC - 1, key + [UK[kc]], [pk[tt]])
                        if part == 1:
                            P.op("act", lambda e: e.activation(out=cgs, in_=pa[:], func=AF.Copy), reads=PSA, writes=CG)
                        else:
                            P.op("dve", lambda e: e.tensor_tensor(out=chp[:, 1:2049], in0=pb[:], in1=cgs, op=ALU.mult),
                                 reads=PSB + CG, writes=CH)
                    for tt in range(NTT):
                        for kc in range(KC):
                            mm(pa[:, tt * TT:(tt + 1) * TT], tv[:, kc, 0, :], ub[:, kc, tt * TT:(tt + 1) * TT],
                               kc == 0, kc == KC - 1, key + [UK[kc]], [PSA[tt]])
                    P.op("act", lambda e: e.activation(out=tA, in_=chp[:, 1:2049], func=AF.Identity,
                                                       scale=sccw[:, j, 1, f:f + 1], bias=sccb[:, j, f:f + 1]),
                         reads=CH + ["sccw", "sccb"], writes=TA)
                    P.op("dve", lambda e: e.scalar_tensor_tensor(out=tA, in0=chp[:, 0:2048], scalar=sccw[:, j, 0, f:f + 1], in1=tA,
                                                                 op0=ALU.mult, op1=ALU.add), reads=CH + TA + ["sccw"], writes=TA)
                    P.op("dve", lambda e: e.scalar_tensor_tensor(out=tA, in0=chp[:, 2:2050], scalar=sccw[:, j, 2, f:f + 1], in1=tA,
                                                                 op0=ALU.mult, op1=ALU.add), reads=CH + TA + ["sccw"], writes=TA)
                    P.op("dve", lambda e: e.tensor_tensor(out=mid[:, f, :], in0=pa[:], in1=tA, op=ALU.mult),
                         reads=PSA + TA, writes=[BK[f]])
                wjob(src, lambda slot: bfview(slot, (KC, 3, 128)), run, scr_blocks[('in', j)])
            for q in range(2):
                src = wb_out[j].rearrange("(k p) o -> p k o", p=128)[:, :, q * 512:(q + 1) * 512]

                def run(tv, key, q=q):
                    for oo in range(4):
                        o = q * 4 + oo
                        pt, pk = (pa, PSA) if o % 2 == 0 else (pb, PSB)
                        for tt in range(NTT):
                            for kc in range(KC):
                                mm(pt[:, tt * TT:(tt + 1) * TT], tv[:, kc, oo * 128:(oo + 1) * 128], mid[:, kc, tt * TT:(tt + 1) * TT],
                                   kc == 0, kc == KC - 1, key + [BK[kc]], [pk[tt]])
                        P.op("dve", lambda e, o=o, pt=pt: e.scalar_tensor_tensor(out=x[:, o, :], in0=pt[:], scalar=modT[:, l, 16 + o, s:s + 1],
                                                                                 in1=x[:, o, :], op0=ALU.mult, op1=ALU.add),
                             reads=pk + [XK[o], ("mod", l)], writes=[XK[o]])
                wjob(src, lambda slot: bfview(slot, (KC, 512)), run, scr_blocks[('out', j)])

        def layer_norm(l, which, s, want_u, perm_u=False):
            gi = 0 if which == 0 else 2

            def run():
                xb = bufC[:, 0:2, :].rearrange("p a b -> p (a b)").bitcast(BF16).rearrange("p (k t) -> p k t", k=KC)
                sq = bufC[:, 2:4, :].rearrange("p a b -> p (a b)").bitcast(BF16).rearrange("p (k t) -> p k t", k=KC)
                XBK = [CK[0], CK[1]]
                SQK = [CK[2], CK[3]]
                allk = []

                def tile_ctx(tt):
                    par = tt % 2
                    sl = slice(tt * TT, (tt + 1) * TT)
                    LX = [("lnx", tt, kc) for kc in range(KC)]
                    ps1 = pa[:, (2 * par) * TT:(2 * par + 1) * TT]
                    ps2 = pa[:, (2 * par + 1) * TT:(2 * par + 2) * TT]
                    PK1, PK2 = PSA[2 * par], PSA[2 * par + 1]
                    c0, c1 = CK[4 + 2 * par], CK[5 + 2 * par]
                    stat = bufC[:, 4 + 2 * par:6 + 2 * par, :].rearrange("p a b -> p (a b)").rearrange("p (a t) -> p a t", a=4)
                    return par, sl, LX, ps1, ps2, PK1, PK2, c0, c1, stat

                def stage_a(tt):
                    par, sl, LX, ps1, ps2, PK1, PK2, c0, c1, stat = tile_ctx(tt)
                    allk.extend(LX)
                    P.op("act", lambda e: e.activation(out=xb, in_=x[:, :, sl], func=AF.Copy), reads=XK, writes=XBK)
                    P.op("act", lambda e: e.activation(out=sq, in_=x[:, :, sl], func=AF.Square), reads=XK, writes=SQK)
                    for kc in range(KC):
                        mm(ps1, ones_b[:], xb[:, kc, :], kc == 0, kc == KC - 1, XBK + ["ones"], [PK1])
                    for kc in range(KC):
                        mm(ps2, ones_b[:], sq[:, kc, :], kc == 0, kc == KC - 1, SQK + ["ones"], [PK2])

                def stage_b(tt):
                    par, sl, LX, ps1, ps2, PK1, PK2, c0, c1, stat = tile_ctx(tt)
                    mean = stat[:, 0, :]
                    msq = stat[:, 1, :]
                    var = stat[:, 2, :]
                    rstd = stat[:, 3, :]
                    s0, s1, s2, s3 = [("st", par, i) for i in range(4)]
                    P.op("dve", lambda e: e.tensor_scalar(out=mean, in0=ps1, scalar1=1.0 / D, scalar2=None, op0=ALU.mult),
                         reads=[PK1], writes=[c0, s0], ss=True)
                    P.op("dve", lambda e: e.tensor_tensor(out=msq, in0=mean, in1=mean, op=ALU.mult), reads=[s0, c0], writes=[c0, s1], ss=True)
                    P.op("dve", lambda e: e.scalar_tensor_tensor(out=var, in0=ps2, scalar=1.0 / D, in1=msq,
                                                                 op0=ALU.mult, op1=ALU.subtract), reads=[PK2, s1, c0], writes=[c1, s2], ss=True)
                    P.op("act", lambda e: e.activation(out=var, in_=var, func=AF.Sqrt, bias=cst[:, 130:131], scale=1.0),
                         reads=[s2, "cst", c1], writes=[c1, s2], ss=True)
                    P.op("dve", lambda e: e.reciprocal(out=rstd, in_=var), reads=[s2, c1], writes=[c1, s3], ss=True)
                    mb = mean.unsqueeze(1).to_broadcast([128, KC, TT])
                    rb = rstd.unsqueeze(1).to_broadcast([128, KC, TT])
                    P.op("dve", lambda e: e.tensor_tensor(out=x[:, :, sl], in0=x[:, :, sl], in1=mb, op=ALU.subtract),
                         reads=XK + [s0, c0], writes=LX)
                    P.op("dve", lambda e: e.tensor_tensor(out=x[:, :, sl], in0=x[:, :, sl], in1=rb, op=ALU.mult),
                         reads=LX + [s3, c1], writes=LX)

                def stage_c(tt):
                    par, sl, LX, ps1, ps2, PK1, PK2, c0, c1, stat = tile_ctx(tt)
                    for kc in range(KC):
                        if want_u:
                            if perm_u:
                                uo = ub[:, kc, :].rearrange("p (t c) -> p t c", t=8)[:, :, tt * 64:(tt + 1) * 64]
                                ui = x[:, kc, sl].rearrange("p (c t) -> p t c", t=8)
                            else:
                                uo = ub[:, kc, sl]
                                ui = x[:, kc, sl]
                            P.op("dve", lambda e, uo=uo, ui=ui, kc=kc: e.tensor_scalar(out=uo, in0=ui, scalar1=g2c[:, l, which, kc, s:s + 1],
                                                                                       scalar2=b2c[:, l, which, kc, s:s + 1], op0=ALU.mult, op1=ALU.add),
                                 reads=[LX[kc], ("g2c", l)], writes=[UK[kc]])
                        P.op("act", lambda e, kc=kc: e.activation(out=x[:, kc, sl], in_=x[:, kc, sl], func=AF.Identity,
                                                                 scale=lnp[:, gi, l, kc:kc + 1], bias=lnp[:, gi + 1, l, kc:kc + 1]),
                             reads=[LX[kc], "lnp"], writes=[LX[kc]])

                stage_a(0)
                for tt in range(NTT):
                    if tt + 1 < NTT:
                        stage_a(tt + 1)
                    stage_b(tt)
                    stage_c(tt)
                P.op("dve", lambda e: e.memset(scT[:, 0:1], 0.0), reads=allk + ["scT"], writes=XK + ["scT"])
            cjob(run)

        def ffn(l, s):
            g = bufB
            tA = bufC[:, 0:3, :].rearrange("p a b -> p (a b)")
            tV = bufC[:, 3:6, :].rearrange("p a b -> p (a b)")
            ga = bufC[:, 6, :].bitcast(BF16)
            TA = [CK[0], CK[1], CK[2]]
            TV = [CK[3], CK[4], CK[5]]
            GA = [CK[6]]
            for (m0, m1) in HGROUPS:
                nk = m1 - m0
                for mp in range(m0, m1, 2):
                    w5 = wb_up[l].rearrange("(k p) (c m o) -> p k c m o", p=128, c=2, m=FC)
                    src = [(w5[:, :, c, mp:mp + 2, :].rearrange("p k m o -> p k (m o)"),
                            (lambda t, c=c: t[:, :, c, :, :].rearrange("p k m o -> p k (m o)"))) for c in range(2)]

                    def run(tv, key, mp=mp, m0=m0):
                        for mi in range(2):
                            m = mp + mi
                            for c, pt, pk in ((0, pa, PSA), (1, pb, PSB)):
                                for tt in range(NTT):
                                    for kc in range(KC):
                                        mm(pt[:, tt * TT:(tt + 1) * TT], tv[:, kc, c, mi, :], ub[:, kc, tt * TT:(tt + 1) * TT],
                                           kc == 0, kc == KC - 1, key + [UK[kc]], [pk[tt]])
                                ch = c * FC + m
                                tt_, tk = (tA, TA) if c == 0 else (tV, TV)
                                P.op("act", lambda e, pt=pt, tt_=tt_, ch=ch: e.activation(out=tt_[:, 1:L + 1], in_=pt[:], func=AF.Identity,
                                                                                     scale=fcw[:, l, 1, ch:ch + 1], bias=fcb[:, l, ch:ch + 1]),
                                     reads=pk + ["fcw", "fcb"], writes=tk)
                                P.op("dve", lambda e, pt=pt, tt_=tt_, ch=ch: e.scalar_tensor_tensor(out=tt_[:, 2:L + 2], in0=pt[:], scalar=fcw[:, l, 0, ch:ch + 1],
                                                                                               in1=tt_[:, 2:L + 2], op0=ALU.mult, op1=ALU.add),
                                     reads=pk + tk + ["fcw"], writes=tk)
                                P.op("dve", lambda e, pt=pt, tt_=tt_, ch=ch: e.scalar_tensor_tensor(out=tt_[:, 0:L], in0=pt[:], scalar=fcw[:, l, 2, ch:ch + 1],
                                                                                               in1=tt_[:, 0:L], op0=ALU.mult, op1=ALU.add),
                                     reads=pk + tk + ["fcw"], writes=tk)
                                if stop == "ffn_dbg" and c == 0:
                                    b32 = bufB[:].rearrange("p a b -> p (a b)").bitcast(F32).rearrange("p (a b) -> p a b", a=4)
                                    P.op("act", lambda e: e.activation(out=b32[:, 0, :], in_=pa[:], func=AF.Copy), reads=PSA + TA, writes=BK)
                                    P.op("act", lambda e: e.activation(out=b32[:, 1, :], in_=tA[:, 1:L + 1], func=AF.Copy), reads=TA, writes=BK)
                                    P.op("act", lambda e: e.activation(out=ga, in_=tA[:, 1:L + 1], func=AF.Gelu_apprx_tanh), reads=TA, writes=GA)
                                    P.op("act", lambda e: e.activation(out=b32[:, 2, :], in_=ga, func=AF.Copy), reads=GA, writes=BK)
                                    P.dma(lambda e: e.dma_start(out=dbg[:], in_=bufB[:]), reads=BK, writes=["dbg"], chan="dbg")
                                    return
                                if c == 0:
                                    P.op("act", lambda e: e.activation(out=ga, in_=tA[:, 1:L + 1], func=AF.Gelu_apprx_tanh), reads=TA, writes=GA)
                                else:
                                    P.op("dve", lambda e, m=m, m0=m0: e.tensor_tensor(out=g[:, m - m0, :], in0=tV[:, 1:L + 1], in1=ga, op=ALU.mult),
                                         reads=TV + GA, writes=[BK[m - m0]])
                    wjob(src, lambda slot: bfview(slot, (KC, 2, 2, 128)), run, scr_blocks[('up', l)])
                if stop == "ffn_dbg":
                    return
                if stop == "ffn_g0":
                    def dump():
                        P.dma(lambda e: e.dma_start(out=dbg[:], in_=bufB[:]), reads=BK, writes=["dbg"], chan="dbg")
                    cjob(dump)
                    return
                for q in range(4):
                    src = wb_dn[l].rearrange("(k p) o -> p k o", p=128)[:, m0:m1, q * 256:(q + 1) * 256]

                    def run(tv, key, q=q, nk=nk):
                        for oo in range(2):
                            o = q * 2 + oo
                            pt, pk = (pa, PSA) if o % 2 == 0 else (pb, PSB)
                            for tt in range(NTT):
                                for kc in range(nk):
                                    mm(pt[:, tt * TT:(tt + 1) * TT], tv[:, kc, oo * 128:(oo + 1) * 128], g[:, kc, tt * TT:(tt + 1) * TT],
                                       kc == 0, kc == nk - 1, key + [BK[kc]], [pk[tt]])
                            P.op("dve", lambda e, o=o, pt=pt: e.scalar_tensor_tensor(out=x[:, o, :], in0=pt[:], scalar=modT[:, l, 40 + o, s:s + 1],
                                                                                     in1=x[:, o, :], op0=ALU.mult, op1=ALU.add),
                                 reads=pk + [XK[o], ("mod", l)], writes=[XK[o]])
                    wjob(src, lambda slot, nk=nk: bfview(slot, (nk, 256)), run, scr_blocks[('dn', l)])

        xs = x[:].rearrange("p a b -> p (a b)")
        TWO_PI = 6.283185307179586

        def s5_setup(j):
            KT = ["s5tab", XK[0], XK[1]]
            A = xs[:, 0:4096]

            def slot(i, n=272):
                return A[:, i * 272:i * 272 + n]

            def sm(i):
                return A[:, 3840 + 16 * i:3840 + 16 * i + 16]

            def t3(ap):
                return ap.rearrange("p (k g) -> p k g", g=16)

            def D_(fn, reads=(), writes=(), eng="dve"):
                P.op(eng, fn, reads=list(reads) + KT, writes=list(writes) + KT, ss=True)

            kv = cst[:, 131:148]
            kvb = kv.unsqueeze(2).to_broadcast([128, 17, 16])
            sgn = cst[:, 128:129]
            tmp1 = xs[:, 4096:8192]
            tmp2 = xs[:, 8192:12288]
            T1K = [XK[2], XK[3]]
            T2K = [XK[4], XK[5]]
            dbase = 12288
            nat = xs[:, dbase:dbase + 256]
            ub32 = ub[:].rearrange("p a b -> p (a b)").bitcast(F32)
            cnat = ub32[:, 0:2048].rearrange("p (g r) -> p g r", g=16)
            cnat2 = ub32[:, 2048:4096].rearrange("p (g r) -> p g r", g=16)
            DK = [XK[6], XK[7], UK[0], UK[1], UK[2], UK[3]]

            def dtile(d, i):
                o = dbase + 256 + (d * 4 + i) * 256
                return xs[:, o:o + 256].rearrange("p (g c) -> p g c", g=16)

            hz = [bufC[:, 0:4, :].rearrange("p a b -> p (a b)"), bufC[:, 4:8, :].rearrange("p a b -> p (a b)")]
            HZK = [[CK[0], CK[1], CK[2], CK[3]], [CK[4], CK[5], CK[6], CK[7]]]
            hcout = bufB[:, 0:2, :].rearrange("p a b -> p (a b)")
            xout = bufB[:, 2:4, :].rearrange("p a b -> p (a b)")
            pbb = pb[:].bitcast(BF16)

            def first():
                P.dma(lambda e: e.dma_start(out=dcol[:, j, :], in_=s5_d[j].rearrange("(g c) -> c g", c=16), allow_slow_non_contiguous=True),
                      writes=[("dcol", j)], chan="s5d", eng="act")

            def expand(out4, Tre, Tim, D1, D2, okeys, rkeys):
                a0 = Tre.rearrange("p k g -> p g k").unsqueeze(3).to_broadcast([128, 16, 16, 16])
                a1 = Tim.rearrange("p k g -> p g k").unsqueeze(3).to_broadcast([128, 16, 16, 16])
                b0 = D1.unsqueeze(2).to_broadcast([128, 16, 16, 16])
                b1 = D2.unsqueeze(2).to_broadcast([128, 16, 16, 16])
                t1 = tmp1.rearrange("p (g k c) -> p g k c", g=16, k=16)
                t2 = tmp2.rearrange("p (g k c) -> p g k c", g=16, k=16)
                P.op("dve", lambda e: e.tensor_tensor(out=t1, in0=a0, in1=b0, op=ALU.mult), reads=KT + DK + rkeys, writes=T1K)
                P.op("dve", lambda e: e.tensor_tensor(out=t2, in0=a1, in1=b1, op=ALU.mult), reads=KT + DK + rkeys, writes=T2K)
                P.op("dve", lambda e: e.tensor_tensor(out=out4, in0=t1, in1=t2, op=ALU.add), reads=T1K + T2K, writes=okeys)

            def per_qd(q, d, g0):
                if True:
                    for h in range(2):
                        P.dma(lambda e, h=h: e.dma_start(out=nat[0:16, 64 * h:64 * h + 64], in_=s5_a_re[j, d, g0:g0 + 16, :]),
                              writes=[("nat", h)] + DK, chan=("s5n", h), eng="act")
                        P.dma(lambda e, h=h: e.dma_start(out=nat[0:16, 128 + 64 * h:192 + 64 * h], in_=s5_a_im[j, d, g0:g0 + 16, :]),
                              writes=[("nat", 2 + h)] + DK, chan=("s5n", 2 + h), eng="act")
                    P.dma(lambda e: e.dma_start(out=sm(2), in_=s5_log_dt[j, d:d + 1, g0:g0 + 16].to_broadcast([128, 16])),
                          writes=["s5dt"] + KT, chan="s5dt", eng="act")
                    NK = [("nat", i) for i in range(4)]
                    P.op("pe", lambda e: e.transpose(out=pa[:, 0:16], in_=nat[0:16, 0:128], identity=ident[0:16, 0:16]),
                         reads=NK + DK + ["cst"], writes=[PSA[0]])
                    P.op("pe", lambda e: e.transpose(out=pa[:, 16:32], in_=nat[0:16, 128:256], identity=ident[0:16, 0:16]),
                         reads=NK + DK + ["cst"], writes=[PSA[0]])
                    are, aim, dtb, x1, q1, den, rden, e1, fre, fim, u1, u2 = [sm(i) for i in range(12)]
                    D_(lambda e: e.tensor_copy(out=A[:, 3840:3872], in_=pa[:, 0:32]), reads=[PSA[0]])
                    D_(lambda e: e.activation(out=dtb, in_=dtb, func=AF.Exp), reads=["s5dt"], eng="act")
                    D_(lambda e: e.tensor_tensor(out=x1, in0=are, in1=dtb, op=ALU.mult))
                    D_(lambda e: e.tensor_tensor(out=q1, in0=aim, in1=dtb, op=ALU.mult))
                    D_(lambda e: e.tensor_scalar(out=q1, in0=q1, scalar1=1.0 / TWO_PI, scalar2=None, op0=ALU.mult))
                    xk, mag, qs, fr, nf, mk, sn, cs, Pre, Pim, PFre, PFim, w1, w2 = [slot(i) for i in range(14)]
                    ni = slot(4).bitcast(I32)
                    D_(lambda e: e.tensor_tensor(out=t3(xk), in0=kvb, in1=x1.unsqueeze(1).to_broadcast([128, 17, 16]), op=ALU.mult))
                    D_(lambda e: e.activation(out=mag, in_=xk, func=AF.Exp), eng="act")
                    D_(lambda e: e.tensor_tensor(out=t3(qs), in0=kvb, in1=q1.unsqueeze(1).to_broadcast([128, 17, 16]), op=ALU.mult))

                    def range_reduce(src):
                        D_(lambda e: e.tensor_copy(out=ni, in_=src))
                        D_(lambda e: e.tensor_copy(out=nf, in_=ni))
                        D_(lambda e: e.tensor_tensor(out=fr, in0=src, in1=nf, op=ALU.subtract))
                        D_(lambda e: e.tensor_single_scalar(out=mk, in_=fr, scalar=0.5, op=ALU.is_gt))
                        D_(lambda e: e.tensor_tensor(out=fr, in0=fr, in1=mk, op=ALU.subtract))
                        D_(lambda e: e.tensor_single_scalar(out=mk, in_=fr, scalar=-0.5, op=ALU.is_lt))
                        D_(lambda e: e.tensor_tensor(out=fr, in0=fr, in1=mk, op=ALU.add))
                    range_reduce(qs)
                    D_(lambda e: e.activation(out=sn, in_=fr, func=AF.Sin, scale=TWO_PI), eng="act")
                    D_(lambda e: e.tensor_scalar(out=qs, in0=qs, scalar1=0.25, scalar2=None, op0=ALU.add))
                    range_reduce(qs)
                    D_(lambda e: e.activation(out=cs, in_=fr, func=AF.Sin, scale=TWO_PI), eng="act")
                    D_(lambda e: e.tensor_tensor(out=Pre, in0=mag, in1=cs, op=ALU.mult))
                    D_(lambda e: e.tensor_tensor(out=Pim, in0=mag, in1=sn, op=ALU.mult))
                    P1re = t3(Pre)[:, 1, :]
                    P1im = t3(Pim)[:, 1, :]
                    D_(lambda e: e.tensor_scalar(out=e1, in0=P1re, scalar1=-1.0, scalar2=None, op0=ALU.add))
                    D_(lambda e: e.tensor_tensor(out=den, in0=are, in1=are, op=ALU.mult))
                    D_(lambda e: e.tensor_tensor(out=u1, in0=aim, in1=aim, op=ALU.mult))
                    D_(lambda e: e.tensor_tensor(out=den, in0=den, in1=u1, op=ALU.add))
                    D_(lambda e: e.reciprocal(out=rden, in_=den))
                    D_(lambda e: e.tensor_tensor(out=fre, in0=e1, in1=are, op=ALU.mult))
                    D_(lambda e: e.tensor_tensor(out=u1, in0=P1im, in1=aim, op=ALU.mult))
                    D_(lambda e: e.tensor_tensor(out=fre, in0=fre, in1=u1, op=ALU.add))
                    D_(lambda e: e.tensor_tensor(out=fre, in0=fre, in1=rden, op=ALU.mult))
                    D_(lambda e: e.tensor_tensor(out=fim, in0=P1im, in1=are, op=ALU.mult))
                    D_(lambda e: e.tensor_tensor(out=u1, in0=e1, in1=aim, op=ALU.mult))
                    D_(lambda e: e.tensor_tensor(out=fim, in0=fim, in1=u1, op=ALU.subtract))
                    D_(lambda e: e.tensor_tensor(out=fim, in0=fim, in1=rden, op=ALU.mult))
                    freb = fre.unsqueeze(1).to_broadcast([128, 16, 16])
                    fimb = fim.unsqueeze(1).to_broadcast([128, 16, 16])
                    P16r = t3(Pre)[:, 0:16, :]
                    P16i = t3(Pim)[:, 0:16, :]
                    W1 = t3(w1)[:, 0:16, :]
                    W2 = t3(w2)[:, 0:16, :]
                    D_(lambda e: e.tensor_tensor(out=W1, in0=P16r, in1=freb, op=ALU.mult))
                    D_(lambda e: e.tensor_tensor(out=W2, in0=P16i, in1=fimb, op=ALU.mult))
                    D_(lambda e: e.tensor_tensor(out=t3(PFre)[:, 0:16, :], in0=W1, in1=W2, op=ALU.subtract))
                    D_(lambda e: e.tensor_tensor(out=W1, in0=P16r, in1=fimb, op=ALU.mult))
                    D_(lambda e: e.tensor_tensor(out=W2, in0=P16i, in1=freb, op=ALU.mult))
                    D_(lambda e: e.tensor_tensor(out=t3(PFim)[:, 0:16, :], in0=W1, in1=W2, op=ALU.add))
                    D_(lambda e: e.tensor_copy(out=scA[:, j, 0, d, g0:g0 + 16], in_=t3(Pre)[:, 16, :]), writes=[("scA", j)])
                    D_(lambda e: e.tensor_scalar(out=scA[:, j, 1, d, g0:g0 + 16], in0=t3(Pim)[:, 16, :], scalar1=sgn, scalar2=None, op0=ALU.mult),
                       reads=["cst"], writes=[("scA", j)])
                    CTn, CTs2, BB, BBs2 = [dtile(d, i) for i in range(4)]
                    P.dma(lambda e: e.dma_start(out=cnat[0:16, :, 0:64], in_=s5_c_re[j, d, g0:g0 + 16].rearrange("g i p -> i g p")), writes=[("cn", 0)] + DK, chan=("s5c", 0), eng="act")
                    P.dma(lambda e: e.dma_start(out=cnat[0:16, :, 64:128], in_=s5_c_im[j, d, g0:g0 + 16].rearrange("g i p -> i g p")), writes=[("cn", 1)] + DK, chan=("s5c", 1), eng="act")
                    P.dma(lambda e: e.dma_start(out=cnat2[0:16, :, 0:64], in_=s5_c_im[j, d, g0:g0 + 16].rearrange("g i p -> i g p")), writes=[("cn", 2)] + DK, chan=("s5c", 2), eng="act")
                    P.dma(lambda e: e.dma_start(out=cnat2[0:16, :, 64:128], in_=s5_c_re[j, d, g0:g0 + 16].rearrange("g i p -> i g p")), writes=[("cn", 3)] + DK, chan=("s5c", 3), eng="act")
                    CNK = [("cn", i) for i in range(4)]
                    for gl in range(16):
                        P.op("pe", lambda e, gl=gl: e.transpose(out=pa[:, 512 + gl * 16:528 + gl * 16], in_=cnat[0:16, gl, :], identity=ident[0:16, 0:16]),
                             reads=CNK + DK + ["cst"], writes=[PSA[1]])
                        P.op("pe", lambda e, gl=gl: e.transpose(out=pa[:, 1024 + gl * 16:1040 + gl * 16], in_=cnat2[0:16, gl, :], identity=ident[0:16, 0:16]),
                             reads=CNK + DK + ["cst"], writes=[PSA[2]])
                    P.op("dve", lambda e: e.tensor_scalar(out=CTn, in0=pa[:, 512:768].rearrange("p (g c) -> p g c", g=16), scalar1=sgn, scalar2=None, op0=ALU.mult),
                         reads=[PSA[1], "cst"], writes=DK + [("dt", d)], ss=True)
                    P.op("dve", lambda e: e.tensor_scalar(out=CTs2, in0=pa[:, 1024:1280].rearrange("p (g c) -> p g c", g=16), scalar1=-1.0, scalar2=None, op0=ALU.mult),
                         reads=[PSA[2]], writes=DK + [("dt", d)], ss=True)
                    P.dma(lambda e: e.dma_start(out=BB[0:64], in_=s5_b_re[j, d, g0:g0 + 16].rearrange("g p c -> p g c")), writes=[("bb", 0)] + DK, chan=("s5b", 0), eng="act")
                    P.dma(lambda e: e.dma_start(out=BB[64:128], in_=s5_b_im[j, d, g0:g0 + 16].rearrange("g p c -> p g c")), writes=[("bb", 1)] + DK, chan=("s5b", 1), eng="act")
                    P.dma(lambda e: e.dma_start(out=BBs2[0:64], in_=s5_b_im[j, d, g0:g0 + 16].rearrange("g p c -> p g c")), writes=[("bb", 2)] + DK, chan=("s5b", 2), eng="act")
                    P.dma(lambda e: e.dma_start(out=BBs2[64:128], in_=s5_b_re[j, d, g0:g0 + 16].rearrange("g p c -> p g c")), writes=[("bb", 3)] + DK, chan=("s5b", 3), eng="act")
                    BBK = [("bb", i) for i in range(4)]
                    P.op("dve", lambda e: e.tensor_scalar(out=BBs2, in0=BBs2, scalar1=sgn, scalar2=-1.0, op0=ALU.mult, op1=ALU.mult),
                         reads=BBK + ["cst"], writes=DK + [("dt", d), ("bb", 2), ("bb", 3)], ss=True)
                    DTK = [("dt", d)] + BBK
                    TPre, TPim, TFre, TFim = t3(Pre), t3(Pim), t3(PFre), t3(PFim)
                    if d == 0:
                        cre, cim = TPre[:, 1:17, :], TPim[:, 1:17, :]
                    else:
                        cre, cim = TPre[:, 16:0:-1, :], TPim[:, 16:0:-1, :]
                    hc4 = hcout.rearrange("p (g k c) -> p g k c", g=16, k=16)
                    expand(hc4, cre, cim, CTn, CTs2, [BK[0], BK[1]], DTK)
                    dstc = twc[j, g0:g0 + 16].rearrange("g p s c -> p g (s c)")[:, :, (3 + 2 * d) * 128:(5 + 2 * d) * 128]
                    P.dma(lambda e, dstc=dstc: e.dma_start(out=dstc, in_=hcout.rearrange("p (g r) -> p g r", g=16)),
                          reads=[BK[0], BK[1]], writes=[("twc", j, q, "c", d)], chan="s5o0", eng="act")
                    if d == 0:
                        zre, zim = TFre[:, 0:16, :], TFim[:, 0:16, :]
                    else:
                        zre, zim = TFre[:, 15::-1, :], TFim[:, 15::-1, :]
                    hz4 = hz[d].rearrange("p (g k c) -> p g k c", g=16, k=16)
                    expand(hz4, zre, zim, CTn, CTs2, HZK[d], DTK)
                    if d == 0:
                        bre, bim = TFre[:, 15::-1, :], TFim[:, 15::-1, :]
                    else:
                        bre, bim = TFre[:, 0:16, :], TFim[:, 0:16, :]
                    x4 = xout.rearrange("p (g k c) -> p g k c", g=16, k=16)
                    expand(x4, bre, bim, BB, BBs2, [BK[2], BK[3]], DTK)
                    xo3 = xout.rearrange("p (g r) -> p g r", g=16)
                    for g4 in range(4):
                        half = g4 % 2
                        PK = [PSB[2 * half], PSB[2 * half + 1]]
                        for gg in range(4):
                            gl = g4 * 4 + gg
                            for b in range(2):
                                idx = gg * 2 + b
                                P.op("pe", lambda e, gl=gl, b=b, idx=idx, half=half: e.transpose(
                                    out=pbb[:, half * 2048 + idx * 128:half * 2048 + (idx + 1) * 128],
                                    in_=xo3[:, gl, b * 128:(b + 1) * 128], identity=identb[:]),
                                    reads=[BK[2], BK[3], "identb"], writes=PK)
                        stg = bufB[:, 4 + half, 0:1024]
                        P.op("act", lambda e, stg=stg, half=half: e.activation(out=stg, in_=pbb[:, half * 2048:half * 2048 + 1024], func=AF.Copy),
                             reads=PK, writes=[BK[4 + half]])
                        dstb = twb[j, g0 + g4 * 4:g0 + g4 * 4 + 4].rearrange("g p s c -> p g (s c)")[:, :, 2 * d * 128:(2 * d + 2) * 128]
                        P.dma(lambda e, dstb=dstb, stg=stg: e.dma_start(out=dstb, in_=stg.rearrange("p (g r) -> p g r", g=4)),
                              reads=[BK[4 + half]], writes=[("twb", j, q, g4, d)], chan=("s5o1", half), eng="act")
            def per_q(q, g0):
                BBf = dtile(0, 2)
                BBb = dtile(1, 2)
                hzf = hz[0].rearrange("p (g r) -> p g r", g=16)
                hzb = hz[1].rearrange("p (g r) -> p g r", g=16)
                P.op("dve", lambda e: e.memset(scT[:, 1:2], 0.0), reads=[], writes=[BK[6], ("zs", 0), ("zs", 1), ("zs", 2), ("zs", 3), "scTb2"])
                for gl in range(16):
                    g = g0 + gl
                    zpt, zpk = (pa, PSA[3]) if gl % 2 == 0 else (pb, PSB[3])
                    P.op("pe", lambda e, gl=gl, zpt=zpt: e.matmul(zpt[0:16, 1536 + 240:1536 + 496], lhsT=BBf[:, gl, :], rhs=hzf[:, gl, :], start=True, stop=False),
                         reads=HZK[0] + DK + [("dt", 0), ("bb", 0), ("bb", 1)], writes=[zpk])
                    P.op("pe", lambda e, gl=gl, zpt=zpt: e.matmul(zpt[0:16, 1536:1536 + 256], lhsT=BBb[:, gl, :], rhs=hzb[:, gl, :], start=False, stop=True),
                         reads=HZK[1] + DK + [("dt", 1), ("bb", 0), ("bb", 1)], writes=[zpk])
                    zi = gl % 4
                    zs = bufB[0:16, 6, zi * 512:zi * 512 + 496]
                    ZK = ("zs", zi)
                    P.op("act", lambda e, zs=zs, zpt=zpt: e.activation(out=zs, in_=zpt[0:16, 1536:2032], func=AF.Copy), reads=[zpk], writes=[ZK])
                    P.op("dve", lambda e, zs=zs, g=g, zpt=zpt: e.scalar_tensor_tensor(out=zs[:, 240:256], in0=cst[0:16, 0:16], scalar=dcol[:, j, g:g + 1],
                                                                            in1=zpt[0:16, 1536 + 240:1536 + 256], op0=ALU.mult, op1=ALU.add),
                         reads=[zpk, ZK, "cst", ("dcol", j)], writes=[ZK], ss=True)
                    zt = bufB[:, 6, :]
                    for di in range(3):
                        off0 = zi * 512 + (15 + 8 * (di - 1)) * 16
                        src = AP(zt.tensor, zt.offset + off0, [[zt.ap[0][0], 16], [-16, 8], [1, 128]])
                        dstz = twc[j, g, :, di, :].rearrange("(s c) r -> c s r", c=16)
                        P.dma(lambda e, src=src, dstz=dstz: e.dma_start(out=dstz, in_=src),
                              reads=[ZK], writes=[("twc", j, q, "z", gl, di)], chan=("s5z", zi), eng="act")
                P.op("dve", lambda e: e.memset(scT[:, 1:2], 0.0), reads=[("zs", 0), ("zs", 1), ("zs", 2), ("zs", 3)], writes=[BK[6], "scTb2"])

            stages = [first]
            for q in range(4):
                stages.append(lambda q=q: per_qd(q, 0, 16 * q))
                stages.append(lambda q=q: (per_qd(q, 1, 16 * q), per_q(q, 16 * q)))
            return stages

        def s5_dump():
            P.dma(lambda e: e.dma_start(out=dbg32[:, :], in_=xs[:, :]), reads=["s5tab"] + XK, writes=["dbg"], chan="dbg")

        def s5_keys(j):
            ks = []
            for q in range(4):
                for d in range(2):
                    ks.append(("twc", j, q, "c", d))
                    for g4 in range(4):
                        ks.append(("twb", j, q, g4, d))
                for gl in range(16):
                    for di in range(3):
                        ks.append(("twc", j, q, "z", gl, di))
            return ks

        def s5_mixer(l, s):
            j = l // 2
            U = bufB[:].rearrange("p a b -> p (a b)").rearrange("p (g c) -> p g c", g=G)
            S = CU[:, :].rearrange("p (m c) -> p m c", m=128)
            CUb = CU[:, :].bitcast(BF16)
            Sbf = CUb[:, 0:8192].rearrange("p (m c) -> p m c", m=64)
            Sbn = CUb[:, 8192:16384].rearrange("p (m c) -> p m c", m=64)
            T3 = scT[:, 0:128].rearrange("p (d g) -> p d g", d=2)
            P3 = scT[:, 128:256].rearrange("p (d g) -> p d g", d=2)
            UD = [("Ud", t) for t in range(8)]
            tkeys = s5_keys(j)

            def bounce_in():
                P.op("sp", lambda e: e.nop(), reads=[("Ug", g) for g in range(G)], writes=["Ufree"] + BK)
                for kc in range(KC):
                    P.dma(lambda e, kc=kc: e.dma_start(out=d_in[:, kc * 128:(kc + 1) * 128, :].rearrange("t f c -> f t c"),
                                                       in_=ub[:, kc, :].rearrange("p (t c) -> p t c", t=8)),
                          reads=[UK[kc]], writes=[("din", kc)], chan=("bi", kc % 4))
                for kc in range(KC):
                    for t in range(8):
                        P.dma(lambda e, t=t, kc=kc: e.dma_start(out=U[t * 16:(t + 1) * 16, kc * 8:(kc + 1) * 8, :],
                                                              in_=d_in[t, kc * 128:(kc + 1) * 128, :].rearrange("(g j) c -> j g c", j=16)),
                              reads=[("din", kc), "Ufree"], writes=[("Ud", kc, t)], chan=("bu", kc))
            cjob(bounce_in)

            for tix in range(8):
                gb = tix * 8
                src = twb[j, gb:gb + 8].rearrange("g p s c -> p g (s c)")

                def run(tv, key, gb=gb, tix=tix):
                    tv = tv.rearrange("p g (s c) -> p g s c", s=4)
                    pt, pk = (pa, PSA) if tix % 2 == 0 else (pb, PSB)
                    for d in range(2):
                        for gl in range(8):
                            g = gb + gl
                            o = pt[:, (d * 8 + gl) * 128:(d * 8 + gl + 1) * 128]
                            kk = pk[(d * 8 + gl) // 4]
                            udk = [("Ud", g // 8, t_) for t_ in range(8)]
                            mm(o, tv[:, gl, 2 * d, :], U[:, g, 0:256:2], True, False, key + udk + [("Ug", g)], [kk])
                            mm(o, tv[:, gl, 2 * d + 1, :], U[:, g, 1:256:2], False, True, key + udk + [("Ug", g)], [kk])
                    P.op("act", lambda e, pt=pt, gb=gb: e.activation(out=S[:, gb:gb + 8, :], in_=pt[:, 0:1024].rearrange("p (g c) -> p g c", g=8), func=AF.Copy),
                         reads=[pk[0], pk[1]], writes=CK)
                    P.op("dve", lambda e, pt=pt, gb=gb: e.tensor_copy(out=S[:, 64 + gb:64 + gb + 8, :],
                                                                    in_=pt[:, 1024:2048].rearrange("p (g c) -> p g c", g=8)[:, :, 127::-1]),
                         reads=[pk[2], pk[3]], writes=UK)
                wjob(src, lambda slot: bfview(slot, (8, 4, 128)).rearrange("p g s c -> p g (s c)"), run, tkeys)

            def scan():
                Ar = scA[:, j, 0, :, :]
                AiN = scA[64:128, j, 1, :, :]
                AiP = scA[0:64, j, 1, :, :]
                SK = CK + UK
                for n in range(1, 128):
                    prev = S[:, :, n - 1].rearrange("p (d g) -> p d g", d=2)
                    cur = S[:, :, n].rearrange("p (d g) -> p d g", d=2)
                    kp, kc_ = ("scS", (n - 1) % 2), ("scS", n % 2)
                    base = SK + [("scA", j)]
                    P.op("dve", lambda e, prev=prev: e.tensor_tensor(out=P3, in0=prev, in1=Ar, op=ALU.mult),
                         reads=base + [kp], writes=["scP"], ss=True)
                    P.op("dve", lambda e, prev=prev: e.tensor_tensor(out=T3[0:64], in0=prev[64:128], in1=AiN, op=ALU.mult),
                         reads=base + [kp], writes=["scTa"], ss=True)
                    P.op("pool", lambda e, prev=prev: e.tensor_tensor(out=T3[64:128], in0=prev[0:64], in1=AiP, op=ALU.mult),
                         reads=base + [kp], writes=["scTb"], ss=True)
                    P.op("dve", lambda e, cur=cur: e.tensor_tensor(out=cur, in0=cur, in1=P3, op=ALU.add),
                         reads=base + ["scP", kc_], writes=[kc_], ss=True)
                    P.op("dve", lambda e, cur=cur: e.tensor_tensor(out=cur, in0=cur, in1=T3, op=ALU.add),
                         reads=base + ["scTa", "scTb", kc_], writes=[kc_], ss=True)
                P.op("act", lambda e: e.activation(out=Sbf, in_=S[:, 0:64, :], func=AF.Copy), reads=CK + [("scS", 0), ("scS", 1)], writes=CK, ss=True)
                P.op("act", lambda e: e.activation(out=Sbn, in_=S[:, 64:128, 127::-1], func=AF.Copy), reads=CK + UK + [("scS", 0), ("scS", 1)], writes=CK, ss=True)
            cjob(scan)

            for tix in range(16):
                gb = tix * 4
                src = twc[j, gb:gb + 4].rearrange("g p s c -> p g (s c)")

                def run(tv, key, gb=gb, tix=tix):
                    tv = tv.rearrange("p g (s c) -> p g s c", s=7)
                    pt, pk = (pa, PSA) if tix % 2 == 0 else (pb, PSB)
                    for gl in range(4):
                        g = gb + gl
                        for bo in range(2):
                            o = pt[:, (gl * 2 + bo) * 128:(gl * 2 + bo + 1) * 128]
                            kk = pk[(gl * 2 + bo) // 4]
                            rk = key + [("Ud", g // 8, t_) for t_ in range(8)] + [("Ug", g)]
                            mm(o, tv[:, gl, bo + 1, :], U[:, g, 0:256:2], True, False, rk, [kk])
                            mm(o, tv[:, gl, bo, :], U[:, g, 1:256:2], False, False, rk, [kk])
                            mm(o[:, 1:128], tv[:, gl, 3 + bo, :], Sbf[:, g, 0:127], False, False, key + CK, [kk])
                            mm(o[:, 0:127], tv[:, gl, 5 + bo, :], Sbn[:, g, 1:128], False, True, key + CK, [kk])
                    yo = U[:, gb:gb + 4, :].rearrange("p g (c b) -> p g b c", b=2)
                    yi = pt[:, 0:1024].rearrange("p (g b c) -> p g b c", g=4, b=2)
                    P.op("act", lambda e, yo=yo, yi=yi: e.activation(out=yo, in_=yi, func=AF.Gelu_apprx_tanh),
                         reads=[pk[0], pk[1]], writes=[("Ug", g_) for g_ in range(gb, gb + 4)] + [("Yd", gb // 4)])
                wjob(src, lambda slot: bfview(slot, (4, 7, 128)).rearrange("p g s c -> p g (s c)"), run, tkeys)

            if stop == "s5_yg":
                def dump():
                    P.dma(lambda e: e.dma_start(out=dbg[:], in_=bufB[:]), reads=[("Yd", i) for i in range(16)] + BK, writes=["dbg"], chan="dbg")
                cjob(dump)
                return
            def bounce_out():
                for kc in range(KC):
                    ykeys = [("Yd", 2 * kc), ("Yd", 2 * kc + 1)] + [("Ug", g) for g in range(kc * 8, kc * 8 + 8)]
                    for t in range(8):
                        P.dma(lambda e, t=t, kc=kc: e.dma_start(out=d_out[t, kc * 128:(kc + 1) * 128, :].rearrange("(g j) c -> j g c", j=16),
                                                              in_=U[t * 16:(t + 1) * 16, kc * 8:(kc + 1) * 8, :]),
                              reads=ykeys, writes=[("dout", kc, t)], chan=("bo", kc))
                    P.dma(lambda e, kc=kc: e.dma_start(out=ub[:, kc, :].rearrange("p (t c) -> p t c", t=8),
                                                       in_=d_out[:, kc * 128:(kc + 1) * 128, :].rearrange("t f c -> f t c")),
                          reads=[("dout", kc, t) for t in range(8)], writes=[UK[kc]], chan=("bz", kc % 4))
            cjob(bounce_out)

            sg = bufC[:, 0:2, :].rearrange("p a b -> p (a b)")
            mt = bufC[:, 2:4, :].rearrange("p a b -> p (a b)")
            for q in range(4):
                w4 = wb_glu[j].rearrange("(k p) (c o) -> p k c o", p=128, c=2)
                src = [(w4[:, :, c, q * 256:(q + 1) * 256], (lambda t, c=c: t[:, :, c, :])) for c in range(2)]

                def run(tv, key, q=q):
                    for oo in range(2):
                        o = q * 2 + oo
                        for c, pt, pk in ((1, pb, PSB), (0, pa, PSA)):
                            for tt in range(NTT):
                                for kc in range(KC):
                                    mm(pt[:, tt * TT:(tt + 1) * TT], tv[:, kc, c, oo * 128:(oo + 1) * 128], ub[:, kc, tt * TT:(tt + 1) * TT],
                                       kc == 0, kc == KC - 1, key + [UK[kc]], [pk[tt]])
                            if c == 1:
                                P.op("act", lambda e: e.activation(out=sg, in_=pb[:], func=AF.Sigmoid), reads=PSB, writes=[CK[0], CK[1]])
                        P.op("dve", lambda e: e.tensor_tensor(out=mt, in0=pa[:], in1=sg, op=ALU.mult), reads=PSA + [CK[0], CK[1]], writes=[CK[2], CK[3]])
                        P.op("dve", lambda e, o=o: e.scalar_tensor_tensor(out=x[:, o, :].rearrange("p (c t) -> p c t", t=8),
                                                                         in0=mt.rearrange("p (t c) -> p c t", t=8), scalar=modT[:, l, 16 + o, s:s + 1],
                                                                         in1=x[:, o, :].rearrange("p (c t) -> p c t", t=8), op0=ALU.mult, op1=ALU.add),
                             reads=[CK[2], CK[3], XK[o], ("mod", l)], writes=[XK[o]])
                wjob(src, lambda slot: bfview(slot, (KC, 2, 256)), run, scr_blocks[('glu', j)])


        P.op("dve", lambda e: e.tensor_copy(out=identb[:], in_=ident), reads=["cst"], writes=["identb"])
        pump(0)
        for l in layers:
            ada_jobs(l)
        for ji, j_ in enumerate(used_s5):
            stages = s5_setup(j_)
            last = (ji == len(used_s5) - 1) and len(used_s5) > 1
            per = (len(ada_list) + 7) // 8
            for st_i, st_fn in enumerate(stages):
                cjob(st_fn)
                if last and st_i >= 1:
                    emit_ada(per)
        if stop == "s5w":
            cjob(s5_dump)
        emit_ada(10 ** 6)
        for li, l in enumerate(layers):
            nxt = layers[li + 1] if li + 1 < len(layers) else None
            ada_post(l, nxt)
        for li, l in enumerate(layers):
            nxt = layers[li + 1] if li + 1 < len(layers) else None
            fuse_consts(l, 0, l, 4, 3)
            if nxt is not None:
                fuse_consts(l, 1, nxt, 1, 0)
        for s in range(NS if stop != "s5w" else 0):
            swap_x(s - 1 if s > 0 else None, s, layers[0] if layers else None)
            for li, l in enumerate(layers):
                nxt = layers[li + 1] if li + 1 < len(layers) else None
                if l % 2 == 0:
                    conv_mixer(l, s)
                else:
                    s5_mixer(l, s)
                if stop in ("mixer", "s5_yg"):
                    break
                layer_norm(l, 0, s, True)
                if stop == "ln1":
                    break
                if stop == "cst":
                    def dump():
                        n1 = DEPTH * 48 * NS
                        n2 = DEPTH * 2 * KC * NS
                        P.dma(lambda e: e.dma_start(out=dbg32[:, 0:n1], in_=modT[:].rearrange("p a b c -> p (a b c)")), reads=[("mod", l)], writes=["dbg"], chan="dbg")
                        P.dma(lambda e: e.dma_start(out=dbg32[:, n1:n1 + n2], in_=g2c[:].rearrange("p a b c d -> p (a b c d)")), reads=[("g2c", l)], writes=["dbg1"], chan="dbg1")
                        P.dma(lambda e: e.dma_start(out=dbg32[:, n1 + n2:n1 + 2 * n2], in_=b2c[:].rearrange("p a b c d -> p (a b c d)")), reads=[("g2c", l)], writes=["dbg2"], chan="dbg2")
                        P.dma(lambda e: e.dma_start(out=dbg32[:, n1 + 2 * n2:n1 + 2 * n2 + 128], in_=lnp[:].rearrange("p a b c -> p (a b c)")), reads=["lnp"], writes=["dbg3"], chan="dbg3")
                    cjob(dump)
                    break
                if stop == "ub":
                    def dump():
                        P.dma(lambda e: e.dma_start(out=dbg[:], in_=ub[:]), reads=UK, writes=["dbg"], chan="dbg")
                    cjob(dump)
                    break
                ffn(l, s)
                if stop in ("ffn", "ffn_g0", "ffn_dbg"):
                    break
                layer_norm(l, 1, s, nxt is not None, perm_u=(nxt is not None and nxt % 2 == 1))
        if stop != "s5w":
            swap_x(NS - 1, None, None)
        run_jobs()
        ykeys = [("yout", s, tb) for s in range(NS) for tb in range(16)] if stop != "s5w" else s5_keys(0) + ["dbg"]
        P.op("sp", lambda e: e.nop(), reads=ykeys + (["dbg"] if stop in ("ffn_g0", "ffn_dbg", "ub", "cst", "s5_yg") else []) + (["dbg1", "dbg2", "dbg3"] if stop == "cst" else []), writes=["done"])

        sems = {}
        for en in ("pe", "act", "dve", "pool", "sp"):
            sems[en] = es.enter_context(nc.semaphore("sem_" + en))
        for ci, ch in enumerate(P.channels()):
            sems[("chan", ch)] = es.enter_context(nc.semaphore("semc_%d" % ci))
        block = es.enter_context(nc.Block())
        P.finalize(block, sems)
    build_program.P = P
    return nc


def make_consts():
    c = np.zeros((128, 256), np.float32)
    c[:, 0:128] = np.eye(128, dtype=np.float32)
    c[0:64, 128] = 1.0
    c[64:128, 128] = -1.0
    c[:, 129] = 1.0
    c[:, 130] = EPS_P
    c[:, 131:131 + 32] = np.arange(32, dtype=np.float32)[None, :]
    return c


_WNAMES = ["ada_w", "ada_b", "ln1_g", "ln1_b", "ln2_g", "ln2_b", "sc_w_in", "sc_conv_w", "sc_conv_b", "sc_w_out",
           "s5_a_re", "s5_a_im", "s5_log_dt", "s5_b_re", "s5_b_im", "s5_c_re", "s5_c_im", "s5_d", "s5_w_glu",
           "ffn_w_up", "ffn_conv_w", "ffn_conv_b", "ffn_w_down"]


def run_cores(x_all, c_all, weights, NS, layers, ncores, stop=None):
    nc = build_program(NS, layers, stop)
    consts = make_consts()
    in_maps = []
    for ci in range(ncores):
        m = {"xin": np.ascontiguousarray(x_all[ci * NS:(ci + 1) * NS]),
             "cin": np.ascontiguousarray(c_all[ci * NS:(ci + 1) * NS]),
             "consts": consts}
        for n in _WNAMES:
            m[n] = weights[n]
        in_maps.append(m)
    res = run_bass_kernel_spmd(nc, in_maps, core_ids=list(range(ncores)))
    if stop:
        run_cores.dbg = res.results[0]["dbg"]
        run_cores.dbg32 = res.results[0]["dbg32"]
    if stop == "wup":
        run_cores.wup = res.results[0]["wb_up"]
    if stop == "s5w":
        run_cores.twb = res.results[0]["twb"]
        run_cores.twc = res.results[0]["twc"]
    return np.concatenate([r["yout"] for r in res.results], axis=0)


def kernel(x_prompt, x_sample, c_prompt, c_sample, **w):
    x_prompt = np.asarray(x_prompt, np.float32)
    x_sample = np.asarray(x_sample, np.float32)
    c_prompt = np.asarray(c_prompt, np.float32)
    c_sample = np.asarray(c_sample, np.float32)
    weights = {n: np.ascontiguousarray(np.asarray(w[n], np.float32)) for n in _WNAMES}
    xs, cs = [], []
    for i in range(NCORES):
        xs.append(x_prompt[4 * i:4 * i + 4]); xs.append(x_sample[i:i + 1])
        cs.append(c_prompt[4 * i:4 * i + 4]); cs.append(c_sample[i:i + 1])
    x_all = np.concatenate(xs, axis=0)
    c_all = np.concatenate(cs, axis=0)
    y = run_cores(x_all, c_all, weights, 5, [0, 1, 2, 3], NCORES)
    y = y.reshape(NCORES, 5, L, D)
    y_prompt = np.ascontiguousarray(y[:, 0:4].reshape(32, L, D))
    y_sample = np.ascontiguousarray(y[:, 4].reshape(8, L, D))
    return (y_prompt, y_sample)
```

```python
from contextlib import ExitStack
import numpy as np
import concourse.bass as bass
import concourse.mybir as mybir
from concourse.bass_utils import run_bass_kernel_spmd
from concourse.ap import AP

F32 = mybir.dt.float32
BF16 = mybir.dt.bfloat16
I32 = mybir.dt.int32
AF = mybir.ActivationFunctionType
ALU = mybir.AluOpType

D = 1024
KC = 8
L = 2048
TT = 512
NTT = 4
FH = 2816
FC = 22
DEPTH = 4
G = 64
PS = 64
NCORES = 8
ALPHA = (2 * DEPTH) ** 0.25
EPS_P = 1e-5 / (ALPHA * ALPHA)
NSLOT = 4
HGROUPS = [(0, 8), (8, 16), (16, 22)]


class Op:
    __slots__ = ("eng", "fn", "reads", "writes", "deps", "is_dma", "chan", "sig", "val", "idx", "ss")

    def __init__(self, eng, fn, reads, writes, is_dma=False, chan=None):
        self.eng = eng
        self.fn = fn
        self.reads = reads
        self.writes = writes
        self.deps = ()
        self.is_dma = is_dma
        self.chan = chan
        self.sig = False
        self.val = 0
        self.idx = -1
        self.ss = False


class Prog:
    ENGS = ("pe", "act", "dve", "pool", "sp")

    def __init__(self):
        self.ops = []
        self.last_w = {}
        self.readers = {}

    def op(self, eng, fn, reads=(), writes=(), ss=False):
        o = Op(eng, fn, tuple(reads), tuple(writes))
        o.ss = ss
        self._add(o)
        return o

    def dma(self, fn, reads=(), writes=(), chan=None, eng="sp"):
        o = Op(eng, fn, tuple(reads), tuple(writes), is_dma=True, chan=chan)
        self._add(o)
        return o

    def _add(self, o):
        o.idx = len(self.ops)
        deps = set()
        for k in o.reads:
            w = self.last_w.get(k)
            if w is not None:
                deps.add(w)
        for k in o.writes:
            w = self.last_w.get(k)
            if w is not None:
                deps.add(w)
            deps.update(self.readers.get(k, ()))
        o.deps = deps
        for k in o.reads:
            self.readers.setdefault(k, []).append(o.idx)
        for k in o.writes:
            self.last_w[k] = o.idx
            self.readers[k] = []
        self.ops.append(o)

    def channels(self):
        return sorted({o.chan for o in self.ops if o.is_dma}, key=str)

    def finalize(self, block, sems):
        ops = self.ops
        for o in ops:
            for d in o.deps:
                p = ops[d]
                if p.is_dma or p.eng != o.eng or o.is_dma or p.ss:
                    p.sig = True
        cnt = {e: 0 for e in self.ENGS}
        ccnt = {}
        for o in ops:
            if o.is_dma:
                ccnt[o.chan] = ccnt.get(o.chan, 0) + 1
                o.val = 16 * ccnt[o.chan]
                o.sig = True
            elif o.sig:
                cnt[o.eng] += 1
                o.val = cnt[o.eng]
        per_eng = {e: [] for e in self.ENGS}
        for o in ops:
            per_eng[o.eng].append(o)

        def emit_stream(engname, e):
            waited = {}
            for o in per_eng[engname]:
                need = {}
                for d in o.deps:
                    p = ops[d]
                    if p.is_dma:
                        key = ("chan", p.chan)
                    else:
                        if p.eng == engname and not o.is_dma and not p.ss:
                            continue
                        key = p.eng
                    if p.val > need.get(key, 0):
                        need[key] = p.val
                for key, v in need.items():
                    if waited.get(key, 0) >= v:
                        continue
                    waited[key] = v
                    e.wait_ge(sems[key], v)
                ins = o.fn(e)
                if o.sig:
                    if o.is_dma:
                        ins.then_inc(sems[("chan", o.chan)], 16)
                    else:
                        ins.then_inc(sems[engname], 1)

        @block.tensor
        def _(e):
            emit_stream("pe", e)

        @block.scalar
        def _(e):
            emit_stream("act", e)

        @block.vector
        def _(e):
            emit_stream("dve", e)

        @block.gpsimd
        def _(e):
            emit_stream("pool", e)

        @block.sync
        def _(e):
            emit_stream("sp", e)


def build_program(NS, layers, stop=None):
    nc = bass.Bass("TRN2", target_bir_lowering=False)

    def din(name, shape):
        return nc.dram_tensor(name, list(shape), F32, kind="ExternalInput").ap()

    xin = din("xin", (NS, L, D))
    cin = din("cin", (NS, D))
    consts = din("consts", (128, 256))
    ada_w = din("ada_w", (DEPTH, D, 6 * D))
    ada_b = din("ada_b", (DEPTH, 6 * D))
    ln_gb = [din("ln1_g", (DEPTH, D)), din("ln1_b", (DEPTH, D)), din("ln2_g", (DEPTH, D)), din("ln2_b", (DEPTH, D))]
    sc_w_in = din("sc_w_in", (2, D, 3 * D))
    sc_conv_w = din("sc_conv_w", (2, 3, D))
    sc_conv_b = din("sc_conv_b", (2, D))
    sc_w_out = din("sc_w_out", (2, D, D))
    s5_a_re = din("s5_a_re", (2, 2, G, PS))
    s5_a_im = din("s5_a_im", (2, 2, G, PS))
    s5_log_dt = din("s5_log_dt", (2, 2, G))
    s5_b_re = din("s5_b_re", (2, 2, G, PS, 16))
    s5_b_im = din("s5_b_im", (2, 2, G, PS, 16))
    s5_c_re = din("s5_c_re", (2, 2, G, 16, PS))
    s5_c_im = din("s5_c_im", (2, 2, G, 16, PS))
    s5_d = din("s5_d", (2, D))
    s5_w_glu = din("s5_w_glu", (2, D, 2 * D))
    ffn_w_up = din("ffn_w_up", (DEPTH, D, 2 * FH))
    ffn_conv_w = din("ffn_conv_w", (DEPTH, 3, 2 * FH))
    ffn_conv_b = din("ffn_conv_b", (DEPTH, 2 * FH))
    ffn_w_down = din("ffn_w_down", (DEPTH, FH, D))
    yout = nc.dram_tensor("yout", [NS, L, D], F32, kind="ExternalOutput").ap()
    dbg = nc.dram_tensor("dbg", [128, 8, L], BF16, kind="ExternalOutput").ap() if stop else None
    dbg32 = nc.dram_tensor("dbg32", [128, 16384], F32, kind="ExternalOutput").ap() if stop else None

    def dscr(name, shape, dt=BF16):
        return nc.dram_tensor(name, list(shape), dt).ap()

    wb_in = dscr("wb_in", (2, D, 3 * D))
    wb_out = dscr("wb_out", (2, D, D))
    wb_glu = dscr("wb_glu", (2, D, 2 * D))
    wb_up = (nc.dram_tensor("wb_up", [DEPTH, D, 2 * FH], BF16, kind="ExternalOutput").ap() if stop == "wup"
             else dscr("wb_up", (DEPTH, D, 2 * FH)))
    wb_dn = dscr("wb_dn", (DEPTH, FH, D))

    if stop == "s5w":
        twb = nc.dram_tensor("twb", [2, G, 128, 4, 128], BF16, kind="ExternalOutput").ap()
        twc = nc.dram_tensor("twc", [2, G, 128, 7, 128], BF16, kind="ExternalOutput").ap()
    else:
        twb = dscr("twb", (2, G, 128, 4, 128))
        twc = dscr("twc", (2, G, 128, 7, 128))
    d_in = dscr("d_in", (8, D, 256))
    d_out = dscr("d_out", (8, D, 256))

    P = Prog()
    jobs = []

    with ExitStack() as es:
        def sb(name, shape, dt):
            return es.enter_context(nc.sbuf_tensor(name, list(shape), dt))

        x = sb("x", (128, KC, L), F32)
        CU = sb("CU", (128, 16384), F32)
        bufC = CU[:, 0:8192].rearrange("p (a b) -> p a b", a=8)
        ub = CU[:, 8192:16384].bitcast(BF16).rearrange("p (a b) -> p a b", a=KC)
        bufB = sb("bufB", (128, 8, L), BF16)
        scT = sb("scT", (128, 256), F32)
        ring = sb("ring", (128, NSLOT, 2048), F32)
        cst = sb("cst", (128, 256), F32)
        ones_b = sb("ones_b", (128, 128), BF16)
        modT = sb("modT", (128, DEPTH, 48, NS), F32)
        g2c = sb("g2c", (128, DEPTH, 2, KC, NS), F32)
        b2c = sb("b2c", (128, DEPTH, 2, KC, NS), F32)
        lnp = sb("lnp", (128, 4, DEPTH, KC), F32)
        adab = sb("adab", (128, DEPTH, 48), F32)
        cT = sb("cT", (128, NS, KC), F32)
        sccw = sb("sccw", (128, 2, 3, KC), F32)
        sccb = sb("sccb", (128, 2, KC), F32)
        fcw = sb("fcw", (128, DEPTH, 3, 2 * FC), F32)
        fcb = sb("fcb", (128, DEPTH, 2 * FC), F32)
        identb = sb("identb", (128, 128), BF16)
        scA = sb("scA", (128, 2, 2, 2, G), F32)
        dcol = sb("dcol", (16, 2, G), F32)
        pa = es.enter_context(nc.psum_tensor("pa", [128, 2048], F32))
        pb = es.enter_context(nc.psum_tensor("pb", [128, 2048], F32))
        ident = cst[:, 0:128]

        PSA = [("ps", i) for i in range(4)]
        PSB = [("ps", 4 + i) for i in range(4)]
        XK = [("x", k) for k in range(KC)]
        UK = [("ub", k) for k in range(KC)]
        BK = [("B", k) for k in range(8)]
        CK = [("C", k) for k in range(8)]

        ring_n = [0]

        def wjob(src, view_fn, run, rkeys=()):
            jobs.append(((src, view_fn, tuple(rkeys)), run))

        def cjob(run):
            jobs.append((None, run))

        def run_jobs():
            loads = [i for i, j in enumerate(jobs) if j[0] is not None]
            tiles = {}
            li = [0]

            def issue(upto):
                while li[0] < len(loads) and li[0] < upto:
                    ji = loads[li[0]]
                    s = ring_n[0] % NSLOT
                    ring_n[0] += 1
                    src, view_fn, rkeys = jobs[ji][0]
                    tv = view_fn(ring[:, s, :])
                    if not isinstance(src, list):
                        src = [(src, lambda t: t)]
                    wkeys = []
                    for si, (sap, sel) in enumerate(src):
                        dst = sel(tv)
                        P.dma(lambda e, dst=dst, sap=sap: e.dma_start(out=dst, in_=sap),
                              reads=rkeys, writes=[("w", s, si)], chan=("w", s))
                        wkeys.append(("w", s, si))
                    for si in range(len(src), 3):
                        wkeys.append(("w", s, si))
                    tiles[ji] = (tv, wkeys)
                    li[0] += 1

            nl = 0
            for ji, (ld, run) in enumerate(jobs):
                if ld is not None:
                    issue(nl + NSLOT - 1)
                    issue(nl + 1)
                    nl += 1
                    tv, key = tiles.pop(ji)
                    run(tv, key)
                else:
                    run()

        def bfview(slot, shape):
            v = slot.bitcast(BF16)
            n = int(np.prod(shape))
            v = v[:, 0:n]
            if len(shape) == 2:
                return v.rearrange("p (a b) -> p a b", a=shape[0])
            if len(shape) == 3:
                return v.rearrange("p (a b c) -> p a b c", a=shape[0], b=shape[1])
            if len(shape) == 4:
                return v.rearrange("p (a b c d) -> p a b c d", a=shape[0], b=shape[1], c=shape[2])
            return v

        def mm(out, lhsT, rhs, start, stop, reads, writes):
            P.op("pe", lambda e: e.matmul(out, lhsT=lhsT, rhs=rhs, start=start, stop=stop),
                 reads=reads, writes=writes)

        P.dma(lambda e: e.dma_start(out=cst[:], in_=consts[:]), writes=["cst"], chan="cst")
        P.op("dve", lambda e: e.memset(ones_b[:], 1.0), writes=["ones"])

        def load_rows_T(src2d, nrows, dst_fn, tag):
            r0 = 0
            i = 0
            while r0 < nrows:
                r = min(128, nrows - r0)
                blk = i % 2
                st = bufC[:, blk, 0:128]
                pst = pa[:, blk * 512: blk * 512 + 128]
                P.dma(lambda e, st=st, r=r, r0=r0: e.dma_start(out=st[0:r, :], in_=src2d[r0:r0 + r, :]),
                      writes=[CK[blk]], chan=("ldT", blk))
                P.op("pe", lambda e, st=st, r=r, pst=pst: e.transpose(out=pst[:, 0:r], in_=st[0:r, :], identity=ident[0:r, 0:r]),
                     reads=[CK[blk], "cst"], writes=[PSA[blk]])
                dst = dst_fn(r0, r)
                P.op("dve", lambda e, dst=dst, r=r, pst=pst: e.tensor_copy(out=dst, in_=pst[:, 0:r]),
                     reads=[PSA[blk]], writes=[tag], ss=True)
                r0 += r
                i += 1

        for i4 in range(4):
            src = ln_gb[i4].rearrange("l (k p) -> (l k) p", p=128)
            load_rows_T(src, DEPTH * KC, lambda r0, r, i4=i4: lnp[:, i4, :, :].rearrange("p l k -> p (l k)")[:, r0:r0 + r], "lnp")
        load_rows_T(ada_b.rearrange("l (o p) -> (l o) p", p=128), DEPTH * 48,
                    lambda r0, r: adab[:].rearrange("p l o -> p (l o)")[:, r0:r0 + r], "adab")
        load_rows_T(cin.rearrange("s (k p) -> (s k) p", p=128), NS * KC,
                    lambda r0, r: cT[:].rearrange("p s k -> p (s k)")[:, r0:r0 + r], "cT")
        load_rows_T(sc_conv_w.rearrange("j t (k p) -> (j t k) p", p=128), 2 * 3 * KC,
                    lambda r0, r: sccw[:].rearrange("p j t k -> p (j t k)")[:, r0:r0 + r], "sccw")
        load_rows_T(sc_conv_b.rearrange("j (k p) -> (j k) p", p=128), 2 * KC,
                    lambda r0, r: sccb[:].rearrange("p j k -> p (j k)")[:, r0:r0 + r], "sccb")
        load_rows_T(ffn_conv_w.rearrange("l t (m p) -> (l t m) p", p=128), DEPTH * 3 * 2 * FC,
                    lambda r0, r: fcw[:].rearrange("p l t m -> p (l t m)")[:, r0:r0 + r], "fcw")
        load_rows_T(ffn_conv_b.rearrange("l (m p) -> (l m) p", p=128), DEPTH * 2 * FC,
                    lambda r0, r: fcb[:].rearrange("p l m -> p (l m)")[:, r0:r0 + r], "fcb")
        P.op("act", lambda e: e.activation(out=cT[:], in_=cT[:], func=AF.Silu), reads=["cT"], writes=["cT"], ss=True)

        conv_ctr = [0]

        scr_blocks = {}

        conv_blocks = []

        def convert(src2d, dst2d, K, M, name):
            scr_blocks[name] = []
            for r0 in range(0, K, 128):
                for c0 in range(0, M, 1024):
                    cw = min(1024, M - c0)
                    conv_blocks.append((src2d, dst2d, r0, c0, cw, name))
                    scr_blocks[name].append(("wscr", name, r0, c0))

        def pump(n):
            blocks = list(conv_blocks)
            del conv_blocks[:]
            nb = len(blocks)
            info = {}

            def rec_load(bi):
                src2d, dst2d, r0, c0, cw, name = blocks[bi]
                i = bi % NSLOT
                st32 = ring[:, i, 0:cw]
                st16 = ring[:, i, 1024:1536].bitcast(BF16)[:, 0:cw]
                k32 = [("w", i, 0), ("w", i, 1)]
                k16 = [("w", i, 2)]
                info[bi] = (st32, st16, k32, k16)
                P.dma(lambda e: e.dma_start(out=st32, in_=src2d[r0:r0 + 128, c0:c0 + cw]), writes=k32, chan=("cvi", i))

            def rec_rest(bi):
                src2d, dst2d, r0, c0, cw, name = blocks[bi]
                i = bi % NSLOT
                st32, st16, k32, k16 = info.pop(bi)
                P.op("pool", lambda e: e.tensor_copy(out=st16, in_=st32), reads=k32, writes=k16)
                P.dma(lambda e: e.dma_start(out=dst2d[r0:r0 + 128, c0:c0 + cw], in_=st16),
                      reads=k16, writes=[("wscr", name, r0, c0)], chan=("cvo", i))
            AHEAD = 2
            for bi in range(min(AHEAD, nb)):
                rec_load(bi)
            for bi in range(nb):
                if bi + AHEAD < nb:
                    rec_load(bi + AHEAD)
                rec_rest(bi)

        used_conv = sorted({l // 2 for l in layers if l % 2 == 0})
        used_s5 = sorted({l // 2 for l in layers if l % 2 == 1})
        wkeys = {}
        for l in layers:
            j = l // 2
            if l % 2 == 0:
                convert(sc_w_in[j], wb_in[j], D, 3 * D, ('in', j))
                convert(sc_w_out[j], wb_out[j], D, D, ('out', j))
            else:
                convert(s5_w_glu[j], wb_glu[j], D, 2 * D, ('glu', j))
            convert(ffn_w_up[l], wb_up[l], D, 2 * FH, ('up', l))
            convert(ffn_w_down[l], wb_dn[l], FH, D, ('dn', l))

        ada_list = []

        def emit_ada(n):
            for _ in range(n):
                if not ada_list:
                    return
                a_src, a_view, a_run = ada_list.pop(0)
                wjob(a_src, a_view, a_run)

        def ada_jobs(l):
            for cb in range(48):
                src = ada_w[l].rearrange("(k p) o -> p k o", p=128)[:, :, cb * 128:(cb + 1) * 128]

                def run(tv, key, cb=cb, l=l):
                    oc = cb
                    bnk = cb % 4
                    pst = pb[0:NS, bnk * 512:bnk * 512 + 128]
                    for kc in range(KC):
                        mm(pst, cT[:, :, kc], tv[:, kc, :], kc == 0, kc == KC - 1, key + ["cT"], [PSB[bnk]])
                    slot_f = tv.tensor
                    tmp = AP(slot_f, tv.offset + 1024, [[tv.ap[0][0], NS], [1, 128]])
                    P.op("act", lambda e, tmp=tmp, pst=pst: e.activation(out=tmp, in_=pst, func=AF.Copy),
                         reads=[PSB[bnk]], writes=[key[2]], ss=True)
                    ptr = pa[:, bnk * 512:bnk * 512 + NS]
                    P.op("pe", lambda e, tmp=tmp, ptr=ptr: e.transpose(out=ptr, in_=tmp, identity=ident[0:NS, 0:NS]),
                         reads=[key[0], key[2], "cst"], writes=[PSA[bnk]])
                    P.op("dve", lambda e, ptr=ptr, oc=oc, l=l: e.tensor_scalar(
                        out=modT[:, l, oc, :], in0=ptr, scalar1=adab[:, l, oc:oc + 1], scalar2=None, op0=ALU.add),
                        reads=[PSA[bnk], "adab"], writes=[("mod", l)], ss=True)
                ada_list.append((src, (lambda slot: slot[:, 0:1024].rearrange("p (k o) -> p k o", k=KC)), run))

        def ada_post(l, nxt):
            def run(l=l, nxt=nxt):
                for w in (1, 4):
                    P.op("dve", lambda e, w=w: e.tensor_scalar(out=modT[:, l, w * 8:(w + 1) * 8, :], in0=modT[:, l, w * 8:(w + 1) * 8, :],
                                                               scalar1=1.0, scalar2=None, op0=ALU.add),
                         reads=[("mod", l)], writes=[("mod", l)], ss=True)
                for w in (2, 5):
                    P.op("dve", lambda e, w=w: e.tensor_scalar(out=modT[:, l, w * 8:(w + 1) * 8, :], in0=modT[:, l, w * 8:(w + 1) * 8, :],
                                                               scalar1=1.0, scalar2=1.0 / ALPHA, op0=ALU.add, op1=ALU.mult),
                         reads=[("mod", l)], writes=[("mod", l)], ss=True)
            cjob(run)

        def fuse_consts(l, which, ml, msc, msh):
            def run():
                gi = 0 if which == 0 else 2
                for s in range(NS):
                    P.op("dve", lambda e, s=s: e.tensor_tensor(out=g2c[:, l, which, :, s], in0=lnp[:, gi, l, :],
                                                                in1=modT[:, ml, msc * 8:(msc + 1) * 8, s], op=ALU.mult),
                         reads=["lnp", ("mod", ml)], writes=[("g2c", l)], ss=True)
                    P.op("dve", lambda e, s=s: e.tensor_tensor(out=b2c[:, l, which, :, s], in0=lnp[:, gi + 1, l, :],
                                                                in1=modT[:, ml, msc * 8:(msc + 1) * 8, s], op=ALU.mult),
                         reads=["lnp", ("mod", ml)], writes=[("g2c", l)], ss=True)
                    P.op("dve", lambda e, s=s: e.tensor_tensor(out=b2c[:, l, which, :, s], in0=b2c[:, l, which, :, s],
                                                                in1=modT[:, ml, msh * 8:(msh + 1) * 8, s], op=ALU.add),
                         reads=[("g2c", l), ("mod", ml)], writes=[("g2c", l)], ss=True)
            cjob(run)

        def swap_x(s_out, s_in, l0):
            def run():
                allk = []
                for tb in range(16):
                    blk = tb % 2
                    tsl = slice(tb * 128, (tb + 1) * 128)
                    XT = ("xt", tb)
                    allk.append(XT)
                    pt = pa if blk == 0 else pb
                    pk = PSA if blk == 0 else PSB
                    if s_in is not None:
                        lst = bufC[:, 2 * blk:2 * blk + 2, :].rearrange("p a b -> p (a b)")[:, 0:D]
                        lkeys = [CK[2 * blk], CK[2 * blk + 1]]
                        P.dma(lambda e, lst=lst, tb=tb: e.dma_start(out=lst, in_=xin[s_in, tb * 128:(tb + 1) * 128, :]),
                              writes=lkeys, chan=("xs", blk))
                    if s_out is not None:
                        sst = bufC[:, 4 + 2 * blk:6 + 2 * blk, :].rearrange("p a b -> p (a b)")[:, 0:D]
                        skeys = [CK[4 + 2 * blk], CK[5 + 2 * blk]]
                        for kc in range(KC):
                            P.op("pe", lambda e, kc=kc, pt=pt, tsl=tsl: e.transpose(out=pt[:, kc * 128:(kc + 1) * 128], in_=x[:, kc, tsl], identity=ident),
                                 reads=[XK[kc], XT, "cst"], writes=[pk[kc // 4]])
                        if blk == 0:
                            P.op("act", lambda e, sst=sst, pt=pt: e.activation(out=sst, in_=pt[:, 0:1024], func=AF.Copy),
                                 reads=[pk[0], pk[1]], writes=skeys)
                        else:
                            P.op("dve", lambda e, sst=sst, pt=pt: e.tensor_copy(out=sst, in_=pt[:, 0:1024]),
                                 reads=[pk[0], pk[1]], writes=skeys)
                        P.dma(lambda e, sst=sst, tb=tb: e.dma_start(out=yout[s_out, tb * 128:(tb + 1) * 128, :], in_=sst),
                              reads=skeys, writes=[("yout", s_out, tb)], chan=("ys", blk))
                    if s_in is not None:
                        for kc in range(KC):
                            P.op("pe", lambda e, lst=lst, kc=kc, pt=pt: e.transpose(out=pt[:, 1024 + kc * 128:1024 + (kc + 1) * 128], in_=lst[:, kc * 128:(kc + 1) * 128], identity=ident),
                                 reads=lkeys + ["cst"], writes=[pk[2 + kc // 4]])
                        if blk == 1:
                            P.op("act", lambda e, tsl=tsl, pt=pt: e.activation(out=x[:, :, tsl], in_=pt[:, 1024:2048].rearrange("p (k t) -> p k t", k=KC), func=AF.Copy),
                                 reads=[pk[2], pk[3]], writes=[XT])
                        else:
                            P.op("dve", lambda e, tsl=tsl, pt=pt: e.tensor_copy(out=x[:, :, tsl], in_=pt[:, 1024:2048].rearrange("p (k t) -> p k t", k=KC)),
                                 reads=[pk[2], pk[3]], writes=[XT])
                if s_in is None:
                    return
                P.op("dve", lambda e: e.memset(scT[:, 0:1], 0.0), reads=allk + ["scT"], writes=XK + ["scT"])
                for kc in range(KC if l0 is not None else 0):
                    if l0 % 2 == 1:
                        uo_ = ub[:, kc, :].rearrange("p (t c) -> p t c", t=8)
                        ui_ = x[:, kc, :].rearrange("p (c t) -> p t c", t=8)
                    else:
                        uo_ = ub[:, kc, :]
                        ui_ = x[:, kc, :]
                    P.op("act", lambda e, kc=kc, uo_=uo_, ui_=ui_: e.activation(out=uo_, in_=ui_, func=AF.Identity,
                                                              scale=modT[:, l0, 8 + kc, s_in:s_in + 1], bias=modT[:, l0, kc, s_in:s_in + 1]),
                         reads=[XK[kc], ("mod", l0)], writes=[UK[kc]])
            cjob(run)

        def conv_mixer(l, s):
            j = l // 2
            mid = bufB
            cgs = bufC[:, 0:2, :].rearrange("p a b -> p (a b)")
            chp = bufC[:, 2:5, :].rearrange("p a b -> p (a b)")
            tA = bufC[:, 5:7, :].rearrange("p a b -> p (a b)")
            CG = [CK[0], CK[1]]
            CH = [CK[2], CK[3], CK[4]]
            TA = [CK[5], CK[6]]

            def pre():
                P.op("dve", lambda e: e.memset(chp[:, 0:1], 0.0), writes=CH)
                P.op("dve", lambda e: e.memset(chp[:, 2049:2050], 0.0), writes=CH)
            cjob(pre)
            for f in range(KC):
                w5 = wb_in[j].rearrange("(k p) (c f o) -> p k c f o", p=128, c=3, f=KC)
                src = [(w5[:, :, c, f, :], (lambda t, c=c: t[:, :, c, :])) for c in range(3)]

                def run(tv, key, f=f):
                    for part, pt, pk in ((1, pa, PSA), (2, pb, PSB)):
                        for tt in range(NTT):
                            for kc in range(KC):
                                mm(pt[:, tt * TT:(tt + 1) * TT], tv[:, kc, part, :], ub[:, kc, tt * TT:(tt + 1) * TT],
                                   kc == 0, kc == KC - 1, key + [UK[kc]], [pk[tt]])
                        if part == 1:
                            P.op("act", lambda e: e.activation(out=cgs, in_=pa[:], func=AF.Copy), reads=PSA, writes=CG)
                        else:
                            P.op("dve", lambda e: e.tensor_tensor(out=chp[:, 1:2049], in0=pb[:], in1=cgs, op=ALU.mult),
                                 reads=PSB + CG, writes=CH)
                    for tt in range(NTT):
                        for kc in range(KC):
                            mm(pa[:, tt * TT:(tt + 1) * TT], tv[:, kc, 0, :], ub[:, kc, tt * TT:(tt + 1) * TT],
                               kc == 0, kc == KC - 1, key + [UK[kc]], [PSA[tt]])
                    P.op("act", lambda e: e.activation(out=tA, in_=chp[:, 1:2049], func=AF.Identity,
                                                       scale=sccw[:, j, 1, f:f + 1], bias=sccb[:, j, f:f + 1]),
                         reads=CH + ["sccw", "sccb"], writes=TA)
                    P.op("dve", lambda e: e.scalar_tensor_tensor(out=tA, in0=chp[:, 0:2048], scalar=sccw[:, j, 0, f:f + 1], in1=tA,
                                                                 op0=ALU.mult, op1=ALU.add), reads=CH + TA + ["sccw"], writes=TA)
                    P.op("dve", lambda e: e.scalar_tensor_tensor(out=tA, in0=chp[:, 2:2050], scalar=sccw[:, j, 2, f:f + 1], in1=tA,
                                                                 op0=ALU.mult, op1=ALU.add), reads=CH + TA + ["sccw"], writes=TA)
                    P.op("dve", lambda e: e.tensor_tensor(out=mid[:, f, :], in0=pa[:], in1=tA, op=ALU.mult),
                         reads=PSA + TA, writes=[BK[f]])
                wjob(src, lambda slot: bfview(slot, (KC, 3, 128)), run, scr_blocks[('in', j)])
            for q in range(2):
                src = wb_out[j].rearrange("(k p) o -> p k o", p=128)[:, :, q * 512:(q + 1) * 512]

                def run(tv, key, q=q):
                    for oo in range(4):
                        o = q * 4 + oo
                        pt, pk = (pa, PSA) if o % 2 == 0 else (pb, PSB)
                        for tt in range(NTT):
                            for kc in range(KC):
                                mm(pt[:, tt * TT:(tt + 1) * TT], tv[:, kc, oo * 128:(oo + 1) * 128], mid[:, kc, tt * TT:(tt + 1) * TT],
                                   kc == 0, kc == KC - 1, key + [BK[kc]], [pk[tt]])
                        P.op("dve", lambda e, o=o, pt=pt: e.scalar_tensor_tensor(out=x[:, o, :], in0=pt[:], scalar=modT[:, l, 16 + o, s:s + 1],
                                                                                 in1=x[:, o, :], op0=ALU.mult, op1=ALU.add),
                             reads=pk + [XK[o], ("mod", l)], writes=[XK[o]])
                wjob(src, lambda slot: bfview(slot, (KC, 512)), run, scr_blocks[('out', j)])

        def layer_norm(l, which, s, want_u, perm_u=False):
            gi = 0 if which == 0 else 2

            def run():
                xb = bufC[:, 0:2, :].rearrange("p a b -> p (a b)").bitcast(BF16).rearrange("p (k t) -> p k t", k=KC)
                sq = bufC[:, 2:4, :].rearrange("p a b -> p (a b)").bitcast(BF16).rearrange("p (k t) -> p k t", k=KC)
                XBK = [CK[0], CK[1]]
                SQK = [CK[2], CK[3]]
                allk = []

                def tile_ctx(tt):
                    par = tt % 2
                    sl = slice(tt * TT, (tt + 1) * TT)
                    LX = [("lnx", tt, kc) for kc in range(KC)]
                    ps1 = pa[:, (2 * par) * TT:(2 * par + 1) * TT]
                    ps2 = pa[:, (2 * par + 1) * TT:(2 * par + 2) * TT]
                    PK1, PK2 = PSA[2 * par], PSA[2 * par + 1]
                    c0, c1 = CK[4 + 2 * par], CK[5 + 2 * par]
                    stat = bufC[:, 4 + 2 * par:6 + 2 * par, :].rearrange("p a b -> p (a b)").rearrange("p (a t) -> p a t", a=4)
                    return par, sl, LX, ps1, ps2, PK1, PK2, c0, c1, stat

                def stage_a(tt):
                    par, sl, LX, ps1, ps2, PK1, PK2, c0, c1, stat = tile_ctx(tt)
                    allk.extend(LX)
                    P.op("act", lambda e: e.activation(out=xb, in_=x[:, :, sl], func=AF.Copy), reads=XK, writes=XBK)
                    P.op("act", lambda e: e.activation(out=sq, in_=x[:, :, sl], func=AF.Square), reads=XK, writes=SQK)
                    for kc in range(KC):
                        mm(ps1, ones_b[:], xb[:, kc, :], kc == 0, kc == KC - 1, XBK + ["ones"], [PK1])
                    for kc in range(KC):
                        mm(ps2, ones_b[:], sq[:, kc, :], kc == 0, kc == KC - 1, SQK + ["ones"], [PK2])

                def stage_b(tt):
                    par, sl, LX, ps1, ps2, PK1, PK2, c0, c1, stat = tile_ctx(tt)
                    mean = stat[:, 0, :]
                    msq = stat[:, 1, :]
                    var = stat[:, 2, :]
                    rstd = stat[:, 3, :]
                    s0, s1, s2, s3 = [("st", par, i) for i in range(4)]
                    P.op("dve", lambda e: e.tensor_scalar(out=mean, in0=ps1, scalar1=1.0 / D, scalar2=None, op0=ALU.mult),
                         reads=[PK1], writes=[c0, s0], ss=True)
                    P.op("dve", lambda e: e.tensor_tensor(out=msq, in0=mean, in1=mean, op=ALU.mult), reads=[s0, c0], writes=[c0, s1], ss=True)
                    P.op("dve", lambda e: e.scalar_tensor_tensor(out=var, in0=ps2, scalar=1.0 / D, in1=msq,
                                                                 op0=ALU.mult, op1=ALU.subtract), reads=[PK2, s1, c0], writes=[c1, s2], ss=True)
                    P.op("act", lambda e: e.activation(out=var, in_=var, func=AF.Sqrt, bias=cst[:, 130:131], scale=1.0),
                         reads=[s2, "cst", c1], writes=[c1, s2], ss=True)
                    P.op("dve", lambda e: e.reciprocal(out=rstd, in_=var), reads=[s2, c1], writes=[c1, s3], ss=True)
                    mb = mean.unsqueeze(1).to_broadcast([128, KC, TT])
                    rb = rstd.unsqueeze(1).to_broadcast([128, KC, TT])
                    P.op("dve", lambda e: e.tensor_tensor(out=x[:, :, sl], in0=x[:, :, sl], in1=mb, op=ALU.subtract),
                         reads=XK + [s0, c0], writes=LX)
                    P.op("dve", lambda e: e.tensor_tensor(out=x[:, :, sl], in0=x[:, :, sl], in1=rb, op=ALU.mult),
                         reads=LX + [s3, c1], writes=LX)

                def stage_c(tt):
                    par, sl, LX, ps1, ps2, PK1, PK2, c0, c1, stat = tile_ctx(tt)
                    for kc in range(KC):
                        if want_u:
                            if perm_u:
                                uo = ub[:, kc, :].rearrange("p (t c) -> p t c", t=8)[:, :, tt * 64:(tt + 1) * 64]
                                ui = x[:, kc, sl].rearrange("p (c t) -> p t c", t=8)
                            else:
                                uo = ub[:, kc, sl]
                                ui = x[:, kc, sl]
                            P.op("dve", lambda e, uo=uo, ui=ui, kc=kc: e.tensor_scalar(out=uo, in0=ui, scalar1=g2c[:, l, which, kc, s:s + 1],
                                                                                       scalar2=b2c[:, l, which, kc, s:s + 1], op0=ALU.mult, op1=ALU.add),
                                 reads=[LX[kc], ("g2c", l)], writes=[UK[kc]])
                        P.op("act", lambda e, kc=kc: e.activation(out=x[:, kc, sl], in_=x[:, kc, sl], func=AF.Identity,
                                                                 scale=lnp[:, gi, l, kc:kc + 1], bias=lnp[:, gi + 1, l, kc:kc + 1]),
                             reads=[LX[kc], "lnp"], writes=[LX[kc]])

                stage_a(0)
                for tt in range(NTT):
                    if tt + 1 < NTT:
                        stage_a(tt + 1)
                    stage_b(tt)
                    stage_c(tt)
                P.op("dve", lambda e: e.memset(scT[:, 0:1], 0.0), reads=allk + ["scT"], writes=XK + ["scT"])
            cjob(run)

        def ffn(l, s):
            g = bufB
            tA = bufC[:, 0:3, :].rearrange("p a b -> p (a b)")
            tV = bufC[:, 3:6, :].rearrange("p a b -> p (a b)")
            ga = bufC[:, 6, :].bitcast(BF16)
            TA = [CK[0], CK[1], CK[2]]
            TV = [CK[3], CK[4], CK[5]]
            GA = [CK[6]]
            for (m0, m1) in HGROUPS:
                nk = m1 - m0
                for mp in range(m0, m1, 2):
                    w5 = wb_up[l].rearrange("(k p) (c m o) -> p k c m o", p=128, c=2, m=FC)
                    src = [(w5[:, :, c, mp:mp + 2, :].rearrange("p k m o -> p k (m o)"),
                            (lambda t, c=c: t[:, :, c, :, :].rearrange("p k m o -> p k (m o)"))) for c in range(2)]

                    def run(tv, key, mp=mp, m0=m0):
                        for mi in range(2):
                            m = mp + mi
                            for c, pt, pk in ((0, pa, PSA), (1, pb, PSB)):
                                for tt in range(NTT):
                                    for kc in range(KC):
                                        mm(pt[:, tt * TT:(tt + 1) * TT], tv[:, kc, c, mi, :], ub[:, kc, tt * TT:(tt + 1) * TT],
                                           kc == 0, kc == KC - 1, key + [UK[kc]], [pk[tt]])
                                ch = c * FC + m
                                tt_, tk = (tA, TA) if c == 0 else (tV, TV)
                                P.op("act", lambda e, pt=pt, tt_=tt_, ch=ch: e.activation(out=tt_[:, 1:L + 1], in_=pt[:], func=AF.Identity,
                                                                                     scale=fcw[:, l, 1, ch:ch + 1], bias=fcb[:, l, ch:ch + 1]),
                                     reads=pk + ["fcw", "fcb"], writes=tk)
                                P.op("dve", lambda e, pt=pt, tt_=tt_, ch=ch: e.scalar_tensor_tensor(out=tt_[:, 2:L + 2], in0=pt[:], scalar=fcw[:, l, 0, ch:ch + 1],
                                                                                               in1=tt_[:, 2:L + 2], op0=ALU.mult, op1=ALU.add),
                                     reads=pk + tk + ["fcw"], writes=tk)
                                P.op("dve", lambda e, pt=pt, tt_=tt_, ch=ch: e.scalar_tensor_tensor(out=tt_[:, 0:L], in0=pt[:], scalar=fcw[:, l, 2, ch:ch + 1],
                                                                                               in1=tt_[:, 0:L], op0=ALU.mult, op1=ALU.add),
                                     reads=pk + tk + ["fcw"], writes=tk)
                                if stop == "ffn_dbg" and c == 0:
                                    b32 = bufB[:].rearrange("p a b -> p (a b)").bitcast(F32).rearrange("p (a b) -> p a b", a=4)
                                    P.op("act", lambda e: e.activation(out=b32[:, 0, :], in_=pa[:], func=AF.Copy), reads=PSA + TA, writes=BK)
                                    P.op("act", lambda e: e.activation(out=b32[:, 1, :], in_=tA[:, 1:L + 1], func=AF.Copy), reads=TA, writes=BK)
                                    P.op("act", lambda e: e.activation(out=ga, in_=tA[:, 1:L + 1], func=AF.Gelu_apprx_tanh), reads=TA, writes=GA)
                                    P.op("act", lambda e: e.activation(out=b32[:, 2, :], in_=ga, func=AF.Copy), reads=GA, writes=BK)
                                    P.dma(lambda e: e.dma_start(out=dbg[:], in_=bufB[:]), reads=BK, writes=["dbg"], chan="dbg")
                                    return
                                if c == 0:
                                    P.op("act", lambda e: e.activation(out=ga, in_=tA[:, 1:L + 1], func=AF.Gelu_apprx_tanh), reads=TA, writes=GA)
                                else:
                                    P.op("dve", lambda e, m=m, m0=m0: e.tensor_tensor(out=g[:, m - m0, :], in0=tV[:, 1:L + 1], in1=ga, op=ALU.mult),
                                         reads=TV + GA, writes=[BK[m - m0]])
                    wjob(src, lambda slot: bfview(slot, (KC, 2, 2, 128)), run, scr_blocks[('up', l)])
                if stop == "ffn_dbg":
                    return
                if stop == "ffn_g0":
                    def dump():
                        P.dma(lambda e: e.dma_start(out=dbg[:], in_=bufB[:]), reads=BK, writes=["dbg"], chan="dbg")
                    cjob(dump)
                    return
                for q in range(4):
                    src = wb_dn[l].rearrange("(k p) o -> p k o", p=128)[:, m0:m1, q * 256:(q + 1) * 256]

                    def run(tv, key, q=q, nk=nk):
                        for oo in range(2):
                            o = q * 2 + oo
                            pt, pk = (pa, PSA) if o % 2 == 0 else (pb, PSB)
                            for tt in range(NTT):
                                for kc in range(nk):
                                    mm(pt[:, tt * TT:(tt + 1) * TT], tv[:, kc, oo * 128:(oo + 1) * 128], g[:, kc, tt * TT:(tt + 1) * TT],
                                       kc == 0, kc == nk - 1, key + [BK[kc]], [pk[tt]])
                            P.op("dve", lambda e, o=o, pt=pt: e.scalar_tensor_tensor(out=x[:, o, :], in0=pt[:], scalar=modT[:, l, 40 + o, s:s + 1],
                                                                                     in1=x[:, o, :], op0=ALU.mult, op1=ALU.add),
                                 reads=pk + [XK[o], ("mod", l)], writes=[XK[o]])
                    wjob(src, lambda slot, nk=nk: bfview(slot, (nk, 256)), run, scr_blocks[('dn', l)])

        xs = x[:].rearrange("p a b -> p (a b)")
        TWO_PI = 6.283185307179586

        def s5_setup(j):
            KT = ["s5tab", XK[0], XK[1]]
            A = xs[:, 0:4096]

            def slot(i, n=272):
                return A[:, i * 272:i * 272 + n]

            def sm(i):
                return A[:, 3840 + 16 * i:3840 + 16 * i + 16]

            def t3(ap):
                return ap.rearrange("p (k g) -> p k g", g=16)

            def D_(fn, reads=(), writes=(), eng="dve"):
                P.op(eng, fn, reads=list(reads) + KT, writes=list(writes) + KT, ss=True)

            kv = cst[:, 131:148]
            kvb = kv.unsqueeze(2).to_broadcast([128, 17, 16])
            sgn = cst[:, 128:129]
            tmp1 = xs[:, 4096:8192]
            tmp2 = xs[:, 8192:12288]
            T1K = [XK[2], XK[3]]
            T2K = [XK[4], XK[5]]
            dbase = 12288
            nat = xs[:, dbase:dbase + 256]
            ub32 = ub[:].rearrange("p a b -> p (a b)").bitcast(F32)
            cnat = ub32[:, 0:2048].rearrange("p (g r) -> p g r", g=16)
            cnat2 = ub32[:, 2048:4096].rearrange("p (g r) -> p g r", g=16)
            DK = [XK[6], XK[7], UK[0], UK[1], UK[2], UK[3]]

            def dtile(d, i):
                o = dbase + 256 + (d * 4 + i) * 256
                return xs[:, o:o + 256].rearrange("p (g c) -> p g c", g=16)

            hz = [bufC[:, 0:4, :].rearrange("p a b -> p (a b)"), bufC[:, 4:8, :].rearrange("p a b -> p (a b)")]
            HZK = [[CK[0], CK[1], CK[2], CK[3]], [CK[4], CK[5], CK[6], CK[7]]]
            hcout = bufB[:, 0:2, :].rearrange("p a b -> p (a b)")
            xout = bufB[:, 2:4, :].rearrange("p a b -> p (a b)")
            pbb = pb[:].bitcast(BF16)

            def first():
                P.dma(lambda e: e.dma_start(out=dcol[:, j, :], in_=s5_d[j].rearrange("(g c) -> c g", c=16), allow_slow_non_contiguous=True),
                      writes=[("dcol", j)], chan="s5d", eng="act")

            def expand(out4, Tre, Tim, D1, D2, okeys, rkeys):
                a0 = Tre.rearrange("p k g -> p g k").unsqueeze(3).to_broadcast([128, 16, 16, 16])
                a1 = Tim.rearrange("p k g -> p g k").unsqueeze(3).to_broadcast([128, 16, 16, 16])
                b0 = D1.unsqueeze(2).to_broadcast([128, 16, 16, 16])
                b1 = D2.unsqueeze(2).to_broadcast([128, 16, 16, 16])
                t1 = tmp1.rearrange("p (g k c) -> p g k c", g=16, k=16)
                t2 = tmp2.rearrange("p (g k c) -> p g k c", g=16, k=16)
                P.op("dve", lambda e: e.tensor_tensor(out=t1, in0=a0, in1=b0, op=ALU.mult), reads=KT + DK + rkeys, writes=T1K)
                P.op("dve", lambda e: e.tensor_tensor(out=t2, in0=a1, in1=b1, op=ALU.mult), reads=KT + DK + rkeys, writes=T2K)
                P.op("dve", lambda e: e.tensor_tensor(out=out4, in0=t1, in1=t2, op=ALU.add), reads=T1K + T2K, writes=okeys)

            def per_qd(q, d, g0):
                if True:
                    for h in range(2):
                        P.dma(lambda e, h=h: e.dma_start(out=nat[0:16, 64 * h:64 * h + 64], in_=s5_a_re[j, d, g0:g0 + 16, :]),
                              writes=[("nat", h)] + DK, chan=("s5n", h), eng="act")
                        P.dma(lambda e, h=h: e.dma_start(out=nat[0:16, 128 + 64 * h:192 + 64 * h], in_=s5_a_im[j, d, g0:g0 + 16, :]),
                              writes=[("nat", 2 + h)] + DK, chan=("s5n", 2 + h), eng="act")
                    P.dma(lambda e: e.dma_start(out=sm(2), in_=s5_log_dt[j, d:d + 1, g0:g0 + 16].to_broadcast([128, 16])),
                          writes=["s5dt"] + KT, chan="s5dt", eng="act")
                    NK = [("nat", i) for i in range(4)]
                    P.op("pe", lambda e: e.transpose(out=pa[:, 0:16], in_=nat[0:16, 0:128], identity=ident[0:16, 0:16]),
                         reads=NK + DK + ["cst"], writes=[PSA[0]])
                    P.op("pe", lambda e: e.transpose(out=pa[:, 16:32], in_=nat[0:16, 128:256], identity=ident[0:16, 0:16]),
                         reads=NK + DK + ["cst"], writes=[PSA[0]])
                    are, aim, dtb, x1, q1, den, rden, e1, fre, fim, u1, u2 = [sm(i) for i in range(12)]
                    D_(lambda e: e.tensor_copy(out=A[:, 3840:3872], in_=pa[:, 0:32]), reads=[PSA[0]])
                    D_(lambda e: e.activation(out=dtb, in_=dtb, func=AF.Exp), reads=["s5dt"], eng="act")
                    D_(lambda e: e.tensor_tensor(out=x1, in0=are, in1=dtb, op=ALU.mult))
                    D_(lambda e: e.tensor_tensor(out=q1, in0=aim, in1=dtb, op=ALU.mult))
                    D_(lambda e: e.tensor_scalar(out=q1, in0=q1, scalar1=1.0 / TWO_PI, scalar2=None, op0=ALU.mult))
                    xk, mag, qs, fr, nf, mk, sn, cs, Pre, Pim, PFre, PFim, w1, w2 = [slot(i) for i in range(14)]
                    ni = slot(4).bitcast(I32)
                    D_(lambda e: e.tensor_tensor(out=t3(xk), in0=kvb, in1=x1.unsqueeze(1).to_broadcast([128, 17, 16]), op=ALU.mult))
                    D_(lambda e: e.activation(out=mag, in_=xk, func=AF.Exp), eng="act")
                    D_(lambda e: e.tensor_tensor(out=t3(qs), in0=kvb, in1=q1.unsqueeze(1).to_broadcast([128, 17, 16]), op=ALU.mult))

                    def range_reduce(src):
                        D_(lambda e: e.tensor_copy(out=ni, in_=src))
                        D_(lambda e: e.tensor_copy(out=nf, in_=ni))
                        D_(lambda e: e.tensor_tensor(out=fr, in0=src, in1=nf, op=ALU.subtract))
                        D_(lambda e: e.tensor_single_scalar(out=mk, in_=fr, scalar=0.5, op=ALU.is_gt))
                        D_(lambda e: e.tensor_tensor(out=fr, in0=fr, in1=mk, op=ALU.subtract))
                        D_(lambda e: e.tensor_single_scalar(out=mk, in_=fr, scalar=-0.5, op=ALU.is_lt))
                        D_(lambda e: e.tensor_tensor(out=fr, in0=fr, in1=mk, op=ALU.add))
                    range_reduce(qs)
                    D_(lambda e: e.activation(out=sn, in_=fr, func=AF.Sin, scale=TWO_PI), eng="act")
                    D_(lambda e: e.tensor_scalar(out=qs, in0=qs, scalar1=0.25, scalar2=None, op0=ALU.add))
                    range_reduce(qs)
                    D_(lambda e: e.activation(out=cs, in_=fr, func=AF.Sin, scale=TWO_PI), eng="act")
                    D_(lambda e: e.tensor_tensor(out=Pre, in0=mag, in1=cs, op=ALU.mult))
                    D_(lambda e: e.tensor_tensor(out=Pim, in0=mag, in1=sn, op=ALU.mult))
                    P1re = t3(Pre)[:, 1, :]
                    P1im = t3(Pim)[:, 1, :]
                    D_(lambda e: e.tensor_scalar(out=e1, in0=P1re, scalar1=-1.0, scalar2=None, op0=ALU.add))
                    D_(lambda e: e.tensor_tensor(out=den, in0=are, in1=are, op=ALU.mult))
                    D_(lambda e: e.tensor_tensor(out=u1, in0=aim, in1=aim, op=ALU.mult))
                    D_(lambda e: e.tensor_tensor(out=den, in0=den, in1=u1, op=ALU.add))
                    D_(lambda e: e.reciprocal(out=rden, in_=den))
                    D_(lambda e: e.tensor_tensor(out=fre, in0=e1, in1=are, op=ALU.mult))
                    D_(lambda e: e.tensor_tensor(out=u1, in0=P1im, in1=aim, op=ALU.mult))
                    D_(lambda e: e.tensor_tensor(out=fre, in0=fre, in1=u1, op=ALU.add))
                    D_(lambda e: e.tensor_tensor(out=fre, in0=fre, in1=rden, op=ALU.mult))
                    D_(lambda e: e.tensor_tensor(out=fim, in0=P1im, in1=are, op=ALU.mult))
                    D_(lambda e: e.tensor_tensor(out=u1, in0=e1, in1=aim, op=ALU.mult))
                    D_(lambda e: e.tensor_tensor(out=fim, in0=fim, in1=u1, op=ALU.subtract))
                    D_(lambda e: e.tensor_tensor(out=fim, in0=fim, in1=rden, op=ALU.mult))
                    freb = fre.unsqueeze(1).to_broadcast([128, 16, 16])
                    fimb = fim.unsqueeze(1).to_broadcast([128, 16, 16])
                    P16r = t3(Pre)[:, 0:16, :]
                    P16i = t3(Pim)[:, 0:16, :]
                    W1 = t3(w1)[:, 0:16, :]
                    W2 = t3(w2)[:, 0:16, :]
                    D_(lambda e: e.tensor_tensor(out=W1, in0=P16r, in1=freb, op=ALU.mult))
                    D_(lambda e: e.tensor_tensor(out=W2, in0=P16i, in1=fimb, op=ALU.mult))
                    D_(lambda e: e.tensor_tensor(out=t3(PFre)[:, 0:16, :], in0=W1, in1=W2, op=ALU.subtract))
                    D_(lambda e: e.tensor_tensor(out=W1, in0=P16r, in1=fimb, op=ALU.mult))
                    D_(lambda e: e.tensor_tensor(out=W2, in0=P16i, in1=freb, op=ALU.mult))
                    D_(lambda e: e.tensor_tensor(out=t3(PFim)[:, 0:16, :], in0=W1, in1=W2, op=ALU.add))
                    D_(lambda e: e.tensor_copy(out=scA[:, j, 0, d, g0:g0 + 16], in_=t3(Pre)[:, 16, :]), writes=[("scA", j)])
                    D_(lambda e: e.tensor_scalar(out=scA[:, j, 1, d, g0:g0 + 16], in0=t3(Pim)[:, 16, :], scalar1=sgn, scalar2=None, op0=ALU.mult),
                       reads=["cst"], writes=[("scA", j)])
                    CTn, CTs2, BB, BBs2 = [dtile(d, i) for i in range(4)]
                    P.dma(lambda e: e.dma_start(out=cnat[0:16, :, 0:64], in_=s5_c_re[j, d, g0:g0 + 16].rearrange("g i p -> i g p")), writes=[("cn", 0)] + DK, chan=("s5c", 0), eng="act")
                    P.dma(lambda e: e.dma_start(out=cnat[0:16, :, 64:128], in_=s5_c_im[j, d, g0:g0 + 16].rearrange("g i p -> i g p")), writes=[("cn", 1)] + DK, chan=("s5c", 1), eng="act")
                    P.dma(lambda e: e.dma_start(out=cnat2[0:16, :, 0:64], in_=s5_c_im[j, d, g0:g0 + 16].rearrange("g i p -> i g p")), writes=[("cn", 2)] + DK, chan=("s5c", 2), eng="act")
                    P.dma(lambda e: e.dma_start(out=cnat2[0:16, :, 64:128], in_=s5_c_re[j, d, g0:g0 + 16].rearrange("g i p -> i g p")), writes=[("cn", 3)] + DK, chan=("s5c", 3), eng="act")
                    CNK = [("cn", i) for i in range(4)]
                    for gl in range(16):
                        P.op("pe", lambda e, gl=gl: e.transpose(out=pa[:, 512 + gl * 16:528 + gl * 16], in_=cnat[0:16, gl, :], identity=ident[0:16, 0:16]),
                             reads=CNK + DK + ["cst"], writes=[PSA[1]])
                        P.op("pe", lambda e, gl=gl: e.transpose(out=pa[:, 1024 + gl * 16:1040 + gl * 16], in_=cnat2[0:16, gl, :], identity=ident[0:16, 0:16]),
                             reads=CNK + DK + ["cst"], writes=[PSA[2]])
                    P.op("dve", lambda e: e.tensor_scalar(out=CTn, in0=pa[:, 512:768].rearrange("p (g c) -> p g c", g=16), scalar1=sgn, scalar2=None, op0=ALU.mult),
                         reads=[PSA[1], "cst"], writes=DK + [("dt", d)], ss=True)
                    P.op("dve", lambda e: e.tensor_scalar(out=CTs2, in0=pa[:, 1024:1280].rearrange("p (g c) -> p g c", g=16), scalar1=-1.0, scalar2=None, op0=ALU.mult),
                         reads=[PSA[2]], writes=DK + [("dt", d)], ss=True)
                    P.dma(lambda e: e.dma_start(out=BB[0:64], in_=s5_b_re[j, d, g0:g0 + 16].rearrange("g p c -> p g c")), writes=[("bb", 0)] + DK, chan=("s5b", 0), eng="act")
                    P.dma(lambda e: e.dma_start(out=BB[64:128], in_=s5_b_im[j, d, g0:g0 + 16].rearrange("g p c -> p g c")), writes=[("bb", 1)] + DK, chan=("s5b", 1), eng="act")
                    P.dma(lambda e: e.dma_start(out=BBs2[0:64], in_=s5_b_im[j, d, g0:g0 + 16].rearrange("g p c -> p g c")), writes=[("bb", 2)] + DK, chan=("s5b", 2), eng="act")
                    P.dma(lambda e: e.dma_start(out=BBs2[64:128], in_=s5_b_re[j, d, g0:g0 + 16].rearrange("g p c -> p g c")), writes=[("bb", 3)] + DK, chan=("s5b", 3), eng="act")
                    BBK = [("bb", i) for i in range(4)]
                    P.op("dve", lambda e: e.tensor_scalar(out=BBs2, in0=BBs2, scalar1=sgn, scalar2=-1.0, op0=ALU.mult, op1=ALU.mult),
                         reads=BBK + ["cst"], writes=DK + [("dt", d), ("bb", 2), ("bb", 3)], ss=True)
                    DTK = [("dt", d)] + BBK
                    TPre, TPim, TFre, TFim = t3(Pre), t3(Pim), t3(PFre), t3(PFim)
                    if d == 0:
                        cre, cim = TPre[:, 1:17, :], TPim[:, 1:17, :]
                    else:
                        cre, cim = TPre[:, 16:0:-1, :], TPim[:, 16:0:-1, :]
                    hc4 = hcout.rearrange("p (g k c) -> p g k c", g=16, k=16)
                    expand(hc4, cre, cim, CTn, CTs2, [BK[0], BK[1]], DTK)
                    dstc = twc[j, g0:g0 + 16].rearrange("g p s c -> p g (s c)")[:, :, (3 + 2 * d) * 128:(5 + 2 * d) * 128]
                    P.dma(lambda e, dstc=dstc: e.dma_start(out=dstc, in_=hcout.rearrange("p (g r) -> p g r", g=16)),
                          reads=[BK[0], BK[1]], writes=[("twc", j, q, "c", d)], chan="s5o0", eng="act")
                    if d == 0:
                        zre, zim = TFre[:, 0:16, :], TFim[:, 0:16, :]
                    else:
                        zre, zim = TFre[:, 15::-1, :], TFim[:, 15::-1, :]
                    hz4 = hz[d].rearrange("p (g k c) -> p g k c", g=16, k=16)
                    expand(hz4, zre, zim, CTn, CTs2, HZK[d], DTK)
                    if d == 0:
                        bre, bim = TFre[:, 15::-1, :], TFim[:, 15::-1, :]
                    else:
                        bre, bim = TFre[:, 0:16, :], TFim[:, 0:16, :]
                    x4 = xout.rearrange("p (g k c) -> p g k c", g=16, k=16)
                    expand(x4, bre, bim, BB, BBs2, [BK[2], BK[3]], DTK)
                    xo3 = xout.rearrange("p (g r) -> p g r", g=16)
                    for g4 in range(4):
                        half = g4 % 2
                        PK = [PSB[2 * half], PSB[2 * half + 1]]
                        for gg in range(4):
                            gl = g4 * 4 + gg
                            for b in range(2):
                                idx = gg * 2 + b
                                P.op("pe", lambda e, gl=gl, b=b, idx=idx, half=half: e.transpose(
                                    out=pbb[:, half * 2048 + idx * 128:half * 2048 + (idx + 1) * 128],
                                    in_=xo3[:, gl, b * 128:(b + 1) * 128], identity=identb[:]),
                                    reads=[BK[2], BK[3], "identb"], writes=PK)
                        stg = bufB[:, 4 + half, 0:1024]
                        P.op("act", lambda e, stg=stg, half=half: e.activation(out=stg, in_=pbb[:, half * 2048:half * 2048 + 1024], func=AF.Copy),
                             reads=PK, writes=[BK[4 + half]])
                        dstb = twb[j, g0 + g4 * 4:g0 + g4 * 4 + 4].rearrange("g p s c -> p g (s c)")[:, :, 2 * d * 128:(2 * d + 2) * 128]
                        P.dma(lambda e, dstb=dstb, stg=stg: e.dma_start(out=dstb, in_=stg.rearrange("p (g r) -> p g r", g=4)),
                              reads=[BK[4 + half]], writes=[("twb", j, q, g4, d)], chan=("s5o1", half), eng="act")
            def per_q(q, g0):
                BBf = dtile(0, 2)
                BBb = dtile(1, 2)
                hzf = hz[0].rearrange("p (g r) -> p g r", g=16)
                hzb = hz[1].rearrange("p (g r) -> p g r", g=16)
                P.op("dve", lambda e: e.memset(scT[:, 1:2], 0.0), reads=[], writes=[BK[6], ("zs", 0), ("zs", 1), ("zs", 2), ("zs", 3), "scTb2"])
                for gl in range(16):
                    g = g0 + gl
                    zpt, zpk = (pa, PSA[3]) if gl % 2 == 0 else (pb, PSB[3])
                    P.op("pe", lambda e, gl=gl, zpt=zpt: e.matmul(zpt[0:16, 1536 + 240:1536 + 496], lhsT=BBf[:, gl, :], rhs=hzf[:, gl, :], start=True, stop=False),
                         reads=HZK[0] + DK + [("dt", 0), ("bb", 0), ("bb", 1)], writes=[zpk])
                    P.op("pe", lambda e, gl=gl, zpt=zpt: e.matmul(zpt[0:16, 1536:1536 + 256], lhsT=BBb[:, gl, :], rhs=hzb[:, gl, :], start=False, stop=True),
                         reads=HZK[1] + DK + [("dt", 1), ("bb", 0), ("bb", 1)], writes=[zpk])
                    zi = gl % 4
                    zs = bufB[0:16, 6, zi * 512:zi * 512 + 496]
                    ZK = ("zs", zi)
                    P.op("act", lambda e, zs=zs, zpt=zpt: e.activation(out=zs, in_=zpt[0:16, 1536:2032], func=AF.Copy), reads=[zpk], writes=[ZK])
                    P.op("dve", lambda e, zs=zs, g=g, zpt=zpt: e.scalar_tensor_tensor(out=zs[:, 240:256], in0=cst[0:16, 0:16], scalar=dcol[:, j, g:g + 1],
                                                                            in1=zpt[0:16, 1536 + 240:1536 + 256], op0=ALU.mult, op1=ALU.add),
                         reads=[zpk, ZK, "cst", ("dcol", j)], writes=[ZK], ss=True)
                    zt = bufB[:, 6, :]
                    for di in range(3):
                        off0 = zi * 512 + (15 + 8 * (di - 1)) * 16
                        src = AP(zt.tensor, zt.offset + off0, [[zt.ap[0][0], 16], [-16, 8], [1, 128]])
                        dstz = twc[j, g, :, di, :].rearrange("(s c) r -> c s r", c=16)
                        P.dma(lambda e, src=src, dstz=dstz: e.dma_start(out=dstz, in_=src),
                              reads=[ZK], writes=[("twc", j, q, "z", gl, di)], chan=("s5z", zi), eng="act")
                P.op("dve", lambda e: e.memset(scT[:, 1:2], 0.0), reads=[("zs", 0), ("zs", 1), ("zs", 2), ("zs", 3)], writes=[BK[6], "scTb2"])

            stages = [first]
            for q in range(4):
                stages.append(lambda q=q: per_qd(q, 0, 16 * q))
                stages.append(lambda q=q: (per_qd(q, 1, 16 * q), per_q(q, 16 * q)))
            return stages

        def s5_dump():
            P.dma(lambda e: e.dma_start(out=dbg32[:, :], in_=xs[:, :]), reads=["s5tab"] + XK, writes=["dbg"], chan="dbg")

        def s5_keys(j):
            ks = []
            for q in range(4):
                for d in range(2):
                    ks.append(("twc", j, q, "c", d))
                    for g4 in range(4):
                        ks.append(("twb", j, q, g4, d))
                for gl in range(16):
                    for di in range(3):
                        ks.append(("twc", j, q, "z", gl, di))
            return ks

        def s5_mixer(l, s):
            j = l // 2
            U = bufB[:].rearrange("p a b -> p (a b)").rearrange("p (g c) -> p g c", g=G)
            S = CU[:, :].rearrange("p (m c) -> p m c", m=128)
            CUb = CU[:, :].bitcast(BF16)
            Sbf = CUb[:, 0:8192].rearrange("p (m c) -> p m c", m=64)
            Sbn = CUb[:, 8192:16384].rearrange("p (m c) -> p m c", m=64)
            T3 = scT[:, 0:128].rearrange("p (d g) -> p d g", d=2)
            P3 = scT[:, 128:256].rearrange("p (d g) -> p d g", d=2)
            UD = [("Ud", t) for t in range(8)]
            tkeys = s5_keys(j)

            def bounce_in():
                P.op("sp", lambda e: e.nop(), reads=[("Ug", g) for g in range(G)], writes=["Ufree"] + BK)
                for kc in range(KC):
                    P.dma(lambda e, kc=kc: e.dma_start(out=d_in[:, kc * 128:(kc + 1) * 128, :].rearrange("t f c -> f t c"),
                                                       in_=ub[:, kc, :].rearrange("p (t c) -> p t c", t=8)),
                          reads=[UK[kc]], writes=[("din", kc)], chan=("bi", kc % 4))
                for kc in range(KC):
                    for t in range(8):
                        P.dma(lambda e, t=t, kc=kc: e.dma_start(out=U[t * 16:(t + 1) * 16, kc * 8:(kc + 1) * 8, :],
                                                              in_=d_in[t, kc * 128:(kc + 1) * 128, :].rearrange("(g j) c -> j g c", j=16)),
                              reads=[("din", kc), "Ufree"], writes=[("Ud", kc, t)], chan=("bu", kc))
            cjob(bounce_in)

            for tix in range(8):
                gb = tix * 8
                src = twb[j, gb:gb + 8].rearrange("g p s c -> p g (s c)")

                def run(tv, key, gb=gb, tix=tix):
                    tv = tv.rearrange("p g (s c) -> p g s c", s=4)
                    pt, pk = (pa, PSA) if tix % 2 == 0 else (pb, PSB)
                    for d in range(2):
                        for gl in range(8):
                            g = gb + gl
                            o = pt[:, (d * 8 + gl) * 128:(d * 8 + gl + 1) * 128]
                            kk = pk[(d * 8 + gl) // 4]
                            udk = [("Ud", g // 8, t_) for t_ in range(8)]
                            mm(o, tv[:, gl, 2 * d, :], U[:, g, 0:256:2], True, False, key + udk + [("Ug", g)], [kk])
                            mm(o, tv[:, gl, 2 * d + 1, :], U[:, g, 1:256:2], False, True, key + udk + [("Ug", g)], [kk])
                    P.op("act", lambda e, pt=pt, gb=gb: e.activation(out=S[:, gb:gb + 8, :], in_=pt[:, 0:1024].rearrange("p (g c) -> p g c", g=8), func=AF.Copy),
                         reads=[pk[0], pk[1]], writes=CK)
                    P.op("dve", lambda e, pt=pt, gb=gb: e.tensor_copy(out=S[:, 64 + gb:64 + gb + 8, :],
                                                                    in_=pt[:, 1024:2048].rearrange("p (g c) -> p g c", g=8)[:, :, 127::-1]),
                         reads=[pk[2], pk[3]], writes=UK)
                wjob(src, lambda slot: bfview(slot, (8, 4, 128)).rearrange("p g s c -> p g (s c)"), run, tkeys)

            def scan():
                Ar = scA[:, j, 0, :, :]
                AiN = scA[64:128, j, 1, :, :]
                AiP = scA[0:64, j, 1, :, :]
                SK = CK + UK
                for n in range(1, 128):
                    prev = S[:, :, n - 1].rearrange("p (d g) -> p d g", d=2)
                    cur = S[:, :, n].rearrange("p (d g) -> p d g", d=2)
                    kp, kc_ = ("scS", (n - 1) % 2), ("scS", n % 2)
                    base = SK + [("scA", j)]
                    P.op("dve", lambda e, prev=prev: e.tensor_tensor(out=P3, in0=prev, in1=Ar, op=ALU.mult),
                         reads=base + [kp], writes=["scP"], ss=True)
                    P.op("dve", lambda e, prev=prev: e.tensor_tensor(out=T3[0:64], in0=prev[64:128], in1=AiN, op=ALU.mult),
                         reads=base + [kp], writes=["scTa"], ss=True)
                    P.op("pool", lambda e, prev=prev: e.tensor_tensor(out=T3[64:128], in0=prev[0:64], in1=AiP, op=ALU.mult),
                         reads=base + [kp], writes=["scTb"], ss=True)
                    P.op("dve", lambda e, cur=cur: e.tensor_tensor(out=cur, in0=cur, in1=P3, op=ALU.add),
                         reads=base + ["scP", kc_], writes=[kc_], ss=True)
                    P.op("dve", lambda e, cur=cur: e.tensor_tensor(out=cur, in0=cur, in1=T3, op=ALU.add),
                         reads=base + ["scTa", "scTb", kc_], writes=[kc_], ss=True)
                P.op("act", lambda e: e.activation(out=Sbf, in_=S[:, 0:64, :], func=AF.Copy), reads=CK + [("scS", 0), ("scS", 1)], writes=CK, ss=True)
                P.op("act", lambda e: e.activation(out=Sbn, in_=S[:, 64:128, 127::-1], func=AF.Copy), reads=CK + UK + [("scS", 0), ("scS", 1)], writes=CK, ss=True)
            cjob(scan)

            for tix in range(16):
                gb = tix * 4
                src = twc[j, gb:gb + 4].rearrange("g p s c -> p g (s c)")

                def run(tv, key, gb=gb, tix=tix):
                    tv = tv.rearrange("p g (s c) -> p g s c", s=7)
                    pt, pk = (pa, PSA) if tix % 2 == 0 else (pb, PSB)
                    for gl in range(4):
                        g = gb + gl
                        for bo in range(2):
                            o = pt[:, (gl * 2 + bo) * 128:(gl * 2 + bo + 1) * 128]
                            kk = pk[(gl * 2 + bo) // 4]
                            rk = key + [("Ud", g // 8, t_) for t_ in range(8)] + [("Ug", g)]
                            mm(o, tv[:, gl, bo + 1, :], U[:, g, 0:256:2], True, False, rk, [kk])
                            mm(o, tv[:, gl, bo, :], U[:, g, 1:256:2], False, False, rk, [kk])
                            mm(o[:, 1:128], tv[:, gl, 3 + bo, :], Sbf[:, g, 0:127], False, False, key + CK, [kk])
                            mm(o[:, 0:127], tv[:, gl, 5 + bo, :], Sbn[:, g, 1:128], False, True, key + CK, [kk])
                    yo = U[:, gb:gb + 4, :].rearrange("p g (c b) -> p g b c", b=2)
                    yi = pt[:, 0:1024].rearrange("p (g b c) -> p g b c", g=4, b=2)
                    P.op("act", lambda e, yo=yo, yi=yi: e.activation(out=yo, in_=yi, func=AF.Gelu_apprx_tanh),
                         reads=[pk[0], pk[1]], writes=[("Ug", g_) for g_ in range(gb, gb + 4)] + [("Yd", gb // 4)])
                wjob(src, lambda slot: bfview(slot, (4, 7, 128)).rearrange("p g s c -> p g (s c)"), run, tkeys)

            if stop == "s5_yg":
                def dump():
                    P.dma(lambda e: e.dma_start(out=dbg[:], in_=bufB[:]), reads=[("Yd", i) for i in range(16)] + BK, writes=["dbg"], chan="dbg")
                cjob(dump)
                return
            def bounce_out():
                for kc in range(KC):
                    ykeys = [("Yd", 2 * kc), ("Yd", 2 * kc + 1)] + [("Ug", g) for g in range(kc * 8, kc * 8 + 8)]
                    for t in range(8):
                        P.dma(lambda e, t=t, kc=kc: e.dma_start(out=d_out[t, kc * 128:(kc + 1) * 128, :].rearrange("(g j) c -> j g c", j=16),
                                                              in_=U[t * 16:(t + 1) * 16, kc * 8:(kc + 1) * 8, :]),
                              reads=ykeys, writes=[("dout", kc, t)], chan=("bo", kc))
                    P.dma(lambda e, kc=kc: e.dma_start(out=ub[:, kc, :].rearrange("p (t c) -> p t c", t=8),
                                                       in_=d_out[:, kc * 128:(kc + 1) * 128, :].rearrange("t f c -> f t c")),
                          reads=[("dout", kc, t) for t in range(8)], writes=[UK[kc]], chan=("bz", kc % 4))
            cjob(bounce_out)

            sg = bufC[:, 0:2, :].rearrange("p a b -> p (a b)")
            mt = bufC[:, 2:4, :].rearrange("p a b -> p (a b)")
            for q in range(4):
                w4 = wb_glu[j].rearrange("(k p) (c o) -> p k c o", p=128, c=2)
                src = [(w4[:, :, c, q * 256:(q + 1) * 256], (lambda t, c=c: t[:, :, c, :])) for c in range(2)]

                def run(tv, key, q=q):
                    for oo in range(2):
                        o = q * 2 + oo
                        for c, pt, pk in ((1, pb, PSB), (0, pa, PSA)):
                            for tt in range(NTT):
                                for kc in range(KC):
                                    mm(pt[:, tt * TT:(tt + 1) * TT], tv[:, kc, c, oo * 128:(oo + 1) * 128], ub[:, kc, tt * TT:(tt + 1) * TT],
                                       kc == 0, kc == KC - 1, key + [UK[kc]], [pk[tt]])
                            if c == 1:
                                P.op("act", lambda e: e.activation(out=sg, in_=pb[:], func=AF.Sigmoid), reads=PSB, writes=[CK[0], CK[1]])
                        P.op("dve", lambda e: e.tensor_tensor(out=mt, in0=pa[:], in1=sg, op=ALU.mult), reads=PSA + [CK[0], CK[1]], writes=[CK[2], CK[3]])
                        P.op("dve", lambda e, o=o: e.scalar_tensor_tensor(out=x[:, o, :].rearrange("p (c t) -> p c t", t=8),
                                                                         in0=mt.rearrange("p (t c) -> p c t", t=8), scalar=modT[:, l, 16 + o, s:s + 1],
                                                                         in1=x[:, o, :].rearrange("p (c t) -> p c t", t=8), op0=ALU.mult, op1=ALU.add),
                             reads=[CK[2], CK[3], XK[o], ("mod", l)], writes=[XK[o]])
                wjob(src, lambda slot: bfview(slot, (KC, 2, 256)), run, scr_blocks[('glu', j)])


        P.op("dve", lambda e: e.tensor_copy(out=identb[:], in_=ident), reads=["cst"], writes=["identb"])
        pump(0)
        for l in layers:
            ada_jobs(l)
        for ji, j_ in enumerate(used_s5):
            stages = s5_setup(j_)
            last = (ji == len(used_s5) - 1) and len(used_s5) > 1
            per = (len(ada_list) + 7) // 8
            for st_i, st_fn in enumerate(stages):
                cjob(st_fn)
                if last and st_i >= 1:
                    emit_ada(per)
        if stop == "s5w":
            cjob(s5_dump)
        emit_ada(10 ** 6)
        for li, l in enumerate(layers):
            nxt = layers[li + 1] if li + 1 < len(layers) else None
            ada_post(l, nxt)
        for li, l in enumerate(layers):
            nxt = layers[li + 1] if li + 1 < len(layers) else None
            fuse_consts(l, 0, l, 4, 3)
            if nxt is not None:
                fuse_consts(l, 1, nxt, 1, 0)
        for s in range(NS if stop != "s5w" else 0):
            swap_x(s - 1 if s > 0 else None, s, layers[0] if layers else None)
            for li, l in enumerate(layers):
                nxt = layers[li + 1] if li + 1 < len(layers) else None
                if l % 2 == 0:
                    conv_mixer(l, s)
                else:
                    s5_mixer(l, s)
                if stop in ("mixer", "s5_yg"):
                    break
                layer_norm(l, 0, s, True)
                if stop == "ln1":
                    break
                if stop == "cst":
                    def dump():
                        n1 = DEPTH * 48 * NS
                        n2 = DEPTH * 2 * KC * NS
                        P.dma(lambda e: e.dma_start(out=dbg32[:, 0:n1], in_=modT[:].rearrange("p a b c -> p (a b c)")), reads=[("mod", l)], writes=["dbg"], chan="dbg")
                        P.dma(lambda e: e.dma_start(out=dbg32[:, n1:n1 + n2], in_=g2c[:].rearrange("p a b c d -> p (a b c d)")), reads=[("g2c", l)], writes=["dbg1"], chan="dbg1")
                        P.dma(lambda e: e.dma_start(out=dbg32[:, n1 + n2:n1 + 2 * n2], in_=b2c[:].rearrange("p a b c d -> p (a b c d)")), reads=[("g2c", l)], writes=["dbg2"], chan="dbg2")
                        P.dma(lambda e: e.dma_start(out=dbg32[:, n1 + 2 * n2:n1 + 2 * n2 + 128], in_=lnp[:].rearrange("p a b c -> p (a b c)")), reads=["lnp"], writes=["dbg3"], chan="dbg3")
                    cjob(dump)
                    break
                if stop == "ub":
                    def dump():
                        P.dma(lambda e: e.dma_start(out=dbg[:], in_=ub[:]), reads=UK, writes=["dbg"], chan="dbg")
                    cjob(dump)
                    break
                ffn(l, s)
                if stop in ("ffn", "ffn_g0", "ffn_dbg"):
                    break
                layer_norm(l, 1, s, nxt is not None, perm_u=(nxt is not None and nxt % 2 == 1))
        if stop != "s5w":
            swap_x(NS - 1, None, None)
        run_jobs()
        ykeys = [("yout", s, tb) for s in range(NS) for tb in range(16)] if stop != "s5w" else s5_keys(0) + ["dbg"]
        P.op("sp", lambda e: e.nop(), reads=ykeys + (["dbg"] if stop in ("ffn_g0", "ffn_dbg", "ub", "cst", "s5_yg") else []) + (["dbg1", "dbg2", "dbg3"] if stop == "cst" else []), writes=["done"])

        sems = {}
        for en in ("pe", "act", "dve", "pool", "sp"):
            sems[en] = es.enter_context(nc.semaphore("sem_" + en))
        for ci, ch in enumerate(P.channels()):
            sems[("chan", ch)] = es.enter_context(nc.semaphore("semc_%d" % ci))
        block = es.enter_context(nc.Block())
        P.finalize(block, sems)
    build_program.P = P
    return nc


def make_consts():
    c = np.zeros((128, 256), np.float32)
    c[:, 0:128] = np.eye(128, dtype=np.float32)
    c[0:64, 128] = 1.0
    c[64:128, 128] = -1.0
    c[:, 129] = 1.0
    c[:, 130] = EPS_P
    c[:, 131:131 + 32] = np.arange(32, dtype=np.float32)[None, :]
    return c


_WNAMES = ["ada_w", "ada_b", "ln1_g", "ln1_b", "ln2_g", "ln2_b", "sc_w_in", "sc_conv_w", "sc_conv_b", "sc_w_out",
           "s5_a_re", "s5_a_im", "s5_log_dt", "s5_b_re", "s5_b_im", "s5_c_re", "s5_c_im", "s5_d", "s5_w_glu",
           "ffn_w_up", "ffn_conv_w", "ffn_conv_b", "ffn_w_down"]


def run_cores(x_all, c_all, weights, NS, layers, ncores, stop=None):
    nc = build_program(NS, layers, stop)
    consts = make_consts()
    in_maps = []
    for ci in range(ncores):
        m = {"xin": np.ascontiguousarray(x_all[ci * NS:(ci + 1) * NS]),
             "cin": np.ascontiguousarray(c_all[ci * NS:(ci + 1) * NS]),
             "consts": consts}
        for n in _WNAMES:
            m[n] = weights[n]
        in_maps.append(m)
    res = run_bass_kernel_spmd(nc, in_maps, core_ids=list(range(ncores)))
    if stop:
        run_cores.dbg = res.results[0]["dbg"]
        run_cores.dbg32 = res.results[0]["dbg32"]
    if stop == "wup":
        run_cores.wup = res.results[0]["wb_up"]
    if stop == "s5w":
        run_cores.twb = res.results[0]["twb"]
        run_cores.twc = res.results[0]["twc"]
    return np.concatenate([r["yout"] for r in res.results], axis=0)


def kernel(x_prompt, x_sample, c_prompt, c_sample, **w):
    x_prompt = np.asarray(x_prompt, np.float32)
    x_sample = np.asarray(x_sample, np.float32)
    c_prompt = np.asarray(c_prompt, np.float32)
    c_sample = np.asarray(c_sample, np.float32)
    weights = {n: np.ascontiguousarray(np.asarray(w[n], np.float32)) for n in _WNAMES}
    xs, cs = [], []
    for i in range(NCORES):
        xs.append(x_prompt[4 * i:4 * i + 4]); xs.append(x_sample[i:i + 1])
        cs.append(c_prompt[4 * i:4 * i + 4]); cs.append(c_sample[i:i + 1])
    x_all = np.concatenate(xs, axis=0)
    c_all = np.concatenate(cs, axis=0)
    y = run_cores(x_all, c_all, weights, 5, [0, 1, 2, 3], NCORES)
    y = y.reshape(NCORES, 5, L, D)
    y_prompt = np.ascontiguousarray(y[:, 0:4].reshape(32, L, D))
    y_sample = np.ascontiguousarray(y[:, 4].reshape(8, L, D))
    return (y_prompt, y_sample)
```
